# Optimizing a Trainium2 kernel written in Bass

```python
import math
import jax, jax.numpy as jnp
from jax import lax
import numpy as np

D_MODEL = 1024
BATCH = 8
SEQ = 4096
DEPTH = 4

GRID_W = 64
CTX_LEN = 256
N_MIXERS = 2
NORM_EPS = 1e-6

DN_QK_HEADS = 8
DN_V_HEADS = 16
DN_HEAD_K = 128
DN_HEAD_V = 128
DN_KEY_DIM = DN_QK_HEADS * DN_HEAD_K
DN_VAL_DIM = DN_V_HEADS * DN_HEAD_V
DN_CONV_DIM = 2 * DN_KEY_DIM + DN_VAL_DIM
DN_CONV_K = 5
DN_CHUNK = 64
DN_IN_DIM = DN_CONV_DIM + DN_VAL_DIM + 4 * DN_V_HEADS
DT_MIN = 0.001
DT_MAX = 0.1

ATT_Q_HEADS = 8
ATT_KV_HEADS = 2
ATT_HEAD_DIM = 128
ATT_Q_DIM = ATT_Q_HEADS * ATT_HEAD_DIM
ATT_KV_DIM = ATT_KV_HEADS * ATT_HEAD_DIM
ATT_IN_DIM = ATT_Q_DIM + 2 * ATT_KV_DIM + ATT_Q_DIM
ATT_BLOCK = 128
ROPE_THETA = 10000.0

N_DN_LAYERS = (DEPTH + N_MIXERS - 1) // N_MIXERS
N_ATT_LAYERS = DEPTH // N_MIXERS

kernel_name = 'hybrid_deltanet_gqa_dit'

F32 = jnp.float32


def rms_norm(x, g):
    xf = x.astype(F32)
    y = xf * lax.rsqrt(jnp.mean(xf * xf, axis=-1, keepdims=True) + NORM_EPS)
    return (y * g.astype(F32)).astype(x.dtype)


def l2_norm(x):
    xf = x.astype(F32)
    return (xf * lax.rsqrt(jnp.sum(xf * xf, axis=-1, keepdims=True) + NORM_EPS)).astype(x.dtype)


def centred_dwconv(x, w):
    k = w.shape[0]
    return lax.conv_general_dilated(x, w[:, None, :].astype(x.dtype), window_strides=(1,),
                                    padding=[(k // 2, k // 2)],
                                    dimension_numbers=('NWC', 'WIO', 'NWC'),
                                    feature_group_count=x.shape[-1])


def chunk_gated_delta(q, k, v, g, beta, s0):
    b, h, l, dk = k.shape
    dv = v.shape[-1]
    n = l // DN_CHUNK
    q = q.astype(F32) * (dk ** -0.5)
    k = k.astype(F32)
    v32 = v.astype(F32)
    ch = lambda t: t.reshape(b, h, n, DN_CHUNK, *t.shape[3:])
    qc, kc, vc = ch(q), ch(k), ch(v32)
    gc = jnp.cumsum(ch(g.astype(F32)), axis=-1)
    bc = ch(beta.astype(F32))
    idx = jnp.arange(DN_CHUNK)
    lower = idx[:, None] >= idx[None, :]
    strict = idx[:, None] > idx[None, :]
    decay = jnp.exp(jnp.where(lower, gc[..., :, None] - gc[..., None, :], -jnp.inf))
    kb = kc * bc[..., None]
    vb = vc * bc[..., None]
    a = jnp.where(strict, jnp.einsum('bhncd,bhnsd->bhncs', kb, kc) * decay, 0.0)
    eye = jnp.eye(DN_CHUNK, dtype=F32)
    t_inv = lax.linalg.triangular_solve(eye + a, jnp.broadcast_to(eye, a.shape),
                                        left_side=True, lower=True, unit_diagonal=True)
    u = jnp.einsum('bhncs,bhnse->bhnce', t_inv, vb)
    w = jnp.einsum('bhncs,bhnsd->bhncd', t_inv, kb * jnp.exp(gc)[..., None])
    qk = jnp.einsum('bhncd,bhnsd->bhncs', qc, kc) * decay
    q_dec = qc * jnp.exp(gc)[..., None]
    k_dec = kc * jnp.exp(gc[..., -1:] - gc)[..., None]
    g_last = jnp.exp(gc[..., -1])

    def step(s, xs):
        qk_i, qd_i, kd_i, u_i, w_i, gl_i = xs
        v_new = u_i - jnp.einsum('bhcd,bhde->bhce', w_i, s)
        o = jnp.einsum('bhcd,bhde->bhce', qd_i, s) + jnp.einsum('bhcs,bhse->bhce', qk_i, v_new)
        s = s * gl_i[..., None, None] + jnp.einsum('bhcd,bhce->bhde', kd_i, v_new)
        return s, o

    xs = tuple(jnp.moveaxis(t_, 2, 0) for t_ in (qk, q_dec, k_dec, u, w, g_last))
    s_fin, o = lax.scan(step, s0.astype(F32), xs)
    o = jnp.moveaxis(o, 0, 2).reshape(b, h, l, dv)
    return o.astype(v.dtype), s_fin


def bidir_delta(q, k, v, g, beta, s0_f, s0_b):
    o_f, s_f = chunk_gated_delta(q, k, v, g[:, 0], beta[:, 0], s0_f)
    fl = lambda t: jnp.flip(t, axis=2)
    o_b, s_b = chunk_gated_delta(fl(q), fl(k), fl(v), fl(g[:, 1]), fl(beta[:, 1]), s0_b)
    return o_f + fl(o_b), s_f, s_b


def deltanet_mixer(u_lat, u_ctx, w_in, conv_w, a_log, dt_bias, o_norm_g, w_out, need_ctx):
    def branch_inputs(u):
        b, l, _ = u.shape
        p = u @ w_in
        qkv = jax.nn.silu(centred_dwconv(p[..., :DN_CONV_DIM], conv_w))
        z = p[..., DN_CONV_DIM:DN_CONV_DIM + DN_VAL_DIM]
        ab = p[..., DN_CONV_DIM + DN_VAL_DIM:].reshape(b, l, 2, 2, DN_V_HEADS).astype(F32)
        q, k, v = jnp.split(qkv, [DN_KEY_DIM, 2 * DN_KEY_DIM], axis=-1)
        rep = DN_V_HEADS // DN_QK_HEADS
        q = jnp.repeat(l2_norm(q.reshape(b, l, DN_QK_HEADS, DN_HEAD_K)), rep, axis=2)
        k = jnp.repeat(l2_norm(k.reshape(b, l, DN_QK_HEADS, DN_HEAD_K)), rep, axis=2)
        v = v.reshape(b, l, DN_V_HEADS, DN_HEAD_V)
        beta = jax.nn.sigmoid(ab[:, :, 0])
        g = -jnp.exp(a_log.astype(F32)) * jax.nn.softplus(ab[:, :, 1] + dt_bias.astype(F32))
        bhl = lambda t: jnp.transpose(t, (0, 2, 1, 3))
        return bhl(q), bhl(k), bhl(v), jnp.transpose(g, (0, 2, 3, 1)), jnp.transpose(beta, (0, 2, 3, 1)), z

    def finish(o, z):
        b, h, l, dv = o.shape
        o = rms_norm(jnp.transpose(o, (0, 2, 1, 3)), o_norm_g).astype(z.dtype)
        return (o.reshape(b, l, DN_VAL_DIM) * jax.nn.silu(z)) @ w_out

    qc, kc, vc, gc, bc, zc = branch_inputs(u_ctx)
    s_zero = jnp.zeros((u_ctx.shape[0], DN_V_HEADS, DN_HEAD_K, DN_HEAD_V), F32)
    o_ctx, s_f, s_b = bidir_delta(qc, kc, vc, gc, bc, s_zero, s_zero)
    ql, kl, vl, gl, bl, zl = branch_inputs(u_lat)
    o_lat, _, _ = bidir_delta(ql, kl, vl, gl, bl, s_f, s_b)
    y_lat = finish(o_lat, zl)
    y_ctx = finish(o_ctx, zc) if need_ctx else None
    return y_lat, y_ctx


def axial_rope_angles(n):
    rows = n // GRID_W
    row = jnp.broadcast_to(jnp.arange(rows)[:, None], (rows, GRID_W)).reshape(-1).astype(F32)
    col = jnp.broadcast_to(jnp.arange(GRID_W)[None, :], (rows, GRID_W)).reshape(-1).astype(F32)
    axis_dim = ATT_HEAD_DIM // 2
    inv = ROPE_THETA ** (-jnp.arange(0, axis_dim, 2, dtype=F32) / axis_dim)
    return row[:, None] * inv, col[:, None] * inv


def rope_rotate(x, ang):
    x1, x2 = jnp.split(x, 2, axis=-1)
    cos = jnp.cos(ang)[:, None, :].astype(x.dtype)
    sin = jnp.sin(ang)[:, None, :].astype(x.dtype)
    return jnp.concatenate([x1 * cos - x2 * sin, x2 * cos + x1 * sin], axis=-1)


def apply_axial_rope(x, ang_r, ang_c):
    xr, xc = jnp.split(x, 2, axis=-1)
    return jnp.concatenate([rope_rotate(xr, ang_r), rope_rotate(xc, ang_c)], axis=-1)


def gqa_attend(q, k, v):
    b, lq, hq, d = q.shape
    qg = q.reshape(b, lq, ATT_KV_HEADS, hq // ATT_KV_HEADS, d)
    s = jnp.einsum('bqhgd,bkhd->bhgqk', qg, k).astype(F32) * (d ** -0.5)
    p = jax.nn.softmax(s, axis=-1).astype(v.dtype)
    return jnp.einsum('bhgqk,bkhd->bqhgd', p, v).reshape(b, lq, hq, d)


def blocked_gqa(q, k, v):
    b, lq, hq, d = q.shape
    nb = lq // ATT_BLOCK
    qb = jnp.moveaxis(q.reshape(b, nb, ATT_BLOCK, hq, d), 1, 0)
    ob = lax.map(lambda qi: gqa_attend(qi, k, v), qb)
    return jnp.moveaxis(ob, 0, 1).reshape(b, lq, hq, d)


def attention_mixer(u_lat, u_ctx, w_in, q_norm_g, k_norm_g, w_out, need_ctx):
    def project(u):
        b, l, _ = u.shape
        p = u @ w_in
        q, k, v, z = jnp.split(p, [ATT_Q_DIM, ATT_Q_DIM + ATT_KV_DIM, ATT_Q_DIM + 2 * ATT_KV_DIM], axis=-1)
        q = rms_norm(q.reshape(b, l, ATT_Q_HEADS, ATT_HEAD_DIM), q_norm_g)
        k = rms_norm(k.reshape(b, l, ATT_KV_HEADS, ATT_HEAD_DIM), k_norm_g)
        v = v.reshape(b, l, ATT_KV_HEADS, ATT_HEAD_DIM)
        return q, k, v, z

    def finish(o, z):
        b, l = o.shape[:2]
        return (o.reshape(b, l, ATT_Q_DIM) * jax.nn.silu(z)) @ w_out

    qc, kc, vc, zc = project(u_ctx)
    ql, kl, vl, zl = project(u_lat)
    ang_r, ang_c = axial_rope_angles(u_lat.shape[1])
    ql = apply_axial_rope(ql, ang_r, ang_c)
    kl = apply_axial_rope(kl, ang_r, ang_c)
    k_all = jnp.concatenate([kc, kl], axis=1)
    v_all = jnp.concatenate([vc, vl], axis=1)
    y_lat = finish(blocked_gqa(ql, k_all, v_all), zl)
    y_ctx = finish(gqa_attend(qc, kc, vc), zc) if need_ctx else None
    return y_lat, y_ctx


def setup_inputs(seed: int = 0) -> dict:
    key = jax.random.key(seed)
    ks = jax.random.split(key, 20)
    nrm = lambda k, shape, scale: jax.random.normal(k, shape, F32) * scale
    d = D_MODEL
    n_dn, n_att = N_DN_LAYERS, N_ATT_LAYERS
    a_log = jnp.log(jax.random.uniform(ks[8], (n_dn, 2, DN_V_HEADS), F32, 1.0, 16.0))
    dt = jnp.exp(jax.random.uniform(ks[9], (n_dn, 2, DN_V_HEADS), F32, math.log(DT_MIN), math.log(DT_MAX)))
    dt_bias = dt + jnp.log(-jnp.expm1(-dt))
    return {
        'x': nrm(ks[0], (BATCH, SEQ, d), 1.0),
        'c': nrm(ks[1], (BATCH, d), 1.0),
        'ctx': nrm(ks[2], (BATCH, CTX_LEN, d), 1.0),
        'c_ctx': nrm(ks[3], (d,), 1.0),
        'norm_g': 1.0 + nrm(ks[4], (DEPTH, d), 0.02),
        'ada_w': nrm(ks[5], (DEPTH, d, 3 * d), 0.5 * d ** -0.5),
        'ada_b': nrm(ks[6], (DEPTH, 3 * d), 0.02),
        'dn_w_in': nrm(ks[7], (n_dn, d, DN_IN_DIM), d ** -0.5),
        'dn_conv_w': nrm(ks[10], (n_dn, DN_CONV_K, DN_CONV_DIM), DN_CONV_K ** -0.5),
        'dn_a_log': a_log,
        'dn_dt_bias': dt_bias,
        'dn_o_norm_g': 1.0 + nrm(ks[11], (n_dn, DN_HEAD_V), 0.02),
        'dn_w_out': nrm(ks[12], (n_dn, DN_VAL_DIM, d), DN_VAL_DIM ** -0.5),
        'att_w_in': nrm(ks[13], (n_att, d, ATT_IN_DIM), d ** -0.5),
        'att_q_norm_g': 1.0 + nrm(ks[14], (n_att, ATT_HEAD_DIM), 0.02),
        'att_k_norm_g': 1.0 + nrm(ks[15], (n_att, ATT_HEAD_DIM), 0.02),
        'att_w_out': nrm(ks[16], (n_att, ATT_Q_DIM, d), ATT_Q_DIM ** -0.5),
        'final_norm_g': 1.0 + nrm(ks[17], (d,), 0.02),
    }


def reference(x, c, ctx, c_ctx, norm_g, ada_w, ada_b, dn_w_in, dn_conv_w, dn_a_log, dn_dt_bias,
              dn_o_norm_g, dn_w_out, att_w_in, att_q_norm_g, att_k_norm_g, att_w_out, final_norm_g):
    h_lat, h_ctx = x, ctx
    sc_lat, sc_ctx = jax.nn.silu(c), jax.nn.silu(c_ctx)
    for i in range(DEPTH):
        need_ctx = i < DEPTH - 1
        shift, scale, gate = jnp.split(sc_lat @ ada_w[i] + ada_b[i], 3, axis=-1)
        shift_c, scale_c, gate_c = jnp.split(sc_ctx @ ada_w[i] + ada_b[i], 3, axis=-1)
        u_lat = rms_norm(h_lat, norm_g[i]) * (1.0 + scale[:, None]) + shift[:, None]
        u_ctx = rms_norm(h_ctx, norm_g[i]) * (1.0 + scale_c) + shift_c
        j = i // N_MIXERS
        if i % N_MIXERS == 0:
            y_lat, y_ctx = deltanet_mixer(u_lat, u_ctx, dn_w_in[j], dn_conv_w[j], dn_a_log[j], dn_dt_bias[j],
                                          dn_o_norm_g[j], dn_w_out[j], need_ctx)
        else:
            y_lat, y_ctx = attention_mixer(u_lat, u_ctx, att_w_in[j], att_q_norm_g[j], att_k_norm_g[j],
                                           att_w_out[j], need_ctx)
        h_lat = h_lat + gate[:, None] * y_lat
        if need_ctx:
            h_ctx = h_ctx + gate_c * y_ctx
    return rms_norm(h_lat, final_norm_g)
```

```python
import numpy as np
import ml_dtypes
import concourse.bass as bass
import concourse.mybir as mybir
from concourse.bass_utils import run_bass_kernel_spmd
from contextlib import ExitStack

F32 = mybir.dt.float32
BF16 = mybir.dt.bfloat16
F32R = mybir.dt.float32r
AF = mybir.ActivationFunctionType
ALU = mybir.AluOpType
CENG = ("pe", "act", "dve", "pool")
ENGS = ("pe", "act", "dve", "pool", "sp")


class V:
    __slots__ = ("t", "ap")

    def __init__(self, t, ap):
        self.t = t
        self.ap = ap


class T:
    __slots__ = ("name", "t", "w", "r", "key", "space")

    def __init__(self, name, t, space, key=None):
        self.name = name
        self.t = t
        self.space = space
        self.w = []
        self.r = []
        self.key = key or name

    def __getitem__(self, k):
        return V(self, self.t[k])

    def all(self):
        return V(self, self.t[:])


class Prog:
    def __init__(self, nc):
        self.nc = nc
        self.es = ExitStack()
        self.ops = {e: [] for e in ENGS}
        self.esem = {e: self.es.enter_context(nc.semaphore("s_" + e)) for e in CENG}
        self.dsems = {}
        self.dcnt = {}
        self.extra = {e: [] for e in ENGS}
        self.ro = T("ro", None, "dram")
        self.vcache = {}

    def sb(self, name, shape, dt):
        return T(name, self.es.enter_context(self.nc.sbuf_tensor(name, list(shape), dt)), "sb")

    def ps(self, name, shape, dt):
        return T(name, self.es.enter_context(self.nc.psum_tensor(name, list(shape), dt)), "ps")

    def dram(self, name, shape, dt, kind="Internal"):
        return T(name, self.nc.dram_tensor(name, list(shape), dt, kind=kind).ap(), "dram")

    def region(self, name):
        return T(name, None, "dram")

    def view(self, name, raw, lo, n, dt, pat=None, **kw):
        ck = (raw.name, lo, n, str(dt), pat, tuple(sorted(kw.items())))
        if ck in self.vcache:
            return self.vcache[ck]
        words = n if dt in (F32, F32R) else n // 2
        ap = raw.t[:, lo:lo + words]
        if dt != ap.dtype:
            ap = ap.bitcast(dt)
        if pat:
            ap = ap.rearrange(pat, **kw)
        t = T(name, ap, raw.space, key="%s@%d" % (raw.name, lo))
        self.vcache[ck] = t
        return t

    def _deps(self, eng, reads, writes, shared=()):
        deps = list(self.extra[eng])
        self.extra[eng] = []
        for t in reads:
            deps.extend(t.w)
        for t in shared:
            deps.extend(t.r)
        for t in writes:
            deps.extend(t.w)
            deps.extend(t.r)
        out = []
        for d in deps:
            if d[0] == "E":
                if d[1] == "pe" and eng == "pe":
                    continue
                self.ops[d[1]][d[2]]["flag"] = True
            out.append(d)
        return out

    def _mark(self, me, reads, writes, shared=()):
        for t in reads:
            t.r.append(me)
            if len(t.r) > 64:
                t.r = t.r[-48:]
        for t in writes:
            t.w = [me]
            t.r = []
        for t in shared:
            t.w.append(me)

    def op(self, eng, fn, reads=(), writes=()):
        writes = list(dict.fromkeys(list(writes) + [x for x in reads if x.space == "ps"]))
        reads = [x for x in dict.fromkeys(reads) if x is not self.ro and x.space != "ps"]
        deps = self._deps(eng, reads, writes)
        idx = len(self.ops[eng])
        self.ops[eng].append(dict(fn=fn, waits=deps, flag=False, dma=None))
        self._mark(("E", eng, idx), reads, writes)

    def dma(self, fn, reads=(), writes=(), shared=(), q="sp"):
        reads = [x for x in reads if x is not self.ro]
        deps = self._deps(q, reads, writes, shared)
        st = [t for t in list(writes) + list(shared) + list(reads) if t.space != "dram"][0]
        if st.key not in self.dsems:
            self.dsems[st.key] = self.es.enter_context(self.nc.semaphore("d%d" % len(self.dsems)))
            self.dcnt[st.key] = 0
        sem = self.dsems[st.key]
        if self.dcnt[st.key]:
            deps.append(("D", sem, self.dcnt[st.key]))
        self.dcnt[st.key] += 16
        me = ("D", sem, self.dcnt[st.key])
        self.ops[q].append(dict(fn=fn, waits=deps, flag=False, dma=sem))
        self._mark(me, reads, writes, shared)
        return me

    def barrier(self):
        deps = []
        for e in CENG:
            if self.ops[e]:
                i = len(self.ops[e]) - 1
                self.ops[e][i]["flag"] = True
                deps.append(("E", e, i))
        for k, sem in self.dsems.items():
            deps.append(("D", sem, self.dcnt[k]))
        for e in ENGS:
            self.extra[e] = list(deps) + self.extra[e]

    @staticmethod
    def _ts(*vs):
        return [v.t for v in vs if isinstance(v, V)]

    @staticmethod
    def _a(v):
        return v.ap if isinstance(v, V) else v

    def mm(self, out, lhsT, rhs, start=True, stop=True):
        self.op("pe", lambda e: e.matmul(out.ap, lhsT=lhsT.ap, rhs=rhs.ap, start=start, stop=stop),
                reads=self._ts(lhsT, rhs), writes=self._ts(out))

    def tr(self, out, in_, ident):
        self.op("pe", lambda e: e.transpose(out.ap, in_.ap, ident.ap),
                reads=self._ts(in_, ident), writes=self._ts(out))

    def act(self, out, in_, func, scale=None, bias=None, accum=None):
        kw = {}
        if scale is not None:
            kw["scale"] = self._a(scale)
        if bias is not None:
            kw["bias"] = self._a(bias)
        if accum is not None:
            kw["accum_out"] = accum.ap
        self.op("act", lambda e: e.activation(out=out.ap, in_=in_.ap, func=func, **kw),
                reads=self._ts(in_, scale, bias), writes=self._ts(out, accum))

    def tt(self, eng, out, in0, in1, op):
        self.op(eng, lambda e: e.tensor_tensor(out=out.ap, in0=in0.ap, in1=in1.ap, op=op),
                reads=self._ts(in0, in1), writes=self._ts(out))

    def tsc(self, eng, out, in0, s1, op0, s2=None, op1=None):
        if op1 is None:
            fn = lambda e: e.tensor_scalar(out=out.ap, in0=in0.ap, scalar1=self._a(s1), scalar2=None, op0=op0)
        else:
            fn = lambda e: e.tensor_scalar(out=out.ap, in0=in0.ap, scalar1=self._a(s1),
                                           scalar2=self._a(s2), op0=op0, op1=op1)
        self.op(eng, fn, reads=self._ts(in0, s1, s2), writes=self._ts(out))

    def stt(self, out, in0, scalar, in1, op0, op1):
        self.op("dve", lambda e: e.scalar_tensor_tensor(out=out.ap, in0=in0.ap, scalar=self._a(scalar),
                                                         in1=in1.ap, op0=op0, op1=op1),
                reads=self._ts(in0, scalar, in1), writes=self._ts(out))

    def cp(self, eng, out, in_):
        if eng == "act":
            self.op("act", lambda e: e.copy(out=out.ap, in_=in_.ap), reads=self._ts(in_), writes=self._ts(out))
        else:
            self.op(eng, lambda e: e.tensor_copy(out=out.ap, in_=in_.ap), reads=self._ts(in_),
                    writes=self._ts(out))

    def memset(self, eng, out, val):
        self.op(eng, lambda e: e.memset(out.ap, val), writes=self._ts(out))

    def ld(self, out, src_ap, src_t=None, q="sp"):
        return self.dma(lambda e: e.dma_start(out=out.ap, in_=src_ap), reads=[src_t or self.ro],
                        writes=[out.t], q=q)

    def st(self, dst_ap, dst_t, in_, shared=True, q="sp"):
        if shared:
            return self.dma(lambda e: e.dma_start(out=dst_ap, in_=in_.ap), reads=[in_.t], shared=[dst_t], q=q)
        return self.dma(lambda e: e.dma_start(out=dst_ap, in_=in_.ap), reads=[in_.t], writes=[dst_t], q=q)

    def emit(self):
        nc = self.nc
        cum = {}
        for e in CENG:
            c = 0
            arr = []
            for o in self.ops[e]:
                if o["flag"]:
                    c += 1
                arr.append(c)
            cum[e] = arr
        self.stats = {}

        def run(e, eng):
            seen = {}
            nw = 0
            for o in self.ops[e]:
                for d in o["waits"]:
                    if d[0] == "E":
                        sem, val, key = self.esem[d[1]], cum[d[1]][d[2]], d[1]
                    else:
                        sem, val, key = d[1], d[2], id(d[1])
                    if seen.get(key, 0) >= val:
                        continue
                    seen[key] = val
                    eng.wait_ge(sem, val)
                    nw += 1
                ins = o["fn"](eng)
                if o["dma"] is not None:
                    ins.then_inc(o["dma"], 16)
                elif o["flag"]:
                    ins.then_inc(self.esem[e], 1)
            if e == "sp":
                for k, sem in self.dsems.items():
                    if seen.get(id(sem), 0) < self.dcnt[k]:
                        eng.wait_ge(sem, self.dcnt[k])
            self.stats[e] = (len(self.ops[e]), nw)

        with nc.Block() as block:
            @block.sync
            def _(eng):
                run("sp", eng)

            @block.tensor
            def _(eng):
                run("pe", eng)

            @block.scalar
            def _(eng):
                run("act", eng)

            @block.vector
            def _(eng):
                run("dve", eng)

            @block.gpsimd
            def _(eng):
                run("pool", eng)
        self.es.close()


D = 1024
TT = 4352
NCH = 34
NST = 17
EPS = 1e-6
DEPTH = 4

(C_ID, C_LS, C_LI, C_US, C_UI, C_NLSD, C_NUSD, C_ONE, C_O1024, C_O128, C_RT,
 C_NLSO, C_NUSO, C_NLSO1, C_NUSO1) = [i * 128 for i in range(15)]
NCST = 15 * 128

V_C = 0
V_NG = V_C + 16
V_AB = V_NG + 32
V_FG = V_AB + 192
V_CW = V_FG + 8
V_QG = V_CW + 320
V_KG = V_QG + 2
V_GO = V_KG + 2
V_AL = V_GO + 256
V_DT = V_AL + 64
NV = V_DT + 64


def _const_tables():
    p = np.arange(128)[:, None]
    f = np.arange(128)[None, :]
    t = np.zeros((128, NCST), np.float32)
    t[:, C_ID:C_ID + 128] = (p == f)
    t[:, C_LS:C_LS + 128] = (p > f)
    t[:, C_LI:C_LI + 128] = (p >= f)
    t[:, C_US:C_US + 128] = (p < f)
    t[:, C_UI:C_UI + 128] = (p <= f)
    bd = (p // 64) == (f // 64)
    bd32 = (p // 32) == (f // 32)
    t[:, C_NLSD:C_NLSD + 128] = -((p > f) & bd32).astype(np.float32)
    t[:, C_NUSD:C_NUSD + 128] = -((p < f) & bd32).astype(np.float32)
    t[:, C_NLSO1:C_NLSO1 + 128] = -((p > f) & bd & ~bd32).astype(np.float32)
    t[:, C_NUSO1:C_NUSO1 + 128] = -((p < f) & bd & ~bd32).astype(np.float32)
    t[:, C_NLSO:C_NLSO + 128] = -((p > f) & ~bd).astype(np.float32)
    t[:, C_NUSO:C_NUSO + 128] = -((p < f) & ~bd).astype(np.float32)
    t[:, C_ONE:C_ONE + 128] = 1.0
    t[:, C_O1024:C_O1024 + 128] = 1.0 / 1024.0
    t[:, C_O128:C_O128 + 128] = 1.0 / 128.0
    R = np.zeros((128, 128), np.float32)
    for m in range(128):
        if (m % 64) < 32:
            R[m, m + 32] = -1.0
        else:
            R[m, m - 32] = 1.0
    t[:, C_RT:C_RT + 128] = R.T
    tok = np.arange(4096)
    row = (tok // 64).astype(np.float32)
    col = (tok % 64).astype(np.float32)
    inv = (10000.0 ** (-np.arange(0, 64, 2, dtype=np.float32) / 64.0)).astype(np.float32)
    ang_r = row[None, :] * inv[:, None]
    ang_c = col[None, :] * inv[:, None]
    ang = np.concatenate([ang_r, ang_r, ang_c, ang_c], 0).astype(np.float32)
    return t, np.cos(ang).astype(np.float32), np.sin(ang).astype(np.float32)


def build(layers=(0, 1, 2, 3), dbg=False, upto="out", pairs=range(8), skip_z=False):
    nc = bass.Bass("TRN2", target_bir_lowering=False)
    P = Prog(nc)
    RO = P.ro
    EI = "ExternalInput"
    hT0 = P.dram("hT0", [8, 128, TT], F32, EI)
    vecs = P.dram("vecs", [128, NV], F32, EI)
    cst = P.dram("cst", [128, NCST], F32, EI)
    ropeC = P.dram("ropeC", [128, 4096], F32, EI)
    ropeS = P.dram("ropeS", [128, 4096], F32, EI)
    ada_w = P.dram("ada_w", [4, 1024, 3072], F32, EI)
    dn_w_in = P.dram("dn_w_in", [2, 1024, 6208], F32, EI)
    dn_w_out = P.dram("dn_w_out", [2, 2048, 1024], F32, EI)
    att_w_in = P.dram("att_w_in", [2, 1024, 2560], F32, EI)
    att_w_out = P.dram("att_w_out", [2, 1024, 1024], F32, EI)
    outT = P.dram("outT", [8, 128, 4096], F32, "ExternalOutput")
    IK = "ExternalOutput" if dbg else "Internal"
    HT = P.dram("HT", [8, 128, TT], F32, IK)
    UT = P.dram("UT", [8, 128, TT], BF16, IK)
    YPT = P.dram("YPT", [16, 128, TT], BF16, IK)
    ZST = P.dram("ZST", [16, 128, TT], BF16, IK)
    HTr = [P.region("HTr%d" % i) for i in range(NST)]
    UTr = [P.region("UTr%d" % i) for i in range(NST)]
    YPr = [P.region("YPr%d" % i) for i in range(NST)]
    ZSr = [P.region("ZSr%d" % i) for i in range(NST)]
    OUTr = P.region("OUTr")
    dbg_out = {}

    CST = P.sb("CST", [128, NCST], F32)
    VEC = P.sb("VEC", [128, NV], F32)
    IDB = P.sb("IDB", [128, 128], BF16)
    ONEB = P.sb("ONEB", [128, 128], BF16)
    SC = P.sb("SC", [128, 16], F32)
    MOD = P.sb("MOD", [128, 48], F32)
    GSC = P.sb("GSC", [128, 16], F32)
    SMALL = P.sb("SMALL", [128, 64], F32)
    RA = P.sb("RA", [128, 8704], F32)
    RB = P.sb("RB", [128, 4480], F32)
    RC = P.sb("RC", [128, 4352], F32)
    RD = P.sb("RD", [128, 6528], F32)
    RE = P.sb("RE", [128, 4096], F32)
    RR = P.sb("RR", [128, 5120], F32R)
    RF = P.sb("RF", [128, 5440], F32)
    RG = P.sb("RG", [128, 5120], F32)
    RH = P.sb("RH", [128, 3072], F32)
    RI = P.sb("RI", [128, 2048], F32)
    PSB = [P.ps("PSB%d" % i, [128, 512], F32) for i in range(8)]

    def cs(col, n=128):
        return CST[:, col:col + n]

    def vcol(col):
        return VEC[:, col:col + 1]

    def rsq(dst, src):
        P.act(dst, src, AF.Sqrt, bias=SMALL[:, 1:2])
        P.op("dve", lambda e: e.reciprocal(out=dst.ap, in_=dst.ap), reads=[dst.t], writes=[dst.t])

    class PV:
        def __init__(self, bank, ap):
            self.bank, self.ap = bank, ap

        def __getitem__(self, k):
            return V(self.bank, self.ap[k])

        def all(self):
            return V(self.bank, self.ap)

    def psv(name, bank, lo, n, dt=F32):
        words = n if dt == F32 else n // 2
        ap = PSB[bank].t[:, lo:lo + words]
        if dt != F32:
            ap = ap.bitcast(dt)
        return PV(PSB[bank], ap)

    P.ld(CST.all(), cst.t[:, :])
    P.ld(VEC.all(), vecs.t[:, :])
    P.cp("dve", IDB.all(), cs(C_ID))
    P.cp("dve", ONEB.all(), cs(C_ONE))
    P.act(SC.all(), VEC[:, V_C:V_C + 16], AF.Silu)

    def stage_mod(i):
        P.barrier()
        WA = [P.view("WA%d" % b, RG, b * 1024, 1024, F32, "p (k n) -> p k n", k=8) for b in range(2)]
        pm = psv("pm", 0, 0, 48)
        aw = ada_w.t[i].rearrange("(k p) n -> p k n", p=128)
        for m in range(24):
            w = WA[m % 2]
            P.ld(w.all(), aw[:, :, m * 128:(m + 1) * 128])
            for k in range(8):
                P.mm(pm[:, 2 * m:2 * m + 2], w[:, k, :], SC[:, 2 * k:2 * k + 2], start=(k == 0), stop=(k == 7))
        P.tt("dve", MOD.all(), pm.all(), VEC[:, V_AB + 48 * i:V_AB + 48 * (i + 1)], ALU.add)
        for k in range(8):
            P.tsc("dve", GSC[:, 2 * k:2 * k + 2], MOD[:, 2 * (8 + k):2 * (8 + k) + 2], 1.0, ALU.add,
                  vcol(V_NG + 8 * i + k), ALU.mult)

    def stage_norm(i, src, srcr):
        P.barrier()
        HIN = [P.view("HIN%d" % b, RC, b * 2048, 2048, F32, "p (k n) -> p k n", k=8) for b in range(2)]
        SQ = P.view("SQ", RB, 0, 2048, F32, "p (k n) -> p k n", k=8)
        TMP = P.view("TMPn", RB, 2048, 2048, F32, "p (k n) -> p k n", k=8)
        UO = [P.view("UO%d" % b, RH, b * 1024, 2048, BF16, "p (k n) -> p k n", k=8) for b in range(2)]
        RS = P.view("RS", RI, 0, 256, F32)
        pn = psv("pn", 0, 0, 256)

        def load(st):
            P.ld(HIN[st % 2].all(), src.t[:, :, st * 256:(st + 1) * 256].rearrange("k p t -> p k t"), srcr[st])
        load(0)
        for st in range(NST):
            if st + 1 < NST:
                load(st + 1)
            h = HIN[st % 2]
            r = 1 if st == 0 else 0
            P.act(SQ.all(), h.all(), AF.Square)
            for k in range(8):
                P.mm(pn.all(), cs(C_O1024), SQ[:, k, :], start=(k == 0), stop=(k == 7))
            rsq(RS.all(), pn.all())
            uo = UO[st % 2]
            for k in range(8):
                P.tt("pool" if k % 2 else "dve", TMP[:, k, :], h[:, k, :], RS.all(), ALU.mult)
                P.act(uo[:, k, :], TMP[:, k, :], AF.Identity, scale=GSC[:, 2 * k + r:2 * k + r + 1],
                      bias=MOD[:, 2 * k + r:2 * k + r + 1])
            P.st(UT.t[:, :, st * 256:(st + 1) * 256].rearrange("k p t -> p k t"), UTr[st], uo.all(), shared=False)

    wstate = {"n": 0}

    def load_wchunk(dst, wsrc2d, c0, ncols=128, kc=8):
        b = wstate["n"] % 2
        wstate["n"] += 1
        stg = P.view("WSTG%d" % b, RG, 3072 + b * 1024, kc * ncols, F32, "p (k n) -> p k n", k=kc)
        P.ld(stg.all(), wsrc2d.rearrange("(k p) n -> p k n", p=128)[:, :, c0:c0 + ncols])
        P.cp("pool", dst.all(), stg.all())

    def stage_out(i, wout2d, kc, use_z, last):
        P.barrier()
        WO = P.view("WO", RA, 0, kc * 1024, BF16, "p (k n) -> p k n", k=kc)
        for c in range(8):
            for k0 in range(0, kc, 8):
                b = wstate["n"] % 2
                wstate["n"] += 1
                stg = P.view("WSTG%d" % b, RG, 3072 + b * 1024, 1024, F32, "p (k n) -> p k n", k=8)
                P.ld(stg.all(), wout2d.rearrange("(k p) n -> p k n", p=128)[:, k0:k0 + 8, c * 128:(c + 1) * 128])
                P.cp("pool", WO[:, k0:k0 + 8, c * 128:(c + 1) * 128], stg.all())
        YT = [P.view("YT%d" % b, RC, b * 2048, kc * 256, BF16, "p (k n) -> p k n", k=kc) for b in range(2)]
        ZT = [P.view("ZT%d" % b, RD, b * 2048, kc * 256, BF16, "p (k n) -> p k n", k=kc) for b in range(2)]
        HI = [P.view("HI%d" % b, RB, b * 2048, 2048, F32, "p (k n) -> p k n", k=8) for b in range(2)]
        HO = [P.view("HO%d" % b, RE, b * 2048, 2048, F32, "p (k n) -> p k n", k=8) for b in range(2)]
        SQ = P.view("SQo", RD, 4096, 2048, F32, "p (k n) -> p k n", k=8)
        RS = P.view("RSo", RI, 0, 256, F32)
        pn = psv("pno", 4, 0, 256)
        pys = [psv("py%d" % b, b, 0, 256) for b in range(4)]
        hsrc = hT0 if i == layers[0] else HT

        def load(st):
            b = st % 2
            P.ld(YT[b].all(), YPT.t[0:kc, :, st * 256:(st + 1) * 256].rearrange("k p t -> p k t"), YPr[st])
            if use_z:
                P.ld(ZT[b].all(), ZST.t[0:kc, :, st * 256:(st + 1) * 256].rearrange("k p t -> p k t"), ZSr[st])
            P.ld(HI[b].all(), hsrc.t[:, :, st * 256:(st + 1) * 256].rearrange("k p t -> p k t"), HTr[st])
        first = 1 if last else 0
        load(first)
        for st in range(first, NST):
            if st + 1 < NST:
                load(st + 1)
            b = st % 2
            r = 1 if st == 0 else 0
            y = YT[b]
            if use_z:
                P.tt("pool", y.all(), y.all(), ZT[b].all(), ALU.mult)
            ho = HO[b]
            for c in range(8):
                py = pys[c % 4]
                for k in range(kc):
                    P.mm(py.all(), WO[:, k, c * 128:(c + 1) * 128], y[:, k, :], start=(k == 0), stop=(k == kc - 1))
                P.stt(ho[:, c, :], py.all(), MOD[:, 2 * (16 + c) + r:2 * (16 + c) + r + 1], HI[b][:, c, :],
                      ALU.mult, ALU.add)
            if not last:
                P.st(HT.t[:, :, st * 256:(st + 1) * 256].rearrange("k p t -> p k t"), HTr[st], ho.all(), shared=False)
            else:
                P.act(SQ.all(), ho.all(), AF.Square)
                for k in range(8):
                    P.mm(pn.all(), cs(C_O1024), SQ[:, k, :], start=(k == 0), stop=(k == 7))
                rsq(RS.all(), pn.all())
                for k in range(8):
                    P.stt(SQ[:, k, :], ho[:, k, :], vcol(V_FG + k), RS.all(), ALU.mult, ALU.mult)
                P.st(outT.t[:, :, (st - 1) * 256:st * 256].rearrange("k p t -> p k t"), OUTr, SQ.all())

    def stage_attn(i):
        j = i // 2
        need_ctx = i < DEPTH - 1
        w2d = att_w_in.t[j]
        P.barrier()
        KT = P.view("KT", RA, 0, 2 * TT, BF16, "p (g t) -> p g t", g=2)
        VTM = P.view("VTM", RA, TT, 2 * TT, BF16, "p (g c d) -> p g c d", g=2, c=NCH)
        QT = P.view("QT", RB, 0, TT, BF16)
        ZS = P.view("ZS", RB, TT // 2, TT, BF16)
        UIN = [P.view("UIN%d" % b, RH, b * 1024, 2048, BF16, "p (k n) -> p k n", k=8) for b in range(2)]
        WQ = P.view("WQ", RG, 0, 1024, BF16, "p (k n) -> p k n", k=8)
        WZ = P.view("WZ", RG, 512, 1024, BF16, "p (k n) -> p k n", k=8)
        WV = P.view("WV", RG, 1024, 1024, BF16, "p (k n) -> p k n", k=8)
        XN = [P.view("XN%d" % b, RD, b * 256, 256, F32) for b in range(2)]
        SQ = [P.view("SQa%d" % b, RD, 512 + b * 256, 256, F32) for b in range(2)]
        RS = [P.view("RSa%d" % b, RD, 1024 + b * 256, 256, F32) for b in range(2)]
        T1 = [P.view("T1a%d" % b, RD, 1536 + b * 256, 256, F32) for b in range(2)]
        T2 = [P.view("T2a%d" % b, RD, 2048 + b * 256, 256, F32) for b in range(2)]
        CT = [P.view("CT%d" % b, RD, 2560 + b * 256, 256, F32) for b in range(2)]
        STb = [P.view("STb%d" % b, RD, 3072 + b * 256, 256, F32) for b in range(2)]
        PT = [P.view("PTa%d" % b, RE, b * 256, 512, BF16) for b in range(3)]
        RIV = P.view("RIV", RE, 1024, 512, F32)
        OT = P.view("OTa", RE, 1536, 512, F32)
        YO = [P.view("YO%d" % b, RE, 2048 + b * 256, 512, BF16) for b in range(2)]
        pqs = [psv("pq%d" % b, 4 + b, 0, 256) for b in range(2)]
        pzs = [psv("pz%d" % b, 2 + b, 0, 256) for b in range(2)]
        psss = [psv("pss%d" % b, 6 + b, 0, 256) for b in range(2)]
        prots = [psv("prot%d" % b, b, 0, 256) for b in range(2)]
        pv = [psv("pv%d" % b, 2 + b, 256, 128) for b in range(2)]

        def load_u(st):
            P.ld(UIN[st % 2].all(), UT.t[:, :, st * 256:(st + 1) * 256].rearrange("k p t -> p k t"), UTr[st])

        rope_n = {"n": 0}

        def proj_norm_rope(st, w, gcol, dst, extra_scale):
            u = UIN[st % 2]
            b2 = rope_n["n"] % 2
            rope_n["n"] += 1
            pq, pss, prot = pqs[b2], psss[b2], prots[b2]
            for k in range(8):
                P.mm(pq.all(), w[:, k, :], u[:, k, :], start=(k == 0), stop=(k == 7))
            P.act(SQ[b2].all(), pq.all(), AF.Square)
            P.mm(pss.all(), cs(C_O128), SQ[b2].all())
            rsq(RS[b2].all(), pss.all())
            if st == 0:
                P.stt(dst, pq.all(), gcol, RS[b2].all(), ALU.mult, ALU.mult)
                return
            P.stt(XN[b2].all(), pq.all(), gcol, RS[b2].all(), ALU.mult, ALU.mult)
            P.ld(CT[b2].all(), ropeC.t[:, (st - 1) * 256:st * 256])
            P.ld(STb[b2].all(), ropeS.t[:, (st - 1) * 256:st * 256])
            P.mm(prot.all(), cs(C_RT), XN[b2].all())
            P.tt("pool", T1[b2].all(), XN[b2].all(), CT[b2].all(), ALU.mult)
            P.tt("dve", T2[b2].all(), prot.all(), STb[b2].all(), ALU.mult)
            P.tt("pool", dst, T1[b2].all(), T2[b2].all(), ALU.add)

        for g in range(2):
            load_wchunk(WQ, w2d, 1024 + g * 128)
            load_wchunk(WV, w2d, 1280 + g * 128)
            load_u(0)
            for st in range(NST):
                if st + 1 < NST:
                    load_u(st + 1)
                proj_norm_rope(st, WQ, vcol(V_KG + j), KT[:, g, st * 256:(st + 1) * 256], None)
                u = UIN[st % 2]
                for half in range(2):
                    c = st * 2 + half
                    p = pv[half]
                    for k in range(8):
                        P.mm(p.all(), u[:, k, half * 128:(half + 1) * 128], WV[:, k, :], start=(k == 0), stop=(k == 7))
                    P.cp("act", VTM[:, g, c, :], p.all())
        pS = [psv("pS%d" % b, b, 0, 512) for b in range(2)]
        pOs = [psv("pO%d" % b, 2 + 2 * b, 0, 512) for b in range(2)]
        pSums = [psv("pSum%d" % b, 3 + 2 * b, 0, 512) for b in range(2)]
        zn = 0
        for h in range(8):
            g = h // 4
            load_wchunk(WQ, w2d, h * 128)
            load_wchunk(WZ, w2d, 1536 + h * 128)
            load_u(0)
            for st in range(NST):
                if st + 1 < NST:
                    load_u(st + 1)
                proj_norm_rope(st, WQ, vcol(V_QG + j), QT[:, st * 256:(st + 1) * 256], None)
                u = UIN[st % 2]
                pz = pzs[zn % 2]
                zn += 1
                for k in range(8):
                    P.mm(pz.all(), WZ[:, k, :], u[:, k, :], start=(k == 0), stop=(k == 7))
                P.act(ZS[:, st * 256:(st + 1) * 256], pz.all(), AF.Silu)
            blocks = ([(0, 256, 2)] if need_ctx else []) + [(256 + 512 * qb, 512, NCH) for qb in range(8)]
            for bi, (q0, nq, nk) in enumerate(blocks):
                pO, pSum = pOs[bi % 2], pSums[bi % 2]

                def score(kt):
                    P.mm(pS[kt % 2][:, 0:nq], KT[:, g, kt * 128:(kt + 1) * 128], QT[:, q0:q0 + nq])
                score(0)
                for kt in range(nk):
                    if kt + 1 < nk:
                        score(kt + 1)
                    pt = PT[kt % 3]
                    P.act(pt[:, 0:nq], pS[kt % 2][:, 0:nq], AF.Exp, scale=float(128.0 ** -0.5), bias=SMALL[:, 0:1])
                    P.mm(pO[:, 0:nq], VTM[:, g, kt, :], pt[:, 0:nq], start=(kt == 0), stop=(kt == nk - 1))
                    P.mm(pSum[:, 0:nq], ONEB.all(), pt[:, 0:nq], start=(kt == 0), stop=(kt == nk - 1))
                P.op("dve", lambda e, o=RIV[:, 0:nq], s=pSum[:, 0:nq]: e.reciprocal(out=o.ap, in_=s.ap),
                     reads=[pSum.bank], writes=[RIV])
                P.tt("dve", OT[:, 0:nq], pO[:, 0:nq], RIV[:, 0:nq], ALU.mult)
                yo = YO[bi % 2]
                P.tt("pool", yo[:, 0:nq], OT[:, 0:nq], ZS[:, q0:q0 + nq], ALU.mult)
                regs = [YPr[(q0 + s * 256) // 256] for s in range(nq // 256)]
                P.dma(lambda e, d=YPT.t[h, :, q0:q0 + nq], y_=yo[:, 0:nq]: e.dma_start(out=d, in_=y_.ap),
                      reads=[yo], shared=regs)


    def stage_dn(i):
        j = i // 2
        w2d = dn_w_in.t[j]
        P.barrier()
        UIN = [P.view("UIN%d" % b, RH, b * 1024, 2048, BF16, "p (k n) -> p k n", k=8) for b in range(2)]
        BETA = P.view("BETA", RF, 0, 1088, F32, "p (c h) -> p c h", c=NCH)
        G = P.view("G", RF, 1088, 1088, F32, "p (c h) -> p c h", c=NCH)
        EGL = P.view("EGL", RF, 2176, 1088, F32, "p (c h) -> p c h", c=NCH)
        EG = P.view("EG", RF, 3264, 1088, F32, "p (c h) -> p c h", c=NCH)
        ET = P.view("ET", RF, 4352, 1088, F32, "p (c h) -> p c h", c=NCH)
        NEGA = P.view("NEGA", RI, 0, 32, F32)

        def load_u(st):
            P.ld(UIN[st % 2].all(), UT.t[:, :, st * 256:(st + 1) * 256].rearrange("k p t -> p k t"), UTr[st])

        WAB = P.view("WAB", RG, 0, 512, BF16, "p (k n) -> p k n", k=8)
        load_wchunk(WAB, w2d, 6144, ncols=64)
        pab = psv("pab", 6, 0, 64)
        pgc = psv("pgc", 7, 0, 64)
        P.act(NEGA.all(), VEC[:, V_AL + 32 * j:V_AL + 32 * (j + 1)], AF.Exp)
        P.tsc("dve", NEGA.all(), NEGA.all(), -1.0, ALU.mult)
        load_u(0)
        for st in range(NST):
            if st + 1 < NST:
                load_u(st + 1)
            u = UIN[st % 2]
            for half in range(2):
                c = st * 2 + half
                for k in range(8):
                    P.mm(pab.all(), u[:, k, half * 128:(half + 1) * 128], WAB[:, k, :], start=(k == 0), stop=(k == 7))
                P.cp("dve", BETA[:, c, :], pab[:, 0:32])
                P.tt("dve", G[:, c, :], pab[:, 32:64], VEC[:, V_DT + 32 * j:V_DT + 32 * (j + 1)], ALU.add)
        P.act(BETA.all(), BETA.all(), AF.Sigmoid)
        P.act(G.all(), G.all(), AF.Exp)
        P.act(G.all(), G.all(), AF.Ln, bias=SMALL[:, 2:3])
        for c in range(NCH):
            P.tt("pool", G[:, c, :], G[:, c, :], NEGA.all(), ALU.mult)
            P.mm(pgc[:, 0:16], cs(C_UI), G[:, c, 0:16])
            P.mm(pgc[:, 16:32], cs(C_LI), G[:, c, 16:32])
            P.mm(pgc[:, 32:64], cs(C_ONE), G[:, c, :])
            P.cp("dve", EGL[:, c, :], pgc[:, 0:32])
            P.cp("dve", ET[:, c, :], pgc[:, 32:64])
        P.act(EG.all(), EGL.all(), AF.Exp)
        P.tt("pool", EGL.all(), ET.all(), EGL.all(), ALU.subtract)
        P.act(EGL.all(), EGL.all(), AF.Exp)
        P.act(ET.all(), ET.all(), AF.Exp)
        BEG = P.view("BEG", RB, 256, 1088, F32, "p (c h) -> p c h", c=NCH)
        P.tt("pool", BEG.all(), BETA.all(), EG.all(), ALU.mult)

        P.barrier()
        WZ = P.view("WZd", RG, 512, 1024, BF16, "p (k n) -> p k n", k=8)
        ZO = [P.view("ZO%d" % b, RI, 64 + b * 128, 256, BF16) for b in range(2)]
        pz = [psv("pzd%d" % b, 4 + b, 0, 256) for b in range(2)]
        n = 0
        for hh in ([] if skip_z else range(16)):
            load_wchunk(WZ, w2d, 4096 + hh * 128)
            load_u(0)
            for st in range(NST):
                if st + 1 < NST:
                    load_u(st + 1)
                u = UIN[st % 2]
                p = pz[n % 2]
                for k in range(8):
                    P.mm(p.all(), WZ[:, k, :], u[:, k, :], start=(k == 0), stop=(k == 7))
                zo = ZO[n % 2]
                n += 1
                P.act(zo.all(), p.all(), AF.Silu)
                P.st(ZST.t[hh, :, st * 256:(st + 1) * 256], ZSr[st], zo.all())

        if dbg:
            P.barrier()
            DRF = P.dram("DRF", [128, 5440], F32, "ExternalOutput")
            P.st(DRF.t[:, :], P.region("drf"), RF.all())
        for jp in pairs:
            dn_pair(i, j, jp, BETA, G, EGL, EG, ET, BEG)

    def dn_pair(i, j, jp, BETA, G, EGL, EG, ET, BEG):
        w2d = dn_w_in.t[j]
        P.barrier()
        QNT = P.view("QNT", RD, 0, TT, BF16)
        KNT = P.view("KNT", RD, 2176, TT, BF16)
        KTM = P.view("KTM", RD, 4352, TT, BF16, "p (c d) -> p c d", c=NCH)
        VTM = P.view("VTMd", RC, 0, 2 * TT, BF16, "p (e c d) -> p e c d", e=2, c=NCH)
        OACC = P.view("OACC", RA, 0, 2 * TT, F32, "p (e c d) -> p e c d", e=2, c=NCH)
        WX = [P.view("WX%d" % x, RG, 512 * x, 1024, BF16, "p (k n) -> p k n", k=8) for x in range(4)]
        cols = [jp * 128, 1024 + jp * 128, 2048 + (2 * jp) * 128, 2048 + (2 * jp + 1) * 128]
        for x in range(4):
            load_wchunk(WX[x], w2d, cols[x])
        UH = [P.view("UH%d" % b, RH, b * 1040, 2080, BF16, "p (k n) -> p k n", k=8) for b in range(2)]
        CV = [P.view("CV%d" % x, RI, 256 * x, 256, F32) for x in range(4)]
        SQc = P.view("SQc", RI, 1024, 256, F32)
        RSc = P.view("RSc", RI, 1280, 256, F32)
        VF = [P.view("VF%d" % b, RI, 1536 + 128 * b, 256, BF16) for b in range(2)]
        pp = [psv("pp%d" % x, x, 0, 260) for x in range(4)]
        pss = psv("pssd", 4, 0, 256)
        ptr = [psv("ptr%d" % b, 6 + b, 0, 512, BF16) for b in range(2)]

        def load_uh(st):
            t0 = st * 256
            lr = 1 if st >= 2 else 0
            rr = 1 if 1 <= st <= 15 else 0
            uh = UH[st % 2]
            a, b = t0 - 2 * lr, t0 + 256 + 2 * rr
            P.ld(uh[:, :, 2 - 2 * lr:258 + 2 * rr], UT.t[:, :, a:b].rearrange("k p t -> p k t"),
                 UTr[st])
            if not lr:
                P.memset("pool", uh[:, :, 0:2], 0.0)
            if not rr:
                P.memset("pool", uh[:, :, 258:260], 0.0)
        load_uh(0)
        ntr = 0
        for st in range(NST):
            if st + 1 < NST:
                load_uh(st + 1)
            uh = UH[st % 2]
            t0 = st * 256
            for x in range(4):
                for k in range(8):
                    P.mm(pp[x].all(), WX[x][:, k, :], uh[:, k, :], start=(k == 0), stop=(k == 7))
            for x in range(4):
                ch = (cols[x] // 128)
                cw = lambda tap: vcol(V_CW + 160 * j + ch * 5 + tap)
                P.tsc("dve", CV[x].all(), pp[x][:, 0:256], cw(0), ALU.mult)
                for tap in range(1, 5):
                    P.stt(CV[x].all(), pp[x][:, tap:tap + 256], cw(tap), CV[x].all(), ALU.mult, ALU.add)
                if x < 2:
                    P.act(CV[x].all(), CV[x].all(), AF.Silu)
                    P.act(SQc.all(), CV[x].all(), AF.Square)
                    P.mm(pss.all(), cs(C_ONE), SQc.all())
                    rsq(RSc.all(), pss.all())
                    dst = (QNT if x == 0 else KNT)[:, t0:t0 + 256]
                    P.stt(dst, CV[x].all(), float(128.0 ** -0.5) if x == 0 else 1.0, RSc.all(), ALU.mult, ALU.mult)
                    if x == 1:
                        pt = ptr[ntr % 2]
                        ntr += 1
                        for hf in range(2):
                            P.tr(pt[:, hf * 128:(hf + 1) * 128], KNT[:, t0 + hf * 128:t0 + (hf + 1) * 128], IDB.all())
                        P.cp("act", KTM[:, 2 * st:2 * st + 2, :], V(pt.bank, pt.ap[:, 0:256].rearrange("p (a b) -> p a b", a=2)))
                else:
                    vf = VF[x - 2]
                    P.act(vf.all(), CV[x].all(), AF.Silu)
                    pt = ptr[ntr % 2]
                    ntr += 1
                    for hf in range(2):
                        P.tr(pt[:, hf * 128:(hf + 1) * 128], vf[:, hf * 128:(hf + 1) * 128], IDB.all())
                    P.cp("act", VTM[:, x - 2, 2 * st:2 * st + 2, :], V(pt.bank, pt.ap[:, 0:256].rearrange("p (a b) -> p a b", a=2)))

        P.barrier()
        if dbg and jp == 0:
            DRD = P.dram("DRD", [128, 6528], F32, "ExternalOutput")
            DRC = P.dram("DRC", [128, 4352], F32, "ExternalOutput")
            P.st(DRD.t[:, :], P.region("drd"), RD.all())
            P.st(DRC.t[:, :], P.region("drc"), RC.all())
        NW = 1664

        def ctile(ch, k, dt=F32):
            if dt == F32:
                return P.view("c%df%d" % (ch, k), RE, ch * NW + 128 * k, 128, F32)
            return P.view("c%db%d" % (ch, k), RE, ch * NW + 1152 + 64 * k, 128, BF16)
        KK = [P.view("KK%d" % d, RI, 128 * d, 128, F32) for d in range(2)]
        KKO = [P.view("KKO%d" % d, RB, 128 * d, 128, F32) for d in range(2)]
        KKO1 = [P.view("KKO1%d" % d, RB, 3392 + 128 * d, 128, F32) for d in range(2)]
        QK = [P.view("QK%d" % d, RI, 256 + 128 * d, 128, F32) for d in range(2)]
        S = [P.view("S%d" % c, RI, 512 + 128 * c, 128, F32) for c in range(4)]
        Sb = [P.view("Sb%d" % c, RI, 1024 + 64 * c, 128, BF16) for c in range(4)]
        ONt = [P.view("ON%d" % c, RI, 1280 + 64 * c, 128, BF16) for c in range(4)]
        YPt = [P.view("YP%d" % c, RI, 1536 + 64 * c, 128, BF16) for c in range(4)]
        JK = P.view("JK", RI, 1792, 128, BF16)
        SSt = P.view("SSt", RI, 1856, 8, F32)
        for c in range(4):
            P.memset("pool", S[c].all(), 0.0)
            P.memset("pool", Sb[c].all(), 0.0)
        pkq = [psv("pkq%d" % d, 4 + d, 0, 256) for d in range(2)]
        pfin = psv("pfin", 7, 0, 512, BF16)
        gob = VEC[:, V_GO + 128 * j:V_GO + 128 * (j + 1)]
        fin_n = {"n": 0}

        def finish(ch, e, m):
            hh = 2 * jp + e
            o = OACC[:, e, m, :]
            ss = SSt[:, ch:ch + 1]
            P.act(JK.all(), o, AF.Square, accum=ss)
            P.act(ss, ss, AF.Sqrt, scale=1.0 / 128.0, bias=SMALL[:, 1:2])
            P.op("dve", lambda e_: e_.reciprocal(out=ss.ap, in_=ss.ap), reads=[SSt], writes=[SSt])
            P.stt(ONt[ch].all(), o, ss, gob, ALU.mult, ALU.mult)
            q = fin_n["n"] % 4
            fin_n["n"] += 1
            P.tr(pfin[:, q * 128:(q + 1) * 128], ONt[ch].all(), IDB.all())
            P.cp("act", YPt[ch].all(), pfin[:, q * 128:(q + 1) * 128])
            P.st(YPT.t[hh, :, m * 128:(m + 1) * 128], YPr[m // 2], YPt[ch].all())

        def f(v):
            return V(v.t, v.ap.bitcast(F32))

        def unit(ch, e, d, m, first):
            hh = 2 * jp + e
            col = d * 16 + hh
            sc = lambda X: X[:, m, col:col + 1]
            Ud = cs(C_UI) if d == 0 else cs(C_LI)
            Vd = cs(C_LS) if d == 0 else cs(C_US)
            base = ch * 384
            rb = ch * 1280
            NNs = [P.view("c%dN%d" % (ch, k), RR, rb + 128 * k, 128, F32R) for k in range(2)]
            NTYs = [P.view("c%dNTY%d" % (ch, k), RR, rb + 256 + 256 * k, 256, F32R) for k in range(2)]
            NOt = P.view("c%dNO" % ch, RR, rb + 768, 128, F32R)
            TDt = P.view("c%dTD" % ch, RR, rb + 896, 128, F32R)
            Mtt = P.view("c%dMt" % ch, RR, rb + 1024, 128, F32R)
            NO1t = P.view("c%dNO1" % ch, RR, rb + 1152, 128, F32R)
            UG = P.view("c%dUG" % ch, RE, base, 128, F32).all()
            Dm = P.view("c%dDm" % ch, RE, base + 128, 128, F32).all()
            O1 = P.view("c%dO1" % ch, RE, base + 256, 128, F32).all()
            Tt, Pm, PTt, KBG, VB, WTN, VN, VND = [P.view("c%db%d" % (ch, k), RB, 1344 + ch * 512 + 64 * k, 128, BF16)
                                                  for k in range(8)]
            pA = psv("", ch, 0, 128)
            pB = psv("", ch, 128, 128)
            pC = psv("", ch, 256, 128)
            pD = psv("", ch, 384, 128)
            pCD = psv("", ch, 256, 256)
            pCb = psv("", ch, 256, 256, BF16)
            msl = slice(m * 128, (m + 1) * 128)
            if e == 0:
                P.mm(pkq[d][:, 0:128], KNT[:, msl], KNT[:, msl])
                P.mm(pkq[d][:, 128:256], QNT[:, msl], KNT[:, msl])
                P.tt("dve", KK[d].all(), pkq[d][:, 0:128], cs(C_NLSD) if d == 0 else cs(C_NUSD), ALU.mult)
                P.tt("dve", KKO[d].all(), pkq[d][:, 0:128], cs(C_NLSO) if d == 0 else cs(C_NUSO), ALU.mult)
                P.tt("dve", KKO1[d].all(), pkq[d][:, 0:128], cs(C_NLSO1) if d == 0 else cs(C_NUSO1), ALU.mult)
                P.tt("dve", QK[d].all(), pkq[d][:, 128:256], cs(C_LI) if d == 0 else cs(C_UI), ALU.mult)
            P.tsc("dve", UG, Ud, sc(G), ALU.mult)
            P.mm(pA.all(), UG, Vd)
            P.cp("pool", NTYs[0][:, 128:256], cs(C_ID))
            yield
            P.act(Dm, pA.all(), AF.Exp)
            yield
            P.stt(NNs[0].all(), Dm, sc(BETA), KK[d].all(), ALU.mult, ALU.mult)
            P.stt(NO1t.all(), Dm, sc(BETA), KKO1[d].all(), ALU.mult, ALU.mult)
            P.stt(NOt.all(), Dm, sc(BETA), KKO[d].all(), ALU.mult, ALU.mult)
            P.tt("pool", Pm.all(), Dm, QK[d].all(), ALU.mult)
            P.tsc("dve", KBG.all(), KTM[:, m, :], sc(BEG), ALU.mult)
            P.tsc("dve", VB.all(), VTM[:, e, m, :], sc(BETA), ALU.mult)
            yield
            P.tr(pB.all(), f(NNs[0].all()), cs(C_ID))
            P.tr(pCb[:, 0:128], Pm.all(), IDB.all())
            yield
            P.cp("act", NTYs[0][:, 0:128], pB.all())
            P.cp("act", PTt.all(), pCb[:, 0:128])
            yield
            a = 0
            for k in range(5):
                N, NTY = NNs[a], NTYs[a]
                N2, NTY2 = NNs[1 - a], NTYs[1 - a]
                if k < 4:
                    P.mm(pA.all(), NTY[:, 0:128], N.all())
                    P.mm(pCD.all(), N.all(), NTY.all())
                    yield
                    P.cp("act", N2.all(), pA.all())
                    P.cp("act", NTY2[:, 0:128], pCD[:, 0:128])
                    P.tt("dve", NTY2[:, 128:256], pCD[:, 128:256], f(NTY[:, 128:256]), ALU.add)
                    yield
                else:
                    P.mm(pC.all(), N.all(), NTY[:, 128:256])
                    yield
                    P.tt("dve", NTY2[:, 128:256], pC.all(), f(NTY[:, 128:256]), ALU.add)
                    yield
                a = 1 - a
            for (NOx, lastm) in ((NO1t, False), (NOt, True)):
                Yd = NTYs[a][:, 128:256]
                P.tr(pA.all(), f(Yd), cs(C_ID))
                P.mm(pB.all(), NOx.all(), Yd)
                yield
                P.cp("act", TDt.all(), pA.all())
                P.cp("dve", Mtt.all(), pB.all())
                yield
                P.mm(pC.all(), TDt.all(), Mtt.all())
                yield
                if lastm:
                    P.tt("dve", Tt.all(), pC.all(), f(Yd), ALU.add)
                else:
                    P.tt("dve", NTYs[1 - a][:, 128:256], pC.all(), f(Yd), ALU.add)
                    a = 1 - a
                yield
            P.mm(pA.all(), KBG.all(), Tt.all())
            yield
            P.act(WTN.all(), pA.all(), AF.Identity, scale=-1.0)
            yield
            P.mm(pB.all(), Tt.all(), VB.all(), start=True, stop=False)
            P.mm(pB.all(), WTN.all(), Sb[ch].all(), start=False, stop=True)
            P.mm(pC.all(), QNT[:, msl], Sb[ch].all())
            yield
            P.cp("dve", VN.all(), pB.all())
            P.act(VND.all(), pB.all(), AF.Identity, scale=sc(EGL))
            P.act(O1, pC.all(), AF.Identity, scale=sc(EG))
            yield
            P.mm(pD.all(), PTt.all(), VN.all())
            P.mm(pA.all(), KTM[:, m, :], VND.all())
            yield
            if not first:
                P.tt("pool", O1, O1, OACC[:, e, m, :], ALU.add)
            P.tt("dve", OACC[:, e, m, :], pD.all(), O1, ALU.add)
            P.stt(S[ch].all(), S[ch].all(), sc(ET), pA.all(), ALU.mult, ALU.add)
            yield
            P.cp("act", Sb[ch].all(), S[ch].all())
            if not first:
                finish(ch, e, m)

        chains = [[], [], [], []]
        steps = [(0, 1, True), (1, 0, False)] + [(2 + n_, 33 - n_, n_ < 16) for n_ in range(32)]
        for (mf, mb, first) in steps:
            chains[0].append((0, 0, mf, first))
            chains[1].append((1, 0, mf, first))
            chains[2].append((0, 1, mb, first))
            chains[3].append((1, 1, mb, first))
        gens = [None] * 4
        pos = [0] * 4
        live = True
        rnd = 0
        delay = {0: 0, 2: 0, 1: 0, 3: 0}
        while live:
            live = False
            rnd += 1
            for ch in (0, 2, 1, 3):
                if rnd <= delay[ch]:
                    live = True
                    continue
                if gens[ch] is None:
                    if pos[ch] >= len(chains[ch]):
                        continue
                    e, d, m, first = chains[ch][pos[ch]]
                    pos[ch] += 1
                    gens[ch] = unit(ch, e, d, m, first)
                live = True
                try:
                    next(gens[ch])
                except StopIteration:
                    gens[ch] = None

    P.memset("pool", SMALL[:, 0:1], -8.0)
    P.memset("pool", SMALL[:, 1:2], EPS)
    P.memset("pool", SMALL[:, 2:3], 1.0)
    for i in layers:
        stage_mod(i)
        stage_norm(i, hT0 if i == layers[0] else HT, HTr)
        if upto == "norm":
            continue
        if i % 2 == 1:
            stage_attn(i)
            if upto == "attn":
                continue
            stage_out(i, att_w_out.t[i // 2], 8, False, i == DEPTH - 1)
        else:
            stage_dn(i)
            if upto == "dn":
                continue
            stage_out(i, dn_w_out.t[i // 2], 16, True, False)
    P.barrier()
    P.emit()
    return nc, P


def _host_inputs(inputs):
    cstt, ropec, ropes = _const_tables()
    f = lambda a: np.ascontiguousarray(np.asarray(a, dtype=np.float32))
    x, c, ctx, c_ctx = f(inputs["x"]), f(inputs["c"]), f(inputs["ctx"]), f(inputs["c_ctx"])
    col = lambda v: v.reshape(-1, 128).T
    maps = []
    for b in range(8):
        vec = np.zeros((128, NV), np.float32)
        cc = np.stack([col(c[b]), col(c_ctx)], -1)
        vec[:, V_C:V_C + 16] = cc.reshape(128, 16)
        ng = f(inputs["norm_g"])
        for l in range(4):
            vec[:, V_NG + 8 * l:V_NG + 8 * (l + 1)] = col(ng[l])
            ab = col(f(inputs["ada_b"])[l])
            vec[:, V_AB + 48 * l:V_AB + 48 * (l + 1)] = np.repeat(ab, 2, axis=1)
        vec[:, V_FG:V_FG + 8] = col(f(inputs["final_norm_g"]))
        cw = f(inputs["dn_conv_w"])
        for l in range(2):
            t = cw[l].T.reshape(32, 128, 5).transpose(1, 0, 2)
            vec[:, V_CW + 160 * l:V_CW + 160 * (l + 1)] = t.reshape(128, 160)
            vec[:, V_QG + l] = f(inputs["att_q_norm_g"])[l]
            vec[:, V_KG + l] = f(inputs["att_k_norm_g"])[l]
            vec[:, V_GO + 128 * l:V_GO + 128 * (l + 1)] = f(inputs["dn_o_norm_g"])[l][None, :]
            vec[:, V_AL + 32 * l:V_AL + 32 * (l + 1)] = f(inputs["dn_a_log"])[l].reshape(1, 32)
            vec[:, V_DT + 32 * l:V_DT + 32 * (l + 1)] = f(inputs["dn_dt_bias"])[l].reshape(1, 32)
        h0 = np.concatenate([ctx[b], x[b]], 0)
        hT = np.ascontiguousarray(h0.T).reshape(8, 128, TT)
        maps.append({
            "hT0": hT, "vecs": vec, "cst": cstt, "ropeC": ropec, "ropeS": ropes,
            "ada_w": f(inputs["ada_w"]), "dn_w_in": f(inputs["dn_w_in"]), "dn_w_out": f(inputs["dn_w_out"]),
            "att_w_in": f(inputs["att_w_in"]), "att_w_out": f(inputs["att_w_out"]),
        })
    return maps


def kernel(**inputs):
    nc, _ = build()
    maps = _host_inputs(inputs)
    res = run_bass_kernel_spmd(nc, maps, core_ids=list(range(8)))
    out = np.stack([np.asarray(r["outT"]).reshape(1024, 4096).T for r in res.results], 0)
    return np.ascontiguousarray(out.astype(np.float32))
```

```python
import numpy as np
import ml_dtypes
import concourse.bass as bass
import concourse.mybir as mybir
from concourse.bass_utils import run_bass_kernel_spmd
from contextlib import ExitStack

F32 = mybir.dt.float32
BF16 = mybir.dt.bfloat16
F32R = mybir.dt.float32r
AF = mybir.ActivationFunctionType
ALU = mybir.AluOpType
CENG = ("pe", "act", "dve", "pool")
ENGS = ("pe", "act", "dve", "pool", "sp")


class V:
    __slots__ = ("t", "ap")

    def __init__(self, t, ap):
        self.t = t
        self.ap = ap


class T:
    __slots__ = ("name", "t", "w", "r", "key", "space")

    def __init__(self, name, t, space, key=None):
        self.name = name
        self.t = t
        self.space = space
        self.w = []
        self.r = []
        self.key = key or name

    def __getitem__(self, k):
        return V(self, self.t[k])

    def all(self):
        return V(self, self.t[:])


class Prog:
    def __init__(self, nc):
        self.nc = nc
        self.es = ExitStack()
        self.ops = {e: [] for e in ENGS}
        self.esem = {e: self.es.enter_context(nc.semaphore("s_" + e)) for e in CENG}
        self.dsems = {}
        self.dcnt = {}
        self.extra = {e: [] for e in ENGS}
        self.ro = T("ro", None, "dram")
        self.vcache = {}

    def sb(self, name, shape, dt):
        return T(name, self.es.enter_context(self.nc.sbuf_tensor(name, list(shape), dt)), "sb")

    def ps(self, name, shape, dt):
        return T(name, self.es.enter_context(self.nc.psum_tensor(name, list(shape), dt)), "ps")

    def dram(self, name, shape, dt, kind="Internal"):
        return T(name, self.nc.dram_tensor(name, list(shape), dt, kind=kind).ap(), "dram")

    def region(self, name):
        return T(name, None, "dram")

    def view(self, name, raw, lo, n, dt, pat=None, **kw):
        ck = (raw.name, lo, n, str(dt), pat, tuple(sorted(kw.items())))
        if ck in self.vcache:
            return self.vcache[ck]
        words = n if dt in (F32, F32R) else n // 2
        ap = raw.t[:, lo:lo + words]
        if dt != ap.dtype:
            ap = ap.bitcast(dt)
        if pat:
            ap = ap.rearrange(pat, **kw)
        t = T(name, ap, raw.space, key="%s@%d" % (raw.name, lo))
        self.vcache[ck] = t
        return t

    def _deps(self, eng, reads, writes, shared=()):
        deps = list(self.extra[eng])
        self.extra[eng] = []
        for t in reads:
            deps.extend(t.w)
        for t in shared:
            deps.extend(t.r)
        for t in writes:
            deps.extend(t.w)
            deps.extend(t.r)
        out = []
        for d in deps:
            if d[0] == "E":
                if d[1] == "pe" and eng == "pe":
                    continue
                self.ops[d[1]][d[2]]["flag"] = True
            out.append(d)
        return out

    def _mark(self, me, reads, writes, shared=()):
        for t in reads:
            t.r.append(me)
            if len(t.r) > 64:
                t.r = t.r[-48:]
        for t in writes:
            t.w = [me]
            t.r = []
        for t in shared:
            t.w.append(me)

    def op(self, eng, fn, reads=(), writes=()):
        writes = list(dict.fromkeys(list(writes) + [x for x in reads if x.space == "ps"]))
        reads = [x for x in dict.fromkeys(reads) if x is not self.ro and x.space != "ps"]
        deps = self._deps(eng, reads, writes)
        idx = len(self.ops[eng])
        self.ops[eng].append(dict(fn=fn, waits=deps, flag=False, dma=None))
        self._mark(("E", eng, idx), reads, writes)

    def dma(self, fn, reads=(), writes=(), shared=(), q="sp"):
        reads = [x for x in reads if x is not self.ro]
        deps = self._deps(q, reads, writes, shared)
        st = [t for t in list(writes) + list(shared) + list(reads) if t.space != "dram"][0]
        if st.key not in self.dsems:
            self.dsems[st.key] = self.es.enter_context(self.nc.semaphore("d%d" % len(self.dsems)))
            self.dcnt[st.key] = 0
        sem = self.dsems[st.key]
        if self.dcnt[st.key]:
            deps.append(("D", sem, self.dcnt[st.key]))
        self.dcnt[st.key] += 16
        me = ("D", sem, self.dcnt[st.key])
        self.ops[q].append(dict(fn=fn, waits=deps, flag=False, dma=sem))
        self._mark(me, reads, writes, shared)
        return me

    def barrier(self):
        deps = []
        for e in CENG:
            if self.ops[e]:
                i = len(self.ops[e]) - 1
                self.ops[e][i]["flag"] = True
                deps.append(("E", e, i))
        for k, sem in self.dsems.items():
            deps.append(("D", sem, self.dcnt[k]))
        for e in ENGS:
            self.extra[e] = list(deps) + self.extra[e]

    @staticmethod
    def _ts(*vs):
        return [v.t for v in vs if isinstance(v, V)]

    @staticmethod
    def _a(v):
        return v.ap if isinstance(v, V) else v

    def mm(self, out, lhsT, rhs, start=True, stop=True):
        self.op("pe", lambda e: e.matmul(out.ap, lhsT=lhsT.ap, rhs=rhs.ap, start=start, stop=stop),
                reads=self._ts(lhsT, rhs), writes=self._ts(out))

    def tr(self, out, in_, ident):
        self.op("pe", lambda e: e.transpose(out.ap, in_.ap, ident.ap),
                reads=self._ts(in_, ident), writes=self._ts(out))

    def act(self, out, in_, func, scale=None, bias=None, accum=None):
        kw = {}
        if scale is not None:
            kw["scale"] = self._a(scale)
        if bias is not None:
            kw["bias"] = self._a(bias)
        if accum is not None:
            kw["accum_out"] = accum.ap
        self.op("act", lambda e: e.activation(out=out.ap, in_=in_.ap, func=func, **kw),
                reads=self._ts(in_, scale, bias), writes=self._ts(out, accum))

    def tt(self, eng, out, in0, in1, op):
        self.op(eng, lambda e: e.tensor_tensor(out=out.ap, in0=in0.ap, in1=in1.ap, op=op),
                reads=self._ts(in0, in1), writes=self._ts(out))

    def tsc(self, eng, out, in0, s1, op0, s2=None, op1=None):
        if op1 is None:
            fn = lambda e: e.tensor_scalar(out=out.ap, in0=in0.ap, scalar1=self._a(s1), scalar2=None, op0=op0)
        else:
            fn = lambda e: e.tensor_scalar(out=out.ap, in0=in0.ap, scalar1=self._a(s1),
                                           scalar2=self._a(s2), op0=op0, op1=op1)
        self.op(eng, fn, reads=self._ts(in0, s1, s2), writes=self._ts(out))

    def stt(self, out, in0, scalar, in1, op0, op1):
        self.op("dve", lambda e: e.scalar_tensor_tensor(out=out.ap, in0=in0.ap, scalar=self._a(scalar),
                                                         in1=in1.ap, op0=op0, op1=op1),
                reads=self._ts(in0, scalar, in1), writes=self._ts(out))

    def cp(self, eng, out, in_):
        if eng == "act":
            self.op("act", lambda e: e.copy(out=out.ap, in_=in_.ap), reads=self._ts(in_), writes=self._ts(out))
        else:
            self.op(eng, lambda e: e.tensor_copy(out=out.ap, in_=in_.ap), reads=self._ts(in_),
                    writes=self._ts(out))

    def memset(self, eng, out, val):
        self.op(eng, lambda e: e.memset(out.ap, val), writes=self._ts(out))

    def ld(self, out, src_ap, src_t=None, q="sp"):
        return self.dma(lambda e: e.dma_start(out=out.ap, in_=src_ap), reads=[src_t or self.ro],
                        writes=[out.t], q=q)

    def st(self, dst_ap, dst_t, in_, shared=True, q="sp"):
        if shared:
            return self.dma(lambda e: e.dma_start(out=dst_ap, in_=in_.ap), reads=[in_.t], shared=[dst_t], q=q)
        return self.dma(lambda e: e.dma_start(out=dst_ap, in_=in_.ap), reads=[in_.t], writes=[dst_t], q=q)

    def emit(self):
        nc = self.nc
        cum = {}
        for e in CENG:
            c = 0
            arr = []
            for o in self.ops[e]:
                if o["flag"]:
                    c += 1
                arr.append(c)
            cum[e] = arr
        self.stats = {}

        def run(e, eng):
            seen = {}
            nw = 0
            for o in self.ops[e]:
                for d in o["waits"]:
                    if d[0] == "E":
                        sem, val, key = self.esem[d[1]], cum[d[1]][d[2]], d[1]
                    else:
                        sem, val, key = d[1], d[2], id(d[1])
                    if seen.get(key, 0) >= val:
                        continue
                    seen[key] = val
                    eng.wait_ge(sem, val)
                    nw += 1
                ins = o["fn"](eng)
                if o["dma"] is not None:
                    ins.then_inc(o["dma"], 16)
                elif o["flag"]:
                    ins.then_inc(self.esem[e], 1)
            if e == "sp":
                for k, sem in self.dsems.items():
                    if seen.get(id(sem), 0) < self.dcnt[k]:
                        eng.wait_ge(sem, self.dcnt[k])
            self.stats[e] = (len(self.ops[e]), nw)

        with nc.Block() as block:
            @block.sync
            def _(eng):
                run("sp", eng)

            @block.tensor
            def _(eng):
                run("pe", eng)

            @block.scalar
            def _(eng):
                run("act", eng)

            @block.vector
            def _(eng):
                run("dve", eng)

            @block.gpsimd
            def _(eng):
                run("pool", eng)
        self.es.close()


D = 1024
TT = 4352
NCH = 34
NST = 17
EPS = 1e-6
DEPTH = 4

(C_ID, C_LS, C_LI, C_US, C_UI, C_NLSD, C_NUSD, C_ONE, C_O1024, C_O128, C_RT,
 C_NLSO, C_NUSO, C_NLSO1, C_NUSO1) = [i * 128 for i in range(15)]
NCST = 15 * 128

V_C = 0
V_NG = V_C + 16
V_AB = V_NG + 32
V_FG = V_AB + 192
V_CW = V_FG + 8
V_QG = V_CW + 320
V_KG = V_QG + 2
V_GO = V_KG + 2
V_AL = V_GO + 256
V_DT = V_AL + 64
NV = V_DT + 64


def _const_tables():
    p = np.arange(128)[:, None]
    f = np.arange(128)[None, :]
    t = np.zeros((128, NCST), np.float32)
    t[:, C_ID:C_ID + 128] = (p == f)
    t[:, C_LS:C_LS + 128] = (p > f)
    t[:, C_LI:C_LI + 128] = (p >= f)
    t[:, C_US:C_US + 128] = (p < f)
    t[:, C_UI:C_UI + 128] = (p <= f)
    bd = (p // 64) == (f // 64)
    bd32 = (p // 32) == (f // 32)
    t[:, C_NLSD:C_NLSD + 128] = -((p > f) & bd32).astype(np.float32)
    t[:, C_NUSD:C_NUSD + 128] = -((p < f) & bd32).astype(np.float32)
    t[:, C_NLSO1:C_NLSO1 + 128] = -((p > f) & bd & ~bd32).astype(np.float32)
    t[:, C_NUSO1:C_NUSO1 + 128] = -((p < f) & bd & ~bd32).astype(np.float32)
    t[:, C_NLSO:C_NLSO + 128] = -((p > f) & ~bd).astype(np.float32)
    t[:, C_NUSO:C_NUSO + 128] = -((p < f) & ~bd).astype(np.float32)
    t[:, C_ONE:C_ONE + 128] = 1.0
    t[:, C_O1024:C_O1024 + 128] = 1.0 / 1024.0
    t[:, C_O128:C_O128 + 128] = 1.0 / 128.0
    R = np.zeros((128, 128), np.float32)
    for m in range(128):
        if (m % 64) < 32:
            R[m, m + 32] = -1.0
        else:
            R[m, m - 32] = 1.0
    t[:, C_RT:C_RT + 128] = R.T
    tok = np.arange(4096)
    row = (tok // 64).astype(np.float32)
    col = (tok % 64).astype(np.float32)
    inv = (10000.0 ** (-np.arange(0, 64, 2, dtype=np.float32) / 64.0)).astype(np.float32)
    ang_r = row[None, :] * inv[:, None]
    ang_c = col[None, :] * inv[:, None]
    ang = np.concatenate([ang_r, ang_r, ang_c, ang_c], 0).astype(np.float32)
    return t, np.cos(ang).astype(np.float32), np.sin(ang).astype(np.float32)


def build(layers=(0, 1, 2, 3), dbg=False, upto="out", pairs=range(8), skip_z=False):
    nc = bass.Bass("TRN2", target_bir_lowering=False)
    P = Prog(nc)
    RO = P.ro
    EI = "ExternalInput"
    hT0 = P.dram("hT0", [8, 128, TT], F32, EI)
    vecs = P.dram("vecs", [128, NV], F32, EI)
    cst = P.dram("cst", [128, NCST], F32, EI)
    ropeC = P.dram("ropeC", [128, 4096], F32, EI)
    ropeS = P.dram("ropeS", [128, 4096], F32, EI)
    ada_w = P.dram("ada_w", [4, 1024, 3072], F32, EI)
    dn_w_in = P.dram("dn_w_in", [2, 1024, 6208], F32, EI)
    dn_w_out = P.dram("dn_w_out", [2, 2048, 1024], F32, EI)
    att_w_in = P.dram("att_w_in", [2, 1024, 2560], F32, EI)
    att_w_out = P.dram("att_w_out", [2, 1024, 1024], F32, EI)
    outT = P.dram("outT", [8, 128, 4096], F32, "ExternalOutput")
    IK = "ExternalOutput" if dbg else "Internal"
    HT = P.dram("HT", [8, 128, TT], F32, IK)
    UT = P.dram("UT", [8, 128, TT], BF16, IK)
    YPT = P.dram("YPT", [16, 128, TT], BF16, IK)
    ZST = P.dram("ZST", [16, 128, TT], BF16, IK)
    HTr = [P.region("HTr%d" % i) for i in range(NST)]
    UTr = [P.region("UTr%d" % i) for i in range(NST)]
    YPr = [P.region("YPr%d" % i) for i in range(NST)]
    ZSr = [P.region("ZSr%d" % i) for i in range(NST)]
    OUTr = P.region("OUTr")
    dbg_out = {}

    CST = P.sb("CST", [128, NCST], F32)
    VEC = P.sb("VEC", [128, NV], F32)
    IDB = P.sb("IDB", [128, 128], BF16)
    ONEB = P.sb("ONEB", [128, 128], BF16)
    SC = P.sb("SC", [128, 16], F32)
    MOD = P.sb("MOD", [128, 48], F32)
    GSC = P.sb("GSC", [128, 16], F32)
    SMALL = P.sb("SMALL", [128, 64], F32)
    RA = P.sb("RA", [128, 8704], F32)
    RB = P.sb("RB", [128, 4480], F32)
    RC = P.sb("RC", [128, 4352], F32)
    RD = P.sb("RD", [128, 6528], F32)
    RE = P.sb("RE", [128, 4096], F32)
    RR = P.sb("RR", [128, 5120], F32R)
    RF = P.sb("RF", [128, 5440], F32)
    RG = P.sb("RG", [128, 5120], F32)
    RH = P.sb("RH", [128, 3072], F32)
    RI = P.sb("RI", [128, 2048], F32)
    PSB = [P.ps("PSB%d" % i, [128, 512], F32) for i in range(8)]

    def cs(col, n=128):
        return CST[:, col:col + n]

    def vcol(col):
        return VEC[:, col:col + 1]

    def rsq(dst, src):
        P.act(dst, src, AF.Sqrt, bias=SMALL[:, 1:2])
        P.op("dve", lambda e: e.reciprocal(out=dst.ap, in_=dst.ap), reads=[dst.t], writes=[dst.t])

    class PV:
        def __init__(self, bank, ap):
            self.bank, self.ap = bank, ap

        def __getitem__(self, k):
            return V(self.bank, self.ap[k])

        def all(self):
            return V(self.bank, self.ap)

    def psv(name, bank, lo, n, dt=F32):
        words = n if dt == F32 else n // 2
        ap = PSB[bank].t[:, lo:lo + words]
        if dt != F32:
            ap = ap.bitcast(dt)
        return PV(PSB[bank], ap)

    P.ld(CST.all(), cst.t[:, :])
    P.ld(VEC.all(), vecs.t[:, :])
    P.cp("dve", IDB.all(), cs(C_ID))
    P.cp("dve", ONEB.all(), cs(C_ONE))
    P.act(SC.all(), VEC[:, V_C:V_C + 16], AF.Silu)

    def stage_mod(i):
        P.barrier()
        WA = [P.view("WA%d" % b, RG, b * 1024, 1024, F32, "p (k n) -> p k n", k=8) for b in range(2)]
        pm = psv("pm", 0, 0, 48)
        aw = ada_w.t[i].rearrange("(k p) n -> p k n", p=128)
        for m in range(24):
            w = WA[m % 2]
            P.ld(w.all(), aw[:, :, m * 128:(m + 1) * 128])
            for k in range(8):
                P.mm(pm[:, 2 * m:2 * m + 2], w[:, k, :], SC[:, 2 * k:2 * k + 2], start=(k == 0), stop=(k == 7))
        P.tt("dve", MOD.all(), pm.all(), VEC[:, V_AB + 48 * i:V_AB + 48 * (i + 1)], ALU.add)
        for k in range(8):
            P.tsc("dve", GSC[:, 2 * k:2 * k + 2], MOD[:, 2 * (8 + k):2 * (8 + k) + 2], 1.0, ALU.add,
                  vcol(V_NG + 8 * i + k), ALU.mult)

    def stage_norm(i, src, srcr):
        P.barrier()
        HIN = [P.view("HIN%d" % b, RC, b * 2048, 2048, F32, "p (k n) -> p k n", k=8) for b in range(2)]
        SQ = P.view("SQ", RB, 0, 2048, F32, "p (k n) -> p k n", k=8)
        TMP = P.view("TMPn", RB, 2048, 2048, F32, "p (k n) -> p k n", k=8)
        UO = [P.view("UO%d" % b, RH, b * 1024, 2048, BF16, "p (k n) -> p k n", k=8) for b in range(2)]
        RS = P.view("RS", RI, 0, 256, F32)
        pn = psv("pn", 0, 0, 256)

        def load(st):
            P.ld(HIN[st % 2].all(), src.t[:, :, st * 256:(st + 1) * 256].rearrange("k p t -> p k t"), srcr[st])
        load(0)
        for st in range(NST):
            if st + 1 < NST:
                load(st + 1)
            h = HIN[st % 2]
            r = 1 if st == 0 else 0
            P.act(SQ.all(), h.all(), AF.Square)
            for k in range(8):
                P.mm(pn.all(), cs(C_O1024), SQ[:, k, :], start=(k == 0), stop=(k == 7))
            rsq(RS.all(), pn.all())
            uo = UO[st % 2]
            for k in range(8):
                P.tt("pool" if k % 2 else "dve", TMP[:, k, :], h[:, k, :], RS.all(), ALU.mult)
                P.act(uo[:, k, :], TMP[:, k, :], AF.Identity, scale=GSC[:, 2 * k + r:2 * k + r + 1],
                      bias=MOD[:, 2 * k + r:2 * k + r + 1])
            P.st(UT.t[:, :, st * 256:(st + 1) * 256].rearrange("k p t -> p k t"), UTr[st], uo.all(), shared=False)

    wstate = {"n": 0}

    def load_wchunk(dst, wsrc2d, c0, ncols=128, kc=8):
        b = wstate["n"] % 2
        wstate["n"] += 1
        stg = P.view("WSTG%d" % b, RG, 3072 + b * 1024, kc * ncols, F32, "p (k n) -> p k n", k=kc)
        P.ld(stg.all(), wsrc2d.rearrange("(k p) n -> p k n", p=128)[:, :, c0:c0 + ncols])
        P.cp("pool", dst.all(), stg.all())

    def stage_out(i, wout2d, kc, use_z, last):
        P.barrier()
        WO = P.view("WO", RA, 0, kc * 1024, BF16, "p (k n) -> p k n", k=kc)
        for c in range(8):
            for k0 in range(0, kc, 8):
                b = wstate["n"] % 2
                wstate["n"] += 1
                stg = P.view("WSTG%d" % b, RG, 3072 + b * 1024, 1024, F32, "p (k n) -> p k n", k=8)
                P.ld(stg.all(), wout2d.rearrange("(k p) n -> p k n", p=128)[:, k0:k0 + 8, c * 128:(c + 1) * 128])
                P.cp("pool", WO[:, k0:k0 + 8, c * 128:(c + 1) * 128], stg.all())
        YT = [P.view("YT%d" % b, RC, b * 2048, kc * 256, BF16, "p (k n) -> p k n", k=kc) for b in range(2)]
        ZT = [P.view("ZT%d" % b, RD, b * 2048, kc * 256, BF16, "p (k n) -> p k n", k=kc) for b in range(2)]
        HI = [P.view("HI%d" % b, RB, b * 2048, 2048, F32, "p (k n) -> p k n", k=8) for b in range(2)]
        HO = [P.view("HO%d" % b, RE, b * 2048, 2048, F32, "p (k n) -> p k n", k=8) for b in range(2)]
        SQ = P.view("SQo", RD, 4096, 2048, F32, "p (k n) -> p k n", k=8)
        RS = P.view("RSo", RI, 0, 256, F32)
        pn = psv("pno", 4, 0, 256)
        pys = [psv("py%d" % b, b, 0, 256) for b in range(4)]
        hsrc = hT0 if i == layers[0] else HT

        def load(st):
            b = st % 2
            P.ld(YT[b].all(), YPT.t[0:kc, :, st * 256:(st + 1) * 256].rearrange("k p t -> p k t"), YPr[st])
            if use_z:
                P.ld(ZT[b].all(), ZST.t[0:kc, :, st * 256:(st + 1) * 256].rearrange("k p t -> p k t"), ZSr[st])
            P.ld(HI[b].all(), hsrc.t[:, :, st * 256:(st + 1) * 256].rearrange("k p t -> p k t"), HTr[st])
        first = 1 if last else 0
        load(first)
        for st in range(first, NST):
            if st + 1 < NST:
                load(st + 1)
            b = st % 2
            r = 1 if st == 0 else 0
            y = YT[b]
            if use_z:
                P.tt("pool", y.all(), y.all(), ZT[b].all(), ALU.mult)
            ho = HO[b]
            for c in range(8):
                py = pys[c % 4]
                for k in range(kc):
                    P.mm(py.all(), WO[:, k, c * 128:(c + 1) * 128], y[:, k, :], start=(k == 0), stop=(k == kc - 1))
                P.stt(ho[:, c, :], py.all(), MOD[:, 2 * (16 + c) + r:2 * (16 + c) + r + 1], HI[b][:, c, :],
                      ALU.mult, ALU.add)
            if not last:
                P.st(HT.t[:, :, st * 256:(st + 1) * 256].rearrange("k p t -> p k t"), HTr[st], ho.all(), shared=False)
            else:
                P.act(SQ.all(), ho.all(), AF.Square)
                for k in range(8):
                    P.mm(pn.all(), cs(C_O1024), SQ[:, k, :], start=(k == 0), stop=(k == 7))
                rsq(RS.all(), pn.all())
                for k in range(8):
                    P.stt(SQ[:, k, :], ho[:, k, :], vcol(V_FG + k), RS.all(), ALU.mult, ALU.mult)
                P.st(outT.t[:, :, (st - 1) * 256:st * 256].rearrange("k p t -> p k t"), OUTr, SQ.all())

    def stage_attn(i):
        j = i // 2
        need_ctx = i < DEPTH - 1
        w2d = att_w_in.t[j]
        P.barrier()
        KT = P.view("KT", RA, 0, 2 * TT, BF16, "p (g t) -> p g t", g=2)
        VTM = P.view("VTM", RA, TT, 2 * TT, BF16, "p (g c d) -> p g c d", g=2, c=NCH)
        QT = P.view("QT", RB, 0, TT, BF16)
        ZS = P.view("ZS", RB, TT // 2, TT, BF16)
        UIN = [P.view("UIN%d" % b, RH, b * 1024, 2048, BF16, "p (k n) -> p k n", k=8) for b in range(2)]
        WQ = P.view("WQ", RG, 0, 1024, BF16, "p (k n) -> p k n", k=8)
        WZ = P.view("WZ", RG, 512, 1024, BF16, "p (k n) -> p k n", k=8)
        WV = P.view("WV", RG, 1024, 1024, BF16, "p (k n) -> p k n", k=8)
        XN = [P.view("XN%d" % b, RD, b * 256, 256, F32) for b in range(2)]
        SQ = [P.view("SQa%d" % b, RD, 512 + b * 256, 256, F32) for b in range(2)]
        RS = [P.view("RSa%d" % b, RD, 1024 + b * 256, 256, F32) for b in range(2)]
        T1 = [P.view("T1a%d" % b, RD, 1536 + b * 256, 256, F32) for b in range(2)]
        T2 = [P.view("T2a%d" % b, RD, 2048 + b * 256, 256, F32) for b in range(2)]
        CT = [P.view("CT%d" % b, RD, 2560 + b * 256, 256, F32) for b in range(2)]
        STb = [P.view("STb%d" % b, RD, 3072 + b * 256, 256, F32) for b in range(2)]
        PT = [P.view("PTa%d" % b, RE, b * 256, 512, BF16) for b in range(3)]
        RIV = P.view("RIV", RE, 1024, 512, F32)
        OT = P.view("OTa", RE, 1536, 512, F32)
        YO = [P.view("YO%d" % b, RE, 2048 + b * 256, 512, BF16) for b in range(2)]
        pqs = [psv("pq%d" % b, 4 + b, 0, 256) for b in range(2)]
        pzs = [psv("pz%d" % b, 2 + b, 0, 256) for b in range(2)]
        psss = [psv("pss%d" % b, 6 + b, 0, 256) for b in range(2)]
        prots = [psv("prot%d" % b, b, 0, 256) for b in range(2)]
        pv = [psv("pv%d" % b, 2 + b, 256, 128) for b in range(2)]

        def load_u(st):
            P.ld(UIN[st % 2].all(), UT.t[:, :, st * 256:(st + 1) * 256].rearrange("k p t -> p k t"), UTr[st])

        rope_n = {"n": 0}

        def proj_norm_rope(st, w, gcol, dst, extra_scale):
            u = UIN[st % 2]
            b2 = rope_n["n"] % 2
            rope_n["n"] += 1
            pq, pss, prot = pqs[b2], psss[b2], prots[b2]
            for k in range(8):
                P.mm(pq.all(), w[:, k, :], u[:, k, :], start=(k == 0), stop=(k == 7))
            P.act(SQ[b2].all(), pq.all(), AF.Square)
            P.mm(pss.all(), cs(C_O128), SQ[b2].all())
            rsq(RS[b2].all(), pss.all())
            if st == 0:
                P.stt(dst, pq.all(), gcol, RS[b2].all(), ALU.mult, ALU.mult)
                return
            P.stt(XN[b2].all(), pq.all(), gcol, RS[b2].all(), ALU.mult, ALU.mult)
            P.ld(CT[b2].all(), ropeC.t[:, (st - 1) * 256:st * 256])
            P.ld(STb[b2].all(), ropeS.t[:, (st - 1) * 256:st * 256])
            P.mm(prot.all(), cs(C_RT), XN[b2].all())
            P.tt("pool", T1[b2].all(), XN[b2].all(), CT[b2].all(), ALU.mult)
            P.tt("dve", T2[b2].all(), prot.all(), STb[b2].all(), ALU.mult)
            P.tt("pool", dst, T1[b2].all(), T2[b2].all(), ALU.add)

        for g in range(2):
            load_wchunk(WQ, w2d, 1024 + g * 128)
            load_wchunk(WV, w2d, 1280 + g * 128)
            load_u(0)
            for st in range(NST):
                if st + 1 < NST:
                    load_u(st + 1)
                proj_norm_rope(st, WQ, vcol(V_KG + j), KT[:, g, st * 256:(st + 1) * 256], None)
                u = UIN[st % 2]
                for half in range(2):
                    c = st * 2 + half
                    p = pv[half]
                    for k in range(8):
                        P.mm(p.all(), u[:, k, half * 128:(half + 1) * 128], WV[:, k, :], start=(k == 0), stop=(k == 7))
                    P.cp("act", VTM[:, g, c, :], p.all())
        pS = [psv("pS%d" % b, b, 0, 512) for b in range(2)]
        pOs = [psv("pO%d" % b, 2 + 2 * b, 0, 512) for b in range(2)]
        pSums = [psv("pSum%d" % b, 3 + 2 * b, 0, 512) for b in range(2)]
        zn = 0
        for h in range(8):
            g = h // 4
            load_wchunk(WQ, w2d, h * 128)
            load_wchunk(WZ, w2d, 1536 + h * 128)
            load_u(0)
            for st in range(NST):
                if st + 1 < NST:
                    load_u(st + 1)
                proj_norm_rope(st, WQ, vcol(V_QG + j), QT[:, st * 256:(st + 1) * 256], None)
                u = UIN[st % 2]
                pz = pzs[zn % 2]
                zn += 1
                for k in range(8):
                    P.mm(pz.all(), WZ[:, k, :], u[:, k, :], start=(k == 0), stop=(k == 7))
                P.act(ZS[:, st * 256:(st + 1) * 256], pz.all(), AF.Silu)
            blocks = ([(0, 256, 2)] if need_ctx else []) + [(256 + 512 * qb, 512, NCH) for qb in range(8)]
            for bi, (q0, nq, nk) in enumerate(blocks):
                pO, pSum = pOs[bi % 2], pSums[bi % 2]

                def score(kt):
                    P.mm(pS[kt % 2][:, 0:nq], KT[:, g, kt * 128:(kt + 1) * 128], QT[:, q0:q0 + nq])
                score(0)
                for kt in range(nk):
                    if kt + 1 < nk:
                        score(kt + 1)
                    pt = PT[kt % 3]
                    P.act(pt[:, 0:nq], pS[kt % 2][:, 0:nq], AF.Exp, scale=float(128.0 ** -0.5), bias=SMALL[:, 0:1])
                    P.mm(pO[:, 0:nq], VTM[:, g, kt, :], pt[:, 0:nq], start=(kt == 0), stop=(kt == nk - 1))
                    P.mm(pSum[:, 0:nq], ONEB.all(), pt[:, 0:nq], start=(kt == 0), stop=(kt == nk - 1))
                P.op("dve", lambda e, o=RIV[:, 0:nq], s=pSum[:, 0:nq]: e.reciprocal(out=o.ap, in_=s.ap),
                     reads=[pSum.bank], writes=[RIV])
                P.tt("dve", OT[:, 0:nq], pO[:, 0:nq], RIV[:, 0:nq], ALU.mult)
                yo = YO[bi % 2]
                P.tt("pool", yo[:, 0:nq], OT[:, 0:nq], ZS[:, q0:q0 + nq], ALU.mult)
                regs = [YPr[(q0 + s * 256) // 256] for s in range(nq // 256)]
                P.dma(lambda e, d=YPT.t[h, :, q0:q0 + nq], y_=yo[:, 0:nq]: e.dma_start(out=d, in_=y_.ap),
                      reads=[yo], shared=regs)


    def stage_dn(i):
        j = i // 2
        w2d = dn_w_in.t[j]
        P.barrier()
        UIN = [P.view("UIN%d" % b, RH, b * 1024, 2048, BF16, "p (k n) -> p k n", k=8) for b in range(2)]
        BETA = P.view("BETA", RF, 0, 1088, F32, "p (c h) -> p c h", c=NCH)
        G = P.view("G", RF, 1088, 1088, F32, "p (c h) -> p c h", c=NCH)
        EGL = P.view("EGL", RF, 2176, 1088, F32, "p (c h) -> p c h", c=NCH)
        EG = P.view("EG", RF, 3264, 1088, F32, "p (c h) -> p c h", c=NCH)
        ET = P.view("ET", RF, 4352, 1088, F32, "p (c h) -> p c h", c=NCH)
        NEGA = P.view("NEGA", RI, 0, 32, F32)

        def load_u(st):
            P.ld(UIN[st % 2].all(), UT.t[:, :, st * 256:(st + 1) * 256].rearrange("k p t -> p k t"), UTr[st])

        WAB = P.view("WAB", RG, 0, 512, BF16, "p (k n) -> p k n", k=8)
        load_wchunk(WAB, w2d, 6144, ncols=64)
        pab = psv("pab", 6, 0, 64)
        pgc = psv("pgc", 7, 0, 64)
        P.act(NEGA.all(), VEC[:, V_AL + 32 * j:V_AL + 32 * (j + 1)], AF.Exp)
        P.tsc("dve", NEGA.all(), NEGA.all(), -1.0, ALU.mult)
        load_u(0)
        for st in range(NST):
            if st + 1 < NST:
                load_u(st + 1)
            u = UIN[st % 2]
            for half in range(2):
                c = st * 2 + half
                for k in range(8):
                    P.mm(pab.all(), u[:, k, half * 128:(half + 1) * 128], WAB[:, k, :], start=(k == 0), stop=(k == 7))
                P.cp("dve", BETA[:, c, :], pab[:, 0:32])
                P.tt("dve", G[:, c, :], pab[:, 32:64], VEC[:, V_DT + 32 * j:V_DT + 32 * (j + 1)], ALU.add)
        P.act(BETA.all(), BETA.all(), AF.Sigmoid)
        P.act(G.all(), G.all(), AF.Exp)
        P.act(G.all(), G.all(), AF.Ln, bias=SMALL[:, 2:3])
        for c in range(NCH):
            P.tt("pool", G[:, c, :], G[:, c, :], NEGA.all(), ALU.mult)
            P.mm(pgc[:, 0:16], cs(C_UI), G[:, c, 0:16])
            P.mm(pgc[:, 16:32], cs(C_LI), G[:, c, 16:32])
            P.mm(pgc[:, 32:64], cs(C_ONE), G[:, c, :])
            P.cp("dve", EGL[:, c, :], pgc[:, 0:32])
            P.cp("dve", ET[:, c, :], pgc[:, 32:64])
        P.act(EG.all(), EGL.all(), AF.Exp)
        P.tt("pool", EGL.all(), ET.all(), EGL.all(), ALU.subtract)
        P.act(EGL.all(), EGL.all(), AF.Exp)
        P.act(ET.all(), ET.all(), AF.Exp)
        BEG = P.view("BEG", RB, 256, 1088, F32, "p (c h) -> p c h", c=NCH)
        P.tt("pool", BEG.all(), BETA.all(), EG.all(), ALU.mult)

        P.barrier()
        WZ = P.view("WZd", RG, 512, 1024, BF16, "p (k n) -> p k n", k=8)
        ZO = [P.view("ZO%d" % b, RI, 64 + b * 128, 256, BF16) for b in range(2)]
        pz = [psv("pzd%d" % b, 4 + b, 0, 256) for b in range(2)]
        n = 0
        for hh in ([] if skip_z else range(16)):
            load_wchunk(WZ, w2d, 4096 + hh * 128)
            load_u(0)
            for st in range(NST):
                if st + 1 < NST:
                    load_u(st + 1)
                u = UIN[st % 2]
                p = pz[n % 2]
                for k in range(8):
                    P.mm(p.all(), WZ[:, k, :], u[:, k, :], start=(k == 0), stop=(k == 7))
                zo = ZO[n % 2]
                n += 1
                P.act(zo.all(), p.all(), AF.Silu)
                P.st(ZST.t[hh, :, st * 256:(st + 1) * 256], ZSr[st], zo.all())

        if dbg:
            P.barrier()
            DRF = P.dram("DRF", [128, 5440], F32, "ExternalOutput")
            P.st(DRF.t[:, :], P.region("drf"), RF.all())
        for jp in pairs:
            dn_pair(i, j, jp, BETA, G, EGL, EG, ET, BEG)

    def dn_pair(i, j, jp, BETA, G, EGL, EG, ET, BEG):
        w2d = dn_w_in.t[j]
        P.barrier()
        QNT = P.view("QNT", RD, 0, TT, BF16)
        KNT = P.view("KNT", RD, 2176, TT, BF16)
        KTM = P.view("KTM", RD, 4352, TT, BF16, "p (c d) -> p c d", c=NCH)
        VTM = P.view("VTMd", RC, 0, 2 * TT, BF16, "p (e c d) -> p e c d", e=2, c=NCH)
        OACC = P.view("OACC", RA, 0, 2 * TT, F32, "p (e c d) -> p e c d", e=2, c=NCH)
        WX = [P.view("WX%d" % x, RG, 512 * x, 1024, BF16, "p (k n) -> p k n", k=8) for x in range(4)]
        cols = [jp * 128, 1024 + jp * 128, 2048 + (2 * jp) * 128, 2048 + (2 * jp + 1) * 128]
        for x in range(4):
            load_wchunk(WX[x], w2d, cols[x])
        UH = [P.view("UH%d" % b, RH, b * 1040, 2080, BF16, "p (k n) -> p k n", k=8) for b in range(2)]
        PJ = [[P.view("PJ%d_%d" % (b, x), RE, (b * 4 + x) * 260, 260, F32) for x in range(4)] for b in range(2)]
        CV = [[P.view("CV%d_%d" % (b, x), RI, (b * 4 + x) * 256, 256, F32) for x in range(4)] for b in range(2)]
        SQc = [[P.view("SQc%d_%d" % (b, x), RB, 3648 + (b * 2 + x) * 128, 256, BF16) for x in range(2)] for b in range(2)]
        RSc = [[P.view("RSc%d_%d" % (b, x), RA, (b * 2 + x) * 256, 256, F32) for x in range(2)] for b in range(2)]
        VF = [[P.view("VF%d_%d" % (b, x), RA, 1024 + (b * 2 + x) * 128, 256, BF16) for x in range(2)] for b in range(2)]
        pp = [psv("pp%d" % x, x, 0, 260) for x in range(4)]
        pssq = [psv("pssd%d" % x, 4 + x, 0, 256) for x in range(2)]
        ptr = [psv("ptr%d" % b, 6 + b, 0, 512, BF16) for b in range(2)]

        def load_uh(st):
            t0 = st * 256
            lr = 1 if st >= 2 else 0
            rr = 1 if 1 <= st <= 15 else 0
            uh = UH[st % 2]
            a, b = t0 - 2 * lr, t0 + 256 + 2 * rr
            P.ld(uh[:, :, 2 - 2 * lr:258 + 2 * rr], UT.t[:, :, a:b].rearrange("k p t -> p k t"),
                 UTr[st])
            if not lr:
                P.memset("pool", uh[:, :, 0:2], 0.0)
            if not rr:
                P.memset("pool", uh[:, :, 258:260], 0.0)
        def phase1(st):
            uh = UH[st % 2]
            b2 = st % 2
            for x in range(4):
                for k in range(8):
                    P.mm(pp[x].all(), WX[x][:, k, :], uh[:, k, :], start=(k == 0), stop=(k == 7))
                P.cp("act", PJ[b2][x].all(), pp[x].all())
            for x in range(4):
                ch = (cols[x] // 128)
                cw = lambda tap: vcol(V_CW + 160 * j + ch * 5 + tap)
                pj, cv = PJ[b2][x], CV[b2][x]
                P.tsc("dve", cv.all(), pj[:, 0:256], cw(0), ALU.mult)
                for tap in range(1, 5):
                    P.stt(cv.all(), pj[:, tap:tap + 256], cw(tap), cv.all(), ALU.mult, ALU.add)

        def phase2(st):
            b2 = st % 2
            t0 = st * 256
            for x in range(2):
                P.act(CV[b2][x].all(), CV[b2][x].all(), AF.Silu)
            for x in range(2):
                P.act(VF[b2][x].all(), CV[b2][2 + x].all(), AF.Silu)
            for x in range(2):
                P.act(SQc[b2][x].all(), CV[b2][x].all(), AF.Square)
                P.mm(pssq[x].all(), ONEB.all(), SQc[b2][x].all())
            for x in range(2):
                rsq(RSc[b2][x].all(), pssq[x].all())
            for x in range(2):
                dst = (QNT if x == 0 else KNT)[:, t0:t0 + 256]
                P.stt(dst, CV[b2][x].all(), float(128.0 ** -0.5) if x == 0 else 1.0, RSc[b2][x].all(),
                      ALU.mult, ALU.mult)
            pt = ptr[0]
            for hf in range(2):
                P.tr(pt[:, hf * 128:(hf + 1) * 128], KNT[:, t0 + hf * 128:t0 + (hf + 1) * 128], IDB.all())
            P.cp("act", KTM[:, 2 * st:2 * st + 2, :], V(pt.bank, pt.ap[:, 0:256].rearrange("p (a b) -> p a b", a=2)))
            pt = ptr[1]
            for x in range(2):
                for hf in range(2):
                    P.tr(pt[:, x * 256 + hf * 128:x * 256 + (hf + 1) * 128], VF[b2][x][:, hf * 128:(hf + 1) * 128],
                         IDB.all())
            for x in range(2):
                P.cp("act", VTM[:, x, 2 * st:2 * st + 2, :],
                     V(pt.bank, pt.ap[:, x * 256:(x + 1) * 256].rearrange("p (a b) -> p a b", a=2)))

        load_uh(0)
        load_uh(1)
        phase1(0)
        for st in range(NST):
            if st + 2 < NST:
                load_uh(st + 2)
            if st + 1 < NST:
                phase1(st + 1)
            phase2(st)

        P.barrier()
        if dbg and jp == 0:
            DRD = P.dram("DRD", [128, 6528], F32, "ExternalOutput")
            DRC = P.dram("DRC", [128, 4352], F32, "ExternalOutput")
            P.st(DRD.t[:, :], P.region("drd"), RD.all())
            P.st(DRC.t[:, :], P.region("drc"), RC.all())
        NW = 1664

        def ctile(ch, k, dt=F32):
            if dt == F32:
                return P.view("c%df%d" % (ch, k), RE, ch * NW + 128 * k, 128, F32)
            return P.view("c%db%d" % (ch, k), RE, ch * NW + 1152 + 64 * k, 128, BF16)
        KK = [P.view("KK%d" % d, RI, 128 * d, 128, F32) for d in range(2)]
        KKO = [P.view("KKO%d" % d, RB, 128 * d, 128, F32) for d in range(2)]
        KKO1 = [P.view("KKO1%d" % d, RB, 3392 + 128 * d, 128, F32) for d in range(2)]
        QK = [P.view("QK%d" % d, RI, 256 + 128 * d, 128, F32) for d in range(2)]
        S = [P.view("S%d" % c, RI, 512 + 128 * c, 128, F32) for c in range(4)]
        Sb = [P.view("Sb%d" % c, RI, 1024 + 64 * c, 128, BF16) for c in range(4)]
        ONt = [P.view("ON%d" % c, RI, 1280 + 64 * c, 128, BF16) for c in range(4)]
        YPt = [P.view("YP%d" % c, RI, 1536 + 64 * c, 128, BF16) for c in range(4)]
        JK = P.view("JK", RI, 1792, 128, BF16)
        SSt = P.view("SSt", RI, 1856, 8, F32)
        for c in range(4):
            P.memset("pool", S[c].all(), 0.0)
            P.memset("pool", Sb[c].all(), 0.0)
        pkq = [psv("pkq%d" % d, 4 + d, 0, 256) for d in range(2)]
        pfin = psv("pfin", 7, 0, 512, BF16)
        gob = VEC[:, V_GO + 128 * j:V_GO + 128 * (j + 1)]
        fin_n = {"n": 0}

        def finish(ch, e, m):
            hh = 2 * jp + e
            o = OACC[:, e, m, :]
            ss = SSt[:, ch:ch + 1]
            P.act(JK.all(), o, AF.Square, accum=ss)
            P.act(ss, ss, AF.Sqrt, scale=1.0 / 128.0, bias=SMALL[:, 1:2])
            P.op("dve", lambda e_: e_.reciprocal(out=ss.ap, in_=ss.ap), reads=[SSt], writes=[SSt])
            P.stt(ONt[ch].all(), o, ss, gob, ALU.mult, ALU.mult)
            q = fin_n["n"] % 4
            fin_n["n"] += 1
            P.tr(pfin[:, q * 128:(q + 1) * 128], ONt[ch].all(), IDB.all())
            P.cp("act", YPt[ch].all(), pfin[:, q * 128:(q + 1) * 128])
            P.st(YPT.t[hh, :, m * 128:(m + 1) * 128], YPr[m // 2], YPt[ch].all())

        def f(v):
            return V(v.t, v.ap.bitcast(F32))

        def unit(ch, e, d, m, first):
            hh = 2 * jp + e
            col = d * 16 + hh
            sc = lambda X: X[:, m, col:col + 1]
            Ud = cs(C_UI) if d == 0 else cs(C_LI)
            Vd = cs(C_LS) if d == 0 else cs(C_US)
            base = ch * 384
            rb = ch * 1280
            NNs = [P.view("c%dN%d" % (ch, k), RR, rb + 128 * k, 128, F32R) for k in range(2)]
            NTYs = [P.view("c%dNTY%d" % (ch, k), RR, rb + 256 + 256 * k, 256, F32R) for k in range(2)]
            NOt = P.view("c%dNO" % ch, RR, rb + 768, 128, F32R)
            TDt = P.view("c%dTD" % ch, RR, rb + 896, 128, F32R)
            Mtt = P.view("c%dMt" % ch, RR, rb + 1024, 128, F32R)
            NO1t = P.view("c%dNO1" % ch, RR, rb + 1152, 128, F32R)
            UG = P.view("c%dUG" % ch, RE, base, 128, F32).all()
            Dm = P.view("c%dDm" % ch, RE, base + 128, 128, F32).all()
            O1 = P.view("c%dO1" % ch, RE, base + 256, 128, F32).all()
            Tt, Pm, PTt, KBG, VB, WTN, VN, VND = [P.view("c%db%d" % (ch, k), RB, 1344 + ch * 512 + 64 * k, 128, BF16)
                                                  for k in range(8)]
            pA = psv("", ch, 0, 128)
            pB = psv("", ch, 128, 128)
            pC = psv("", ch, 256, 128)
            pD = psv("", ch, 384, 128)
            pCD = psv("", ch, 256, 256)
            pCb = psv("", ch, 256, 256, BF16)
            msl = slice(m * 128, (m + 1) * 128)
            if e == 0:
                P.mm(pkq[d][:, 0:128], KNT[:, msl], KNT[:, msl])
                P.mm(pkq[d][:, 128:256], QNT[:, msl], KNT[:, msl])
                P.tt("dve", KK[d].all(), pkq[d][:, 0:128], cs(C_NLSD) if d == 0 else cs(C_NUSD), ALU.mult)
                P.tt("dve", KKO[d].all(), pkq[d][:, 0:128], cs(C_NLSO) if d == 0 else cs(C_NUSO), ALU.mult)
                P.tt("dve", KKO1[d].all(), pkq[d][:, 0:128], cs(C_NLSO1) if d == 0 else cs(C_NUSO1), ALU.mult)
                P.tt("dve", QK[d].all(), pkq[d][:, 128:256], cs(C_LI) if d == 0 else cs(C_UI), ALU.mult)
            P.tsc("dve", UG, Ud, sc(G), ALU.mult)
            P.mm(pA.all(), UG, Vd)
            P.cp("pool", NTYs[0][:, 128:256], cs(C_ID))
            yield
            P.act(Dm, pA.all(), AF.Exp)
            yield
            P.stt(NNs[0].all(), Dm, sc(BETA), KK[d].all(), ALU.mult, ALU.mult)
            P.stt(NO1t.all(), Dm, sc(BETA), KKO1[d].all(), ALU.mult, ALU.mult)
            P.stt(NOt.all(), Dm, sc(BETA), KKO[d].all(), ALU.mult, ALU.mult)
            P.tt("pool", Pm.all(), Dm, QK[d].all(), ALU.mult)
            P.tsc("dve", KBG.all(), KTM[:, m, :], sc(BEG), ALU.mult)
            P.tsc("dve", VB.all(), VTM[:, e, m, :], sc(BETA), ALU.mult)
            yield
            P.tr(pB.all(), f(NNs[0].all()), cs(C_ID))
            P.tr(pCb[:, 0:128], Pm.all(), IDB.all())
            yield
            P.cp("act", NTYs[0][:, 0:128], pB.all())
            P.cp("act", PTt.all(), pCb[:, 0:128])
            yield
            a = 0
            for k in range(5):
                N, NTY = NNs[a], NTYs[a]
                N2, NTY2 = NNs[1 - a], NTYs[1 - a]
                if k < 4:
                    P.mm(pA.all(), NTY[:, 0:128], N.all())
                    P.mm(pCD.all(), N.all(), NTY.all())
                    yield
                    P.cp("act", N2.all(), pA.all())
                    P.cp("act", NTY2[:, 0:128], pCD[:, 0:128])
                    P.tt("dve", NTY2[:, 128:256], pCD[:, 128:256], f(NTY[:, 128:256]), ALU.add)
                    yield
                else:
                    P.mm(pC.all(), N.all(), NTY[:, 128:256])
                    yield
                    P.tt("dve", NTY2[:, 128:256], pC.all(), f(NTY[:, 128:256]), ALU.add)
                    yield
                a = 1 - a
            for (NOx, lastm) in ((NO1t, False), (NOt, True)):
                Yd = NTYs[a][:, 128:256]
                P.tr(pA.all(), f(Yd), cs(C_ID))
                P.mm(pB.all(), NOx.all(), Yd)
                yield
                P.cp("act", TDt.all(), pA.all())
                P.cp("dve", Mtt.all(), pB.all())
                yield
                P.mm(pC.all(), TDt.all(), Mtt.all())
                yield
                if lastm:
                    P.tt("dve", Tt.all(), pC.all(), f(Yd), ALU.add)
                else:
                    P.tt("dve", NTYs[1 - a][:, 128:256], pC.all(), f(Yd), ALU.add)
                    a = 1 - a
                yield
            P.mm(pA.all(), KBG.all(), Tt.all())
            yield
            P.act(WTN.all(), pA.all(), AF.Identity, scale=-1.0)
            yield
            P.mm(pB.all(), Tt.all(), VB.all(), start=True, stop=False)
            P.mm(pB.all(), WTN.all(), Sb[ch].all(), start=False, stop=True)
            P.mm(pC.all(), QNT[:, msl], Sb[ch].all())
            yield
            P.cp("dve", VN.all(), pB.all())
            P.act(VND.all(), pB.all(), AF.Identity, scale=sc(EGL))
            P.act(O1, pC.all(), AF.Identity, scale=sc(EG))
            yield
            P.mm(pD.all(), PTt.all(), VN.all())
            P.mm(pA.all(), KTM[:, m, :], VND.all())
            yield
            if not first:
                P.tt("pool", O1, O1, OACC[:, e, m, :], ALU.add)
            P.tt("dve", OACC[:, e, m, :], pD.all(), O1, ALU.add)
            P.stt(S[ch].all(), S[ch].all(), sc(ET), pA.all(), ALU.mult, ALU.add)
            yield
            P.cp("act", Sb[ch].all(), S[ch].all())
            if not first:
                finish(ch, e, m)

        chains = [[], [], [], []]
        steps = [(0, 1, True), (1, 0, False)] + [(2 + n_, 33 - n_, n_ < 16) for n_ in range(32)]
        for (mf, mb, first) in steps:
            chains[0].append((0, 0, mf, first))
            chains[1].append((1, 0, mf, first))
            chains[2].append((0, 1, mb, first))
            chains[3].append((1, 1, mb, first))
        gens = [None] * 4
        pos = [0] * 4
        live = True
        rnd = 0
        delay = {0: 0, 2: 0, 1: 0, 3: 0}
        while live:
            live = False
            rnd += 1
            for ch in (0, 2, 1, 3):
                if rnd <= delay[ch]:
                    live = True
                    continue
                if gens[ch] is None:
                    if pos[ch] >= len(chains[ch]):
                        continue
                    e, d, m, first = chains[ch][pos[ch]]
                    pos[ch] += 1
                    gens[ch] = unit(ch, e, d, m, first)
                live = True
                try:
                    next(gens[ch])
                except StopIteration:
                    gens[ch] = None

    P.memset("pool", SMALL[:, 0:1], -8.0)
    P.memset("pool", SMALL[:, 1:2], EPS)
    P.memset("pool", SMALL[:, 2:3], 1.0)
    for i in layers:
        stage_mod(i)
        stage_norm(i, hT0 if i == layers[0] else HT, HTr)
        if upto == "norm":
            continue
        if i % 2 == 1:
            stage_attn(i)
            if upto == "attn":
                continue
            stage_out(i, att_w_out.t[i // 2], 8, False, i == DEPTH - 1)
        else:
            stage_dn(i)
            if upto == "dn":
                continue
            stage_out(i, dn_w_out.t[i // 2], 16, True, False)
    P.barrier()
    P.emit()
    return nc, P


def _host_inputs(inputs):
    cstt, ropec, ropes = _const_tables()
    f = lambda a: np.ascontiguousarray(np.asarray(a, dtype=np.float32))
    x, c, ctx, c_ctx = f(inputs["x"]), f(inputs["c"]), f(inputs["ctx"]), f(inputs["c_ctx"])
    col = lambda v: v.reshape(-1, 128).T
    maps = []
    for b in range(8):
        vec = np.zeros((128, NV), np.float32)
        cc = np.stack([col(c[b]), col(c_ctx)], -1)
        vec[:, V_C:V_C + 16] = cc.reshape(128, 16)
        ng = f(inputs["norm_g"])
        for l in range(4):
            vec[:, V_NG + 8 * l:V_NG + 8 * (l + 1)] = col(ng[l])
            ab = col(f(inputs["ada_b"])[l])
            vec[:, V_AB + 48 * l:V_AB + 48 * (l + 1)] = np.repeat(ab, 2, axis=1)
        vec[:, V_FG:V_FG + 8] = col(f(inputs["final_norm_g"]))
        cw = f(inputs["dn_conv_w"])
        for l in range(2):
            t = cw[l].T.reshape(32, 128, 5).transpose(1, 0, 2)
            vec[:, V_CW + 160 * l:V_CW + 160 * (l + 1)] = t.reshape(128, 160)
            vec[:, V_QG + l] = f(inputs["att_q_norm_g"])[l]
            vec[:, V_KG + l] = f(inputs["att_k_norm_g"])[l]
            vec[:, V_GO + 128 * l:V_GO + 128 * (l + 1)] = f(inputs["dn_o_norm_g"])[l][None, :]
            vec[:, V_AL + 32 * l:V_AL + 32 * (l + 1)] = f(inputs["dn_a_log"])[l].reshape(1, 32)
            vec[:, V_DT + 32 * l:V_DT + 32 * (l + 1)] = f(inputs["dn_dt_bias"])[l].reshape(1, 32)
        h0 = np.concatenate([ctx[b], x[b]], 0)
        hT = np.ascontiguousarray(h0.T).reshape(8, 128, TT)
        maps.append({
            "hT0": hT, "vecs": vec, "cst": cstt, "ropeC": ropec, "ropeS": ropes,
            "ada_w": f(inputs["ada_w"]), "dn_w_in": f(inputs["dn_w_in"]), "dn_w_out": f(inputs["dn_w_out"]),
            "att_w_in": f(inputs["att_w_in"]), "att_w_out": f(inputs["att_w_out"]),
        })
    return maps


def kernel(**inputs):
    nc, _ = build()
    maps = _host_inputs(inputs)
    res = run_bass_kernel_spmd(nc, maps, core_ids=list(range(8)))
    out = np.stack([np.asarray(r["outT"]).reshape(1024, 4096).T for r in res.results], 0)
    return np.ascontiguousarray(out.astype(np.float32))
```

```python
import numpy as np
import ml_dtypes
import concourse.bass as bass
import concourse.mybir as mybir
from concourse.bass_utils import run_bass_kernel_spmd
from contextlib import ExitStack

F32 = mybir.dt.float32
BF16 = mybir.dt.bfloat16
F32R = mybir.dt.float32r
AF = mybir.ActivationFunctionType
ALU = mybir.AluOpType
CENG = ("pe", "act", "dve", "pool")
ENGS = ("pe", "act", "dve", "pool", "sp")


class V:
    __slots__ = ("t", "ap")

    def __init__(self, t, ap):
        self.t = t
        self.ap = ap


class T:
    __slots__ = ("name", "t", "w", "r", "key", "space")

    def __init__(self, name, t, space, key=None):
        self.name = name
        self.t = t
        self.space = space
        self.w = []
        self.r = []
        self.key = key or name

    def __getitem__(self, k):
        return V(self, self.t[k])

    def all(self):
        return V(self, self.t[:])


class Prog:
    def __init__(self, nc):
        self.nc = nc
        self.es = ExitStack()
        self.ops = {e: [] for e in ENGS}
        self.esem = {e: self.es.enter_context(nc.semaphore("s_" + e)) for e in CENG}
        self.dsems = {}
        self.dcnt = {}
        self.extra = {e: [] for e in ENGS}
        self.ro = T("ro", None, "dram")
        self.vcache = {}

    def sb(self, name, shape, dt):
        return T(name, self.es.enter_context(self.nc.sbuf_tensor(name, list(shape), dt)), "sb")

    def ps(self, name, shape, dt):
        return T(name, self.es.enter_context(self.nc.psum_tensor(name, list(shape), dt)), "ps")

    def dram(self, name, shape, dt, kind="Internal"):
        return T(name, self.nc.dram_tensor(name, list(shape), dt, kind=kind).ap(), "dram")

    def region(self, name):
        return T(name, None, "dram")

    def view(self, name, raw, lo, n, dt, pat=None, **kw):
        ck = (raw.name, lo, n, str(dt), pat, tuple(sorted(kw.items())))
        if ck in self.vcache:
            return self.vcache[ck]
        words = n if dt in (F32, F32R) else n // 2
        ap = raw.t[:, lo:lo + words]
        if dt != ap.dtype:
            ap = ap.bitcast(dt)
        if pat:
            ap = ap.rearrange(pat, **kw)
        t = T(name, ap, raw.space, key="%s@%d" % (raw.name, lo))
        self.vcache[ck] = t
        return t

    def _deps(self, eng, reads, writes, shared=()):
        deps = list(self.extra[eng])
        self.extra[eng] = []
        for t in reads:
            deps.extend(t.w)
        for t in shared:
            deps.extend(t.r)
        for t in writes:
            deps.extend(t.w)
            deps.extend(t.r)
        out = []
        for d in deps:
            if d[0] == "E":
                if d[1] == "pe" and eng == "pe":
                    continue
                self.ops[d[1]][d[2]]["flag"] = True
            out.append(d)
        return out

    def _mark(self, me, reads, writes, shared=()):
        for t in reads:
            t.r.append(me)
            if len(t.r) > 64:
                t.r = t.r[-48:]
        for t in writes:
            t.w = [me]
            t.r = []
        for t in shared:
            t.w.append(me)

    def op(self, eng, fn, reads=(), writes=()):
        writes = list(dict.fromkeys(list(writes) + [x for x in reads if x.space == "ps"]))
        reads = [x for x in dict.fromkeys(reads) if x is not self.ro and x.space != "ps"]
        deps = self._deps(eng, reads, writes)
        idx = len(self.ops[eng])
        self.ops[eng].append(dict(fn=fn, waits=deps, flag=False, dma=None))
        self._mark(("E", eng, idx), reads, writes)

    def dma(self, fn, reads=(), writes=(), shared=(), q="sp"):
        reads = [x for x in reads if x is not self.ro]
        deps = self._deps(q, reads, writes, shared)
        st = [t for t in list(writes) + list(shared) + list(reads) if t.space != "dram"][0]
        if st.key not in self.dsems:
            self.dsems[st.key] = self.es.enter_context(self.nc.semaphore("d%d" % len(self.dsems)))
            self.dcnt[st.key] = 0
        sem = self.dsems[st.key]
        if self.dcnt[st.key]:
            deps.append(("D", sem, self.dcnt[st.key]))
        self.dcnt[st.key] += 16
        me = ("D", sem, self.dcnt[st.key])
        self.ops[q].append(dict(fn=fn, waits=deps, flag=False, dma=sem))
        self._mark(me, reads, writes, shared)
        return me

    def barrier(self):
        deps = []
        for e in CENG:
            if self.ops[e]:
                i = len(self.ops[e]) - 1
                self.ops[e][i]["flag"] = True
                deps.append(("E", e, i))
        for k, sem in self.dsems.items():
            deps.append(("D", sem, self.dcnt[k]))
        for e in ENGS:
            self.extra[e] = list(deps) + self.extra[e]

    @staticmethod
    def _ts(*vs):
        return [v.t for v in vs if isinstance(v, V)]

    @staticmethod
    def _a(v):
        return v.ap if isinstance(v, V) else v

    def mm(self, out, lhsT, rhs, start=True, stop=True):
        self.op("pe", lambda e: e.matmul(out.ap, lhsT=lhsT.ap, rhs=rhs.ap, start=start, stop=stop),
                reads=self._ts(lhsT, rhs), writes=self._ts(out))

    def tr(self, out, in_, ident):
        self.op("pe", lambda e: e.transpose(out.ap, in_.ap, ident.ap),
                reads=self._ts(in_, ident), writes=self._ts(out))

    def act(self, out, in_, func, scale=None, bias=None, accum=None):
        kw = {}
        if scale is not None:
            kw["scale"] = self._a(scale)
        if bias is not None:
            kw["bias"] = self._a(bias)
        if accum is not None:
            kw["accum_out"] = accum.ap
        self.op("act", lambda e: e.activation(out=out.ap, in_=in_.ap, func=func, **kw),
                reads=self._ts(in_, scale, bias), writes=self._ts(out, accum))

    def tt(self, eng, out, in0, in1, op):
        self.op(eng, lambda e: e.tensor_tensor(out=out.ap, in0=in0.ap, in1=in1.ap, op=op),
                reads=self._ts(in0, in1), writes=self._ts(out))

    def tsc(self, eng, out, in0, s1, op0, s2=None, op1=None):
        if op1 is None:
            fn = lambda e: e.tensor_scalar(out=out.ap, in0=in0.ap, scalar1=self._a(s1), scalar2=None, op0=op0)
        else:
            fn = lambda e: e.tensor_scalar(out=out.ap, in0=in0.ap, scalar1=self._a(s1),
                                           scalar2=self._a(s2), op0=op0, op1=op1)
        self.op(eng, fn, reads=self._ts(in0, s1, s2), writes=self._ts(out))

    def stt(self, out, in0, scalar, in1, op0, op1):
        self.op("dve", lambda e: e.scalar_tensor_tensor(out=out.ap, in0=in0.ap, scalar=self._a(scalar),
                                                         in1=in1.ap, op0=op0, op1=op1),
                reads=self._ts(in0, scalar, in1), writes=self._ts(out))

    def cp(self, eng, out, in_):
        if eng == "act":
            self.op("act", lambda e: e.copy(out=out.ap, in_=in_.ap), reads=self._ts(in_), writes=self._ts(out))
        else:
            self.op(eng, lambda e: e.tensor_copy(out=out.ap, in_=in_.ap), reads=self._ts(in_),
                    writes=self._ts(out))

    def memset(self, eng, out, val):
        self.op(eng, lambda e: e.memset(out.ap, val), writes=self._ts(out))

    def ld(self, out, src_ap, src_t=None, q="sp"):
        return self.dma(lambda e: e.dma_start(out=out.ap, in_=src_ap), reads=[src_t or self.ro],
                        writes=[out.t], q=q)

    def st(self, dst_ap, dst_t, in_, shared=True, q="sp"):
        if shared:
            return self.dma(lambda e: e.dma_start(out=dst_ap, in_=in_.ap), reads=[in_.t], shared=[dst_t], q=q)
        return self.dma(lambda e: e.dma_start(out=dst_ap, in_=in_.ap), reads=[in_.t], writes=[dst_t], q=q)

    def emit(self):
        nc = self.nc
        cum = {}
        for e in CENG:
            c = 0
            arr = []
            for o in self.ops[e]:
                if o["flag"]:
                    c += 1
                arr.append(c)
            cum[e] = arr
        self.stats = {}

        def run(e, eng):
            seen = {}
            nw = 0
            for o in self.ops[e]:
                for d in o["waits"]:
                    if d[0] == "E":
                        sem, val, key = self.esem[d[1]], cum[d[1]][d[2]], d[1]
                    else:
                        sem, val, key = d[1], d[2], id(d[1])
                    if seen.get(key, 0) >= val:
                        continue
                    seen[key] = val
                    eng.wait_ge(sem, val)
                    nw += 1
                ins = o["fn"](eng)
                if o["dma"] is not None:
                    ins.then_inc(o["dma"], 16)
                elif o["flag"]:
                    ins.then_inc(self.esem[e], 1)
            if e == "sp":
                for k, sem in self.dsems.items():
                    if seen.get(id(sem), 0) < self.dcnt[k]:
                        eng.wait_ge(sem, self.dcnt[k])
            self.stats[e] = (len(self.ops[e]), nw)

        with nc.Block() as block:
            @block.sync
            def _(eng):
                run("sp", eng)

            @block.tensor
            def _(eng):
                run("pe", eng)

            @block.scalar
            def _(eng):
                run("act", eng)

            @block.vector
            def _(eng):
                run("dve", eng)

            @block.gpsimd
            def _(eng):
                run("pool", eng)
        self.es.close()


D = 1024
TT = 4352
NCH = 34
NST = 17
EPS = 1e-6
DEPTH = 4

(C_ID, C_LS, C_LI, C_US, C_UI, C_NLSD, C_NUSD, C_ONE, C_O1024, C_O128, C_RT,
 C_NLSO, C_NUSO, C_NLSO1, C_NUSO1) = [i * 128 for i in range(15)]
NCST = 15 * 128

V_C = 0
V_NG = V_C + 16
V_AB = V_NG + 32
V_FG = V_AB + 192
V_CW = V_FG + 8
V_QG = V_CW + 320
V_KG = V_QG + 2
V_GO = V_KG + 2
V_AL = V_GO + 256
V_DT = V_AL + 64
NV = V_DT + 64


def _const_tables():
    p = np.arange(128)[:, None]
    f = np.arange(128)[None, :]
    t = np.zeros((128, NCST), np.float32)
    t[:, C_ID:C_ID + 128] = (p == f)
    t[:, C_LS:C_LS + 128] = (p > f)
    t[:, C_LI:C_LI + 128] = (p >= f)
    t[:, C_US:C_US + 128] = (p < f)
    t[:, C_UI:C_UI + 128] = (p <= f)
    bd = (p // 64) == (f // 64)
    bd32 = (p // 32) == (f // 32)
    t[:, C_NLSD:C_NLSD + 128] = -((p > f) & bd32).astype(np.float32)
    t[:, C_NUSD:C_NUSD + 128] = -((p < f) & bd32).astype(np.float32)
    t[:, C_NLSO1:C_NLSO1 + 128] = -((p > f) & bd & ~bd32).astype(np.float32)
    t[:, C_NUSO1:C_NUSO1 + 128] = -((p < f) & bd & ~bd32).astype(np.float32)
    t[:, C_NLSO:C_NLSO + 128] = -((p > f) & ~bd).astype(np.float32)
    t[:, C_NUSO:C_NUSO + 128] = -((p < f) & ~bd).astype(np.float32)
    t[:, C_ONE:C_ONE + 128] = 1.0
    t[:, C_O1024:C_O1024 + 128] = 1.0 / 1024.0
    t[:, C_O128:C_O128 + 128] = 1.0 / 128.0
    R = np.zeros((128, 128), np.float32)
    for m in range(128):
        if (m % 64) < 32:
            R[m, m + 32] = -1.0
        else:
            R[m, m - 32] = 1.0
    t[:, C_RT:C_RT + 128] = R.T
    tok = np.arange(4096)
    row = (tok // 64).astype(np.float32)
    col = (tok % 64).astype(np.float32)
    inv = (10000.0 ** (-np.arange(0, 64, 2, dtype=np.float32) / 64.0)).astype(np.float32)
    ang_r = row[None, :] * inv[:, None]
    ang_c = col[None, :] * inv[:, None]
    ang = np.concatenate([ang_r, ang_r, ang_c, ang_c], 0).astype(np.float32)
    return t, np.cos(ang).astype(np.float32), np.sin(ang).astype(np.float32)


def build(layers=(0, 1, 2, 3), dbg=False, upto="out", pairs=range(8), skip_z=False):
    nc = bass.Bass("TRN2", target_bir_lowering=False)
    P = Prog(nc)
    RO = P.ro
    EI = "ExternalInput"
    hT0 = P.dram("hT0", [8, 128, TT], F32, EI)
    vecs = P.dram("vecs", [128, NV], F32, EI)
    cst = P.dram("cst", [128, NCST], F32, EI)
    ropeC = P.dram("ropeC", [128, 4096], F32, EI)
    ropeS = P.dram("ropeS", [128, 4096], F32, EI)
    ada_w = P.dram("ada_w", [4, 1024, 3072], F32, EI)
    dn_w_in = P.dram("dn_w_in", [2, 1024, 6208], F32, EI)
    dn_w_out = P.dram("dn_w_out", [2, 2048, 1024], F32, EI)
    att_w_in = P.dram("att_w_in", [2, 1024, 2560], F32, EI)
    att_w_out = P.dram("att_w_out", [2, 1024, 1024], F32, EI)
    outT = P.dram("outT", [8, 128, 4096], F32, "ExternalOutput")
    IK = "ExternalOutput" if dbg else "Internal"
    HT = P.dram("HT", [8, 128, TT], F32, IK)
    UT = P.dram("UT", [8, 128, TT], BF16, IK)
    YPT = P.dram("YPT", [16, 128, TT], BF16, IK)
    ZST = P.dram("ZST", [16, 128, TT], BF16, IK)
    HTr = [P.region("HTr%d" % i) for i in range(NST)]
    UTr = [P.region("UTr%d" % i) for i in range(NST)]
    YPr = [P.region("YPr%d" % i) for i in range(NST)]
    ZSr = [P.region("ZSr%d" % i) for i in range(NST)]
    OUTr = P.region("OUTr")
    dbg_out = {}

    CST = P.sb("CST", [128, NCST], F32)
    VEC = P.sb("VEC", [128, NV], F32)
    IDB = P.sb("IDB", [128, 128], BF16)
    ONEB = P.sb("ONEB", [128, 128], BF16)
    SC = P.sb("SC", [128, 16], F32)
    MOD = P.sb("MOD", [128, 48], F32)
    GSC = P.sb("GSC", [128, 16], F32)
    SMALL = P.sb("SMALL", [128, 64], F32)
    RA = P.sb("RA", [128, 8704], F32)
    RB = P.sb("RB", [128, 4480], F32)
    RC = P.sb("RC", [128, 4352], F32)
    RD = P.sb("RD", [128, 6528], F32)
    RE = P.sb("RE", [128, 4096], F32)
    RR = P.sb("RR", [128, 5120], F32R)
    RF = P.sb("RF", [128, 5440], F32)
    RG = P.sb("RG", [128, 5120], F32)
    RH = P.sb("RH", [128, 3072], F32)
    RI = P.sb("RI", [128, 2048], F32)
    PSB = [P.ps("PSB%d" % i, [128, 512], F32) for i in range(8)]

    def cs(col, n=128):
        return CST[:, col:col + n]

    def vcol(col):
        return VEC[:, col:col + 1]

    def rsq(dst, src):
        P.act(dst, src, AF.Sqrt, bias=SMALL[:, 1:2])
        P.op("dve", lambda e: e.reciprocal(out=dst.ap, in_=dst.ap), reads=[dst.t], writes=[dst.t])

    class PV:
        def __init__(self, bank, ap):
            self.bank, self.ap = bank, ap

        def __getitem__(self, k):
            return V(self.bank, self.ap[k])

        def all(self):
            return V(self.bank, self.ap)

    def psv(name, bank, lo, n, dt=F32):
        words = n if dt == F32 else n // 2
        ap = PSB[bank].t[:, lo:lo + words]
        if dt != F32:
            ap = ap.bitcast(dt)
        return PV(PSB[bank], ap)

    P.ld(CST.all(), cst.t[:, :])
    P.ld(VEC.all(), vecs.t[:, :])
    P.cp("dve", IDB.all(), cs(C_ID))
    P.cp("dve", ONEB.all(), cs(C_ONE))
    P.act(SC.all(), VEC[:, V_C:V_C + 16], AF.Silu)

    def stage_mod(i):
        P.barrier()
        WA = [P.view("WA%d" % b, RG, b * 1024, 1024, F32, "p (k n) -> p k n", k=8) for b in range(2)]
        pm = psv("pm", 0, 0, 48)
        aw = ada_w.t[i].rearrange("(k p) n -> p k n", p=128)
        for m in range(24):
            w = WA[m % 2]
            P.ld(w.all(), aw[:, :, m * 128:(m + 1) * 128])
            for k in range(8):
                P.mm(pm[:, 2 * m:2 * m + 2], w[:, k, :], SC[:, 2 * k:2 * k + 2], start=(k == 0), stop=(k == 7))
        P.tt("dve", MOD.all(), pm.all(), VEC[:, V_AB + 48 * i:V_AB + 48 * (i + 1)], ALU.add)
        for k in range(8):
            P.tsc("dve", GSC[:, 2 * k:2 * k + 2], MOD[:, 2 * (8 + k):2 * (8 + k) + 2], 1.0, ALU.add,
                  vcol(V_NG + 8 * i + k), ALU.mult)

    def stage_norm(i, src, srcr):
        P.barrier()
        HIN = [P.view("HIN%d" % b, RC, b * 2048, 2048, F32, "p (k n) -> p k n", k=8) for b in range(2)]
        SQ = P.view("SQ", RB, 0, 2048, F32, "p (k n) -> p k n", k=8)
        TMP = P.view("TMPn", RB, 2048, 2048, F32, "p (k n) -> p k n", k=8)
        UO = [P.view("UO%d" % b, RH, b * 1024, 2048, BF16, "p (k n) -> p k n", k=8) for b in range(2)]
        RS = P.view("RS", RI, 0, 256, F32)
        pn = psv("pn", 0, 0, 256)

        def load(st):
            P.ld(HIN[st % 2].all(), src.t[:, :, st * 256:(st + 1) * 256].rearrange("k p t -> p k t"), srcr[st])
        load(0)
        for st in range(NST):
            if st + 1 < NST:
                load(st + 1)
            h = HIN[st % 2]
            r = 1 if st == 0 else 0
            P.act(SQ.all(), h.all(), AF.Square)
            for k in range(8):
                P.mm(pn.all(), cs(C_O1024), SQ[:, k, :], start=(k == 0), stop=(k == 7))
            rsq(RS.all(), pn.all())
            uo = UO[st % 2]
            for k in range(8):
                P.tt("pool" if k % 2 else "dve", TMP[:, k, :], h[:, k, :], RS.all(), ALU.mult)
                P.act(uo[:, k, :], TMP[:, k, :], AF.Identity, scale=GSC[:, 2 * k + r:2 * k + r + 1],
                      bias=MOD[:, 2 * k + r:2 * k + r + 1])
            P.st(UT.t[:, :, st * 256:(st + 1) * 256].rearrange("k p t -> p k t"), UTr[st], uo.all(), shared=False)

    wstate = {"n": 0}

    def load_wchunk(dst, wsrc2d, c0, ncols=128, kc=8):
        b = wstate["n"] % 2
        wstate["n"] += 1
        stg = P.view("WSTG%d" % b, RG, 3072 + b * 1024, kc * ncols, F32, "p (k n) -> p k n", k=kc)
        P.ld(stg.all(), wsrc2d.rearrange("(k p) n -> p k n", p=128)[:, :, c0:c0 + ncols])
        P.cp("pool", dst.all(), stg.all())

    def stage_out(i, wout2d, kc, use_z, last):
        P.barrier()
        WO = P.view("WO", RA, 0, kc * 1024, BF16, "p (k n) -> p k n", k=kc)
        for c in range(8):
            for k0 in range(0, kc, 8):
                b = wstate["n"] % 2
                wstate["n"] += 1
                stg = P.view("WSTG%d" % b, RG, 3072 + b * 1024, 1024, F32, "p (k n) -> p k n", k=8)
                P.ld(stg.all(), wout2d.rearrange("(k p) n -> p k n", p=128)[:, k0:k0 + 8, c * 128:(c + 1) * 128])
                P.cp("pool", WO[:, k0:k0 + 8, c * 128:(c + 1) * 128], stg.all())
        YT = [P.view("YT%d" % b, RC, b * 2048, kc * 256, BF16, "p (k n) -> p k n", k=kc) for b in range(2)]
        ZT = [P.view("ZT%d" % b, RD, b * 2048, kc * 256, BF16, "p (k n) -> p k n", k=kc) for b in range(2)]
        HI = [P.view("HI%d" % b, RB, b * 2048, 2048, F32, "p (k n) -> p k n", k=8) for b in range(2)]
        HO = [P.view("HO%d" % b, RE, b * 2048, 2048, F32, "p (k n) -> p k n", k=8) for b in range(2)]
        SQ = P.view("SQo", RD, 4096, 2048, F32, "p (k n) -> p k n", k=8)
        RS = P.view("RSo", RI, 0, 256, F32)
        pn = psv("pno", 4, 0, 256)
        pys = [psv("py%d" % b, b, 0, 256) for b in range(4)]
        hsrc = hT0 if i == layers[0] else HT

        def load(st):
            b = st % 2
            P.ld(YT[b].all(), YPT.t[0:kc, :, st * 256:(st + 1) * 256].rearrange("k p t -> p k t"), YPr[st])
            if use_z:
                P.ld(ZT[b].all(), ZST.t[0:kc, :, st * 256:(st + 1) * 256].rearrange("k p t -> p k t"), ZSr[st])
            P.ld(HI[b].all(), hsrc.t[:, :, st * 256:(st + 1) * 256].rearrange("k p t -> p k t"), HTr[st])
        first = 1 if last else 0
        load(first)
        for st in range(first, NST):
            if st + 1 < NST:
                load(st + 1)
            b = st % 2
            r = 1 if st == 0 else 0
            y = YT[b]
            if use_z:
                P.tt("pool", y.all(), y.all(), ZT[b].all(), ALU.mult)
            ho = HO[b]
            for c in range(8):
                py = pys[c % 4]
                for k in range(kc):
                    P.mm(py.all(), WO[:, k, c * 128:(c + 1) * 128], y[:, k, :], start=(k == 0), stop=(k == kc - 1))
                P.stt(ho[:, c, :], py.all(), MOD[:, 2 * (16 + c) + r:2 * (16 + c) + r + 1], HI[b][:, c, :],
                      ALU.mult, ALU.add)
            if not last:
                P.st(HT.t[:, :, st * 256:(st + 1) * 256].rearrange("k p t -> p k t"), HTr[st], ho.all(), shared=False)
            else:
                P.act(SQ.all(), ho.all(), AF.Square)
                for k in range(8):
                    P.mm(pn.all(), cs(C_O1024), SQ[:, k, :], start=(k == 0), stop=(k == 7))
                rsq(RS.all(), pn.all())
                for k in range(8):
                    P.stt(SQ[:, k, :], ho[:, k, :], vcol(V_FG + k), RS.all(), ALU.mult, ALU.mult)
                P.st(outT.t[:, :, (st - 1) * 256:st * 256].rearrange("k p t -> p k t"), OUTr, SQ.all())

    def stage_attn(i):
        j = i // 2
        need_ctx = i < DEPTH - 1
        w2d = att_w_in.t[j]
        P.barrier()
        KT = P.view("KT", RA, 0, 2 * TT, BF16, "p (g t) -> p g t", g=2)
        VTM = P.view("VTM", RA, TT, 2 * TT, BF16, "p (g c d) -> p g c d", g=2, c=NCH)
        QT = P.view("QT", RB, 0, TT, BF16)
        ZS = P.view("ZS", RB, TT // 2, TT, BF16)
        UIN = [P.view("UIN%d" % b, RH, b * 1024, 2048, BF16, "p (k n) -> p k n", k=8) for b in range(2)]
        WQ = P.view("WQ", RG, 0, 1024, BF16, "p (k n) -> p k n", k=8)
        WZ = P.view("WZ", RG, 512, 1024, BF16, "p (k n) -> p k n", k=8)
        WV = P.view("WV", RG, 1024, 1024, BF16, "p (k n) -> p k n", k=8)
        XN = [P.view("XN%d" % b, RD, b * 256, 256, F32) for b in range(2)]
        SQ = [P.view("SQa%d" % b, RD, 512 + b * 256, 256, F32) for b in range(2)]
        RS = [P.view("RSa%d" % b, RD, 1024 + b * 256, 256, F32) for b in range(2)]
        T1 = [P.view("T1a%d" % b, RD, 1536 + b * 256, 256, F32) for b in range(2)]
        T2 = [P.view("T2a%d" % b, RD, 2048 + b * 256, 256, F32) for b in range(2)]
        CT = [P.view("CT%d" % b, RD, 2560 + b * 256, 256, F32) for b in range(2)]
        STb = [P.view("STb%d" % b, RD, 3072 + b * 256, 256, F32) for b in range(2)]
        PT = [P.view("PTa%d" % b, RE, b * 256, 512, BF16) for b in range(3)]
        RIV = P.view("RIV", RE, 1024, 512, F32)
        OT = P.view("OTa", RE, 1536, 512, F32)
        YO = [P.view("YO%d" % b, RE, 2048 + b * 256, 512, BF16) for b in range(2)]
        pqs = [psv("pq%d" % b, 4 + b, 0, 256) for b in range(2)]
        pzs = [psv("pz%d" % b, 2 + b, 0, 256) for b in range(2)]
        psss = [psv("pss%d" % b, 6 + b, 0, 256) for b in range(2)]
        prots = [psv("prot%d" % b, b, 0, 256) for b in range(2)]
        pv = [psv("pv%d" % b, 2 + b, 256, 128) for b in range(2)]

        def load_u(st):
            P.ld(UIN[st % 2].all(), UT.t[:, :, st * 256:(st + 1) * 256].rearrange("k p t -> p k t"), UTr[st])

        rope_n = {"n": 0}

        def proj_norm_rope(st, w, gcol, dst, extra_scale):
            u = UIN[st % 2]
            b2 = rope_n["n"] % 2
            rope_n["n"] += 1
            pq, pss, prot = pqs[b2], psss[b2], prots[b2]
            for k in range(8):
                P.mm(pq.all(), w[:, k, :], u[:, k, :], start=(k == 0), stop=(k == 7))
            P.act(SQ[b2].all(), pq.all(), AF.Square)
            P.mm(pss.all(), cs(C_O128), SQ[b2].all())
            rsq(RS[b2].all(), pss.all())
            if st == 0:
                P.stt(dst, pq.all(), gcol, RS[b2].all(), ALU.mult, ALU.mult)
                return
            P.stt(XN[b2].all(), pq.all(), gcol, RS[b2].all(), ALU.mult, ALU.mult)
            P.ld(CT[b2].all(), ropeC.t[:, (st - 1) * 256:st * 256])
            P.ld(STb[b2].all(), ropeS.t[:, (st - 1) * 256:st * 256])
            P.mm(prot.all(), cs(C_RT), XN[b2].all())
            P.tt("pool", T1[b2].all(), XN[b2].all(), CT[b2].all(), ALU.mult)
            P.tt("dve", T2[b2].all(), prot.all(), STb[b2].all(), ALU.mult)
            P.tt("pool", dst, T1[b2].all(), T2[b2].all(), ALU.add)

        for g in range(2):
            load_wchunk(WQ, w2d, 1024 + g * 128)
            load_wchunk(WV, w2d, 1280 + g * 128)
            load_u(0)
            for st in range(NST):
                if st + 1 < NST:
                    load_u(st + 1)
                proj_norm_rope(st, WQ, vcol(V_KG + j), KT[:, g, st * 256:(st + 1) * 256], None)
                u = UIN[st % 2]
                for half in range(2):
                    c = st * 2 + half
                    p = pv[half]
                    for k in range(8):
                        P.mm(p.all(), u[:, k, half * 128:(half + 1) * 128], WV[:, k, :], start=(k == 0), stop=(k == 7))
                    P.cp("act", VTM[:, g, c, :], p.all())
        pS = [psv("pS%d" % b, b, 0, 512) for b in range(2)]
        pOs = [psv("pO%d" % b, 2 + 2 * b, 0, 512) for b in range(2)]
        pSums = [psv("pSum%d" % b, 3 + 2 * b, 0, 512) for b in range(2)]
        zn = 0
        for h in range(8):
            g = h // 4
            load_wchunk(WQ, w2d, h * 128)
            load_wchunk(WZ, w2d, 1536 + h * 128)
            load_u(0)
            for st in range(NST):
                if st + 1 < NST:
                    load_u(st + 1)
                proj_norm_rope(st, WQ, vcol(V_QG + j), QT[:, st * 256:(st + 1) * 256], None)
                u = UIN[st % 2]
                pz = pzs[zn % 2]
                zn += 1
                for k in range(8):
                    P.mm(pz.all(), WZ[:, k, :], u[:, k, :], start=(k == 0), stop=(k == 7))
                P.act(ZS[:, st * 256:(st + 1) * 256], pz.all(), AF.Silu)
            blocks = ([(0, 256, 2)] if need_ctx else []) + [(256 + 512 * qb, 512, NCH) for qb in range(8)]
            for bi, (q0, nq, nk) in enumerate(blocks):
                pO, pSum = pOs[bi % 2], pSums[bi % 2]

                def score(kt):
                    P.mm(pS[kt % 2][:, 0:nq], KT[:, g, kt * 128:(kt + 1) * 128], QT[:, q0:q0 + nq])
                score(0)
                for kt in range(nk):
                    if kt + 1 < nk:
                        score(kt + 1)
                    pt = PT[kt % 3]
                    P.act(pt[:, 0:nq], pS[kt % 2][:, 0:nq], AF.Exp, scale=float(128.0 ** -0.5), bias=SMALL[:, 0:1])
                    P.mm(pO[:, 0:nq], VTM[:, g, kt, :], pt[:, 0:nq], start=(kt == 0), stop=(kt == nk - 1))
                    P.mm(pSum[:, 0:nq], ONEB.all(), pt[:, 0:nq], start=(kt == 0), stop=(kt == nk - 1))
                P.op("dve", lambda e, o=RIV[:, 0:nq], s=pSum[:, 0:nq]: e.reciprocal(out=o.ap, in_=s.ap),
                     reads=[pSum.bank], writes=[RIV])
                P.tt("dve", OT[:, 0:nq], pO[:, 0:nq], RIV[:, 0:nq], ALU.mult)
                yo = YO[bi % 2]
                P.tt("pool", yo[:, 0:nq], OT[:, 0:nq], ZS[:, q0:q0 + nq], ALU.mult)
                regs = [YPr[(q0 + s * 256) // 256] for s in range(nq // 256)]
                P.dma(lambda e, d=YPT.t[h, :, q0:q0 + nq], y_=yo[:, 0:nq]: e.dma_start(out=d, in_=y_.ap),
                      reads=[yo], shared=regs)


    def stage_dn(i):
        j = i // 2
        w2d = dn_w_in.t[j]
        P.barrier()
        UIN = [P.view("UIN%d" % b, RH, b * 1024, 2048, BF16, "p (k n) -> p k n", k=8) for b in range(2)]
        BETA = P.view("BETA", RF, 0, 1088, F32, "p (c h) -> p c h", c=NCH)
        G = P.view("G", RF, 1088, 1088, F32, "p (c h) -> p c h", c=NCH)
        EGL = P.view("EGL", RF, 2176, 1088, F32, "p (c h) -> p c h", c=NCH)
        EG = P.view("EG", RF, 3264, 1088, F32, "p (c h) -> p c h", c=NCH)
        ET = P.view("ET", RF, 4352, 1088, F32, "p (c h) -> p c h", c=NCH)
        NEGA = P.view("NEGA", RI, 0, 32, F32)

        def load_u(st):
            P.ld(UIN[st % 2].all(), UT.t[:, :, st * 256:(st + 1) * 256].rearrange("k p t -> p k t"), UTr[st])

        WAB = P.view("WAB", RG, 0, 512, BF16, "p (k n) -> p k n", k=8)
        load_wchunk(WAB, w2d, 6144, ncols=64)
        pab = psv("pab", 6, 0, 64)
        pgc = psv("pgc", 7, 0, 64)
        P.act(NEGA.all(), VEC[:, V_AL + 32 * j:V_AL + 32 * (j + 1)], AF.Exp)
        P.tsc("dve", NEGA.all(), NEGA.all(), -1.0, ALU.mult)
        load_u(0)
        for st in range(NST):
            if st + 1 < NST:
                load_u(st + 1)
            u = UIN[st % 2]
            for half in range(2):
                c = st * 2 + half
                for k in range(8):
                    P.mm(pab.all(), u[:, k, half * 128:(half + 1) * 128], WAB[:, k, :], start=(k == 0), stop=(k == 7))
                P.cp("dve", BETA[:, c, :], pab[:, 0:32])
                P.tt("dve", G[:, c, :], pab[:, 32:64], VEC[:, V_DT + 32 * j:V_DT + 32 * (j + 1)], ALU.add)
        P.act(BETA.all(), BETA.all(), AF.Sigmoid)
        P.act(G.all(), G.all(), AF.Exp)
        P.act(G.all(), G.all(), AF.Ln, bias=SMALL[:, 2:3])
        for c in range(NCH):
            P.tt("pool", G[:, c, :], G[:, c, :], NEGA.all(), ALU.mult)
            P.mm(pgc[:, 0:16], cs(C_UI), G[:, c, 0:16])
            P.mm(pgc[:, 16:32], cs(C_LI), G[:, c, 16:32])
            P.mm(pgc[:, 32:64], cs(C_ONE), G[:, c, :])
            P.cp("dve", EGL[:, c, :], pgc[:, 0:32])
            P.cp("dve", ET[:, c, :], pgc[:, 32:64])
        P.act(EG.all(), EGL.all(), AF.Exp)
        P.tt("pool", EGL.all(), ET.all(), EGL.all(), ALU.subtract)
        P.act(EGL.all(), EGL.all(), AF.Exp)
        P.act(ET.all(), ET.all(), AF.Exp)
        BEG = P.view("BEG", RB, 256, 1088, F32, "p (c h) -> p c h", c=NCH)
        P.tt("pool", BEG.all(), BETA.all(), EG.all(), ALU.mult)

        P.barrier()
        WZ = P.view("WZd", RG, 512, 1024, BF16, "p (k n) -> p k n", k=8)
        ZO = [P.view("ZO%d" % b, RI, 64 + b * 128, 256, BF16) for b in range(2)]
        pz = [psv("pzd%d" % b, 4 + b, 0, 256) for b in range(2)]
        n = 0
        for hh in ([] if skip_z else range(16)):
            load_wchunk(WZ, w2d, 4096 + hh * 128)
            load_u(0)
            for st in range(NST):
                if st + 1 < NST:
                    load_u(st + 1)
                u = UIN[st % 2]
                p = pz[n % 2]
                for k in range(8):
                    P.mm(p.all(), WZ[:, k, :], u[:, k, :], start=(k == 0), stop=(k == 7))
                zo = ZO[n % 2]
                n += 1
                P.act(zo.all(), p.all(), AF.Silu)
                P.st(ZST.t[hh, :, st * 256:(st + 1) * 256], ZSr[st], zo.all())

        if dbg:
            P.barrier()
            DRF = P.dram("DRF", [128, 5440], F32, "ExternalOutput")
            P.st(DRF.t[:, :], P.region("drf"), RF.all())
        for jp in pairs:
            dn_pair(i, j, jp, BETA, G, EGL, EG, ET, BEG)

    def dn_pair(i, j, jp, BETA, G, EGL, EG, ET, BEG):
        w2d = dn_w_in.t[j]
        P.barrier()
        QNT = P.view("QNT", RD, 0, TT, BF16)
        KNT = P.view("KNT", RD, 2176, TT, BF16)
        KTM = P.view("KTM", RD, 4352, TT, BF16, "p (c d) -> p c d", c=NCH)
        VTM = P.view("VTMd", RC, 0, 2 * TT, BF16, "p (e c d) -> p e c d", e=2, c=NCH)
        OACC = P.view("OACC", RA, 0, 2 * TT, F32, "p (e c d) -> p e c d", e=2, c=NCH)
        WX = [P.view("WX%d" % x, RG, 512 * x, 1024, BF16, "p (k n) -> p k n", k=8) for x in range(4)]
        cols = [jp * 128, 1024 + jp * 128, 2048 + (2 * jp) * 128, 2048 + (2 * jp + 1) * 128]
        for x in range(4):
            load_wchunk(WX[x], w2d, cols[x])
        UH = [P.view("UH%d" % b, RH, b * 1040, 2080, BF16, "p (k n) -> p k n", k=8) for b in range(2)]
        PJ = [[P.view("PJ%d_%d" % (b, x), RE, (b * 4 + x) * 260, 260, F32) for x in range(4)] for b in range(2)]
        CV = [[P.view("CV%d_%d" % (b, x), RI, (b * 4 + x) * 256, 256, F32) for x in range(4)] for b in range(2)]
        SQc = [[P.view("SQc%d_%d" % (b, x), RB, 3648 + (b * 2 + x) * 128, 256, BF16) for x in range(2)] for b in range(2)]
        RSc = [[P.view("RSc%d_%d" % (b, x), RA, (b * 2 + x) * 256, 256, F32) for x in range(2)] for b in range(2)]
        VF = [[P.view("VF%d_%d" % (b, x), RA, 1024 + (b * 2 + x) * 128, 256, BF16) for x in range(2)] for b in range(2)]
        pp = [psv("pp%d" % x, x, 0, 260) for x in range(4)]
        pssq = [psv("pssd%d" % x, 4 + x, 0, 256) for x in range(2)]
        ptr = [psv("ptr%d" % b, 6 + b, 0, 512, BF16) for b in range(2)]

        def load_uh(st):
            t0 = st * 256
            lr = 1 if st >= 2 else 0
            rr = 1 if 1 <= st <= 15 else 0
            uh = UH[st % 2]
            a, b = t0 - 2 * lr, t0 + 256 + 2 * rr
            P.ld(uh[:, :, 2 - 2 * lr:258 + 2 * rr], UT.t[:, :, a:b].rearrange("k p t -> p k t"),
                 UTr[st])
            if not lr:
                P.memset("pool", uh[:, :, 0:2], 0.0)
            if not rr:
                P.memset("pool", uh[:, :, 258:260], 0.0)
        def phase1(st):
            uh = UH[st % 2]
            b2 = st % 2
            for x in range(4):
                for k in range(8):
                    P.mm(pp[x].all(), WX[x][:, k, :], uh[:, k, :], start=(k == 0), stop=(k == 7))
                P.cp("act", PJ[b2][x].all(), pp[x].all())
            for x in range(4):
                ch = (cols[x] // 128)
                cw = lambda tap: vcol(V_CW + 160 * j + ch * 5 + tap)
                pj, cv = PJ[b2][x], CV[b2][x]
                P.tsc("dve", cv.all(), pj[:, 0:256], cw(0), ALU.mult)
                for tap in range(1, 5):
                    P.stt(cv.all(), pj[:, tap:tap + 256], cw(tap), cv.all(), ALU.mult, ALU.add)

        def phase2(st):
            b2 = st % 2
            t0 = st * 256
            for x in range(2):
                P.act(CV[b2][x].all(), CV[b2][x].all(), AF.Silu)
            for x in range(2):
                P.act(VF[b2][x].all(), CV[b2][2 + x].all(), AF.Silu)
            for x in range(2):
                P.act(SQc[b2][x].all(), CV[b2][x].all(), AF.Square)
                P.mm(pssq[x].all(), ONEB.all(), SQc[b2][x].all())
            for x in range(2):
                rsq(RSc[b2][x].all(), pssq[x].all())
            for x in range(2):
                dst = (QNT if x == 0 else KNT)[:, t0:t0 + 256]
                P.stt(dst, CV[b2][x].all(), float(128.0 ** -0.5) if x == 0 else 1.0, RSc[b2][x].all(),
                      ALU.mult, ALU.mult)
            pt = ptr[0]
            for hf in range(2):
                P.tr(pt[:, hf * 128:(hf + 1) * 128], KNT[:, t0 + hf * 128:t0 + (hf + 1) * 128], IDB.all())
            P.cp("act", KTM[:, 2 * st:2 * st + 2, :], V(pt.bank, pt.ap[:, 0:256].rearrange("p (a b) -> p a b", a=2)))
            pt = ptr[1]
            for x in range(2):
                for hf in range(2):
                    P.tr(pt[:, x * 256 + hf * 128:x * 256 + (hf + 1) * 128], VF[b2][x][:, hf * 128:(hf + 1) * 128],
                         IDB.all())
            for x in range(2):
                P.cp("act", VTM[:, x, 2 * st:2 * st + 2, :],
                     V(pt.bank, pt.ap[:, x * 256:(x + 1) * 256].rearrange("p (a b) -> p a b", a=2)))

        load_uh(0)
        load_uh(1)
        phase1(0)
        for st in range(NST):
            if st + 2 < NST:
                load_uh(st + 2)
            if st + 1 < NST:
                phase1(st + 1)
            phase2(st)

        P.barrier()
        if dbg and jp == 0:
            DRD = P.dram("DRD", [128, 6528], F32, "ExternalOutput")
            DRC = P.dram("DRC", [128, 4352], F32, "ExternalOutput")
            P.st(DRD.t[:, :], P.region("drd"), RD.all())
            P.st(DRC.t[:, :], P.region("drc"), RC.all())
        NW = 1664

        def ctile(ch, k, dt=F32):
            if dt == F32:
                return P.view("c%df%d" % (ch, k), RE, ch * NW + 128 * k, 128, F32)
            return P.view("c%db%d" % (ch, k), RE, ch * NW + 1152 + 64 * k, 128, BF16)
        KK = [P.view("KK%d" % d, RI, 128 * d, 128, F32) for d in range(2)]
        KKO = [P.view("KKO%d" % d, RB, 128 * d, 128, F32) for d in range(2)]
        KKO1 = [P.view("KKO1%d" % d, RB, 3392 + 128 * d, 128, F32) for d in range(2)]
        QK = [P.view("QK%d" % d, RI, 256 + 128 * d, 128, F32) for d in range(2)]
        S = [P.view("S%d" % c, RI, 512 + 128 * c, 128, F32) for c in range(4)]
        Sb = [P.view("Sb%d" % c, RI, 1024 + 64 * c, 128, BF16) for c in range(4)]
        ONt = [P.view("ON%d" % c, RI, 1280 + 64 * c, 128, BF16) for c in range(4)]
        YPt = [P.view("YP%d" % c, RI, 1536 + 64 * c, 128, BF16) for c in range(4)]
        JK = P.view("JK", RI, 1792, 128, BF16)
        SSt = P.view("SSt", RI, 1856, 8, F32)
        for c in range(4):
            P.memset("pool", S[c].all(), 0.0)
            P.memset("pool", Sb[c].all(), 0.0)
        pkq = [psv("pkq%d" % d, 4 + d, 0, 256) for d in range(2)]
        pfin = psv("pfin", 7, 0, 512, BF16)
        gob = VEC[:, V_GO + 128 * j:V_GO + 128 * (j + 1)]
        fin_n = {"n": 0}

        def finish(ch, e, m):
            hh = 2 * jp + e
            o = OACC[:, e, m, :]
            ss = SSt[:, ch:ch + 1]
            P.op("dve", lambda e_, jk=JK.all(), o_=o, s_=ss: e_.scalar_tensor_tensor(
                out=jk.ap, in0=o_.ap, scalar=1.0, in1=o_.ap, op0=ALU.mult, op1=ALU.mult, accum_out=s_.ap),
                reads=[OACC], writes=[JK, SSt])
            P.act(ss, ss, AF.Ln, scale=1.0 / 128.0, bias=SMALL[:, 1:2])
            P.act(ss, ss, AF.Exp, scale=-0.5)
            P.stt(ONt[ch].all(), o, ss, gob, ALU.mult, ALU.mult)
            q = fin_n["n"] % 4
            fin_n["n"] += 1
            P.tr(pfin[:, q * 128:(q + 1) * 128], ONt[ch].all(), IDB.all())
            P.cp("act", YPt[ch].all(), pfin[:, q * 128:(q + 1) * 128])
            P.st(YPT.t[hh, :, m * 128:(m + 1) * 128], YPr[m // 2], YPt[ch].all())

        def f(v):
            return V(v.t, v.ap.bitcast(F32))

        def unit(ch, e, d, m, first):
            hh = 2 * jp + e
            col = d * 16 + hh
            sc = lambda X: X[:, m, col:col + 1]
            Ud = cs(C_UI) if d == 0 else cs(C_LI)
            Vd = cs(C_LS) if d == 0 else cs(C_US)
            base = ch * 384
            rb = ch * 1280
            NNs = [P.view("c%dN%d" % (ch, k), RR, rb + 128 * k, 128, F32R) for k in range(2)]
            NTYs = [P.view("c%dNTY%d" % (ch, k), RR, rb + 256 + 256 * k, 256, F32R) for k in range(2)]
            NOt = P.view("c%dNO" % ch, RR, rb + 768, 128, F32R)
            TDt = P.view("c%dTD" % ch, RR, rb + 896, 128, F32R)
            Mtt = P.view("c%dMt" % ch, RR, rb + 1024, 128, F32R)
            NO1t = P.view("c%dNO1" % ch, RR, rb + 1152, 128, F32R)
            UG = P.view("c%dUG" % ch, RE, base, 128, F32).all()
            Dm = P.view("c%dDm" % ch, RE, base + 128, 128, F32).all()
            O1 = P.view("c%dO1" % ch, RE, base + 256, 128, F32).all()
            Tt, Pm, PTt, KBG, VB, WTN, VN, VND = [P.view("c%db%d" % (ch, k), RB, 1344 + ch * 512 + 64 * k, 128, BF16)
                                                  for k in range(8)]
            pA = psv("", ch, 0, 128)
            pB = psv("", ch, 128, 128)
            pC = psv("", ch, 256, 128)
            pD = psv("", ch, 384, 128)
            pCD = psv("", ch, 256, 256)
            pCb = psv("", ch, 256, 256, BF16)
            msl = slice(m * 128, (m + 1) * 128)
            if e == 0:
                P.mm(pkq[d][:, 0:128], KNT[:, msl], KNT[:, msl])
                P.mm(pkq[d][:, 128:256], QNT[:, msl], KNT[:, msl])
                P.tt("dve", KK[d].all(), pkq[d][:, 0:128], cs(C_NLSD) if d == 0 else cs(C_NUSD), ALU.mult)
                P.tt("dve", KKO[d].all(), pkq[d][:, 0:128], cs(C_NLSO) if d == 0 else cs(C_NUSO), ALU.mult)
                P.tt("dve", KKO1[d].all(), pkq[d][:, 0:128], cs(C_NLSO1) if d == 0 else cs(C_NUSO1), ALU.mult)
                P.tt("dve", QK[d].all(), pkq[d][:, 128:256], cs(C_LI) if d == 0 else cs(C_UI), ALU.mult)
            P.tsc("dve", UG, Ud, sc(G), ALU.mult)
            P.mm(pA.all(), UG, Vd)
            P.cp("pool", NTYs[0][:, 128:256], cs(C_ID))
            yield
            P.act(Dm, pA.all(), AF.Exp)
            yield
            P.stt(NNs[0].all(), Dm, sc(BETA), KK[d].all(), ALU.mult, ALU.mult)
            P.stt(NO1t.all(), Dm, sc(BETA), KKO1[d].all(), ALU.mult, ALU.mult)
            P.stt(NOt.all(), Dm, sc(BETA), KKO[d].all(), ALU.mult, ALU.mult)
            P.tt("pool", Pm.all(), Dm, QK[d].all(), ALU.mult)
            P.tsc("dve", KBG.all(), KTM[:, m, :], sc(BEG), ALU.mult)
            P.tsc("dve", VB.all(), VTM[:, e, m, :], sc(BETA), ALU.mult)
            yield
            P.tr(pB.all(), f(NNs[0].all()), cs(C_ID))
            P.tr(pCb[:, 0:128], Pm.all(), IDB.all())
            yield
            P.cp("act", NTYs[0][:, 0:128], pB.all())
            P.cp("act", PTt.all(), pCb[:, 0:128])
            yield
            a = 0
            for k in range(5):
                N, NTY = NNs[a], NTYs[a]
                N2, NTY2 = NNs[1 - a], NTYs[1 - a]
                if k < 4:
                    P.mm(pA.all(), NTY[:, 0:128], N.all())
                    P.mm(pCD.all(), N.all(), NTY.all())
                    yield
                    P.cp("act", N2.all(), pA.all())
                    P.cp("act", NTY2[:, 0:128], pCD[:, 0:128])
                    P.tt("dve", NTY2[:, 128:256], pCD[:, 128:256], f(NTY[:, 128:256]), ALU.add)
                    yield
                else:
                    P.mm(pC.all(), N.all(), NTY[:, 128:256])
                    yield
                    P.tt("dve", NTY2[:, 128:256], pC.all(), f(NTY[:, 128:256]), ALU.add)
                    yield
                a = 1 - a
            for (NOx, lastm) in ((NO1t, False), (NOt, True)):
                Yd = NTYs[a][:, 128:256]
                P.tr(pA.all(), f(Yd), cs(C_ID))
                P.mm(pB.all(), NOx.all(), Yd)
                yield
                P.cp("act", TDt.all(), pA.all())
                P.cp("dve", Mtt.all(), pB.all())
                yield
                P.mm(pC.all(), TDt.all(), Mtt.all())
                yield
                if lastm:
                    P.tt("dve", Tt.all(), pC.all(), f(Yd), ALU.add)
                else:
                    P.tt("dve", NTYs[1 - a][:, 128:256], pC.all(), f(Yd), ALU.add)
                    a = 1 - a
                yield
            P.mm(pA.all(), KBG.all(), Tt.all())
            yield
            P.act(WTN.all(), pA.all(), AF.Identity, scale=-1.0)
            yield
            P.mm(pB.all(), Tt.all(), VB.all(), start=True, stop=False)
            P.mm(pB.all(), WTN.all(), Sb[ch].all(), start=False, stop=True)
            P.mm(pC.all(), QNT[:, msl], Sb[ch].all())
            yield
            P.cp("dve", VN.all(), pB.all())
            P.act(VND.all(), pB.all(), AF.Identity, scale=sc(EGL))
            P.act(O1, pC.all(), AF.Identity, scale=sc(EG))
            yield
            P.mm(pD.all(), PTt.all(), VN.all())
            P.mm(pA.all(), KTM[:, m, :], VND.all())
            yield
            if not first:
                P.tt("pool", O1, O1, OACC[:, e, m, :], ALU.add)
            P.tt("dve", OACC[:, e, m, :], pD.all(), O1, ALU.add)
            P.stt(S[ch].all(), S[ch].all(), sc(ET), pA.all(), ALU.mult, ALU.add)
            yield
            P.cp("act", Sb[ch].all(), S[ch].all())
            if not first:
                finish(ch, e, m)

        chains = [[], [], [], []]
        steps = [(0, 1, True), (1, 0, False)] + [(2 + n_, 33 - n_, n_ < 16) for n_ in range(32)]
        for (mf, mb, first) in steps:
            chains[0].append((0, 0, mf, first))
            chains[1].append((1, 0, mf, first))
            chains[2].append((0, 1, mb, first))
            chains[3].append((1, 1, mb, first))
        gens = [None] * 4
        pos = [0] * 4
        live = True
        rnd = 0
        delay = {0: 0, 2: 0, 1: 0, 3: 0}
        while live:
            live = False
            rnd += 1
            for ch in (0, 2, 1, 3):
                if rnd <= delay[ch]:
                    live = True
                    continue
                if gens[ch] is None:
                    if pos[ch] >= len(chains[ch]):
                        continue
                    e, d, m, first = chains[ch][pos[ch]]
                    pos[ch] += 1
                    gens[ch] = unit(ch, e, d, m, first)
                live = True
                try:
                    next(gens[ch])
                except StopIteration:
                    gens[ch] = None

    P.memset("pool", SMALL[:, 0:1], -8.0)
    P.memset("pool", SMALL[:, 1:2], EPS)
    P.memset("pool", SMALL[:, 2:3], 1.0)
    for i in layers:
        stage_mod(i)
        stage_norm(i, hT0 if i == layers[0] else HT, HTr)
        if upto == "norm":
            continue
        if i % 2 == 1:
            stage_attn(i)
            if upto == "attn":
                continue
            stage_out(i, att_w_out.t[i // 2], 8, False, i == DEPTH - 1)
        else:
            stage_dn(i)
            if upto == "dn":
                continue
            stage_out(i, dn_w_out.t[i // 2], 16, True, False)
    P.barrier()
    P.emit()
    return nc, P


def _host_inputs(inputs):
    cstt, ropec, ropes = _const_tables()
    f = lambda a: np.ascontiguousarray(np.asarray(a, dtype=np.float32))
    x, c, ctx, c_ctx = f(inputs["x"]), f(inputs["c"]), f(inputs["ctx"]), f(inputs["c_ctx"])
    col = lambda v: v.reshape(-1, 128).T
    maps = []
    for b in range(8):
        vec = np.zeros((128, NV), np.float32)
        cc = np.stack([col(c[b]), col(c_ctx)], -1)
        vec[:, V_C:V_C + 16] = cc.reshape(128, 16)
        ng = f(inputs["norm_g"])
        for l in range(4):
            vec[:, V_NG + 8 * l:V_NG + 8 * (l + 1)] = col(ng[l])
            ab = col(f(inputs["ada_b"])[l])
            vec[:, V_AB + 48 * l:V_AB + 48 * (l + 1)] = np.repeat(ab, 2, axis=1)
        vec[:, V_FG:V_FG + 8] = col(f(inputs["final_norm_g"]))
        cw = f(inputs["dn_conv_w"])
        for l in range(2):
            t = cw[l].T.reshape(32, 128, 5).transpose(1, 0, 2)
            vec[:, V_CW + 160 * l:V_CW + 160 * (l + 1)] = t.reshape(128, 160)
            vec[:, V_QG + l] = f(inputs["att_q_norm_g"])[l]
            vec[:, V_KG + l] = f(inputs["att_k_norm_g"])[l]
            vec[:, V_GO + 128 * l:V_GO + 128 * (l + 1)] = f(inputs["dn_o_norm_g"])[l][None, :]
            vec[:, V_AL + 32 * l:V_AL + 32 * (l + 1)] = f(inputs["dn_a_log"])[l].reshape(1, 32)
            vec[:, V_DT + 32 * l:V_DT + 32 * (l + 1)] = f(inputs["dn_dt_bias"])[l].reshape(1, 32)
        h0 = np.concatenate([ctx[b], x[b]], 0)
        hT = np.ascontiguousarray(h0.T).reshape(8, 128, TT)
        maps.append({
            "hT0": hT, "vecs": vec, "cst": cstt, "ropeC": ropec, "ropeS": ropes,
            "ada_w": f(inputs["ada_w"]), "dn_w_in": f(inputs["dn_w_in"]), "dn_w_out": f(inputs["dn_w_out"]),
            "att_w_in": f(inputs["att_w_in"]), "att_w_out": f(inputs["att_w_out"]),
        })
    return maps


def kernel(**inputs):
    nc, _ = build()
    maps = _host_inputs(inputs)
    res = run_bass_kernel_spmd(nc, maps, core_ids=list(range(8)))
    out = np.stack([np.asarray(r["outT"]).reshape(1024, 4096).T for r in res.results], 0)
    return np.ascontiguousarray(out.astype(np.float32))
```

```python
import numpy as np
import ml_dtypes
import concourse.bass as bass
import concourse.mybir as mybir
from concourse.bass_utils import run_bass_kernel_spmd
from contextlib import ExitStack

F32 = mybir.dt.float32
BF16 = mybir.dt.bfloat16
F32R = mybir.dt.float32r
AF = mybir.ActivationFunctionType
ALU = mybir.AluOpType
CENG = ("pe", "act", "dve", "pool")
ENGS = ("pe", "act", "dve", "pool", "sp")


class V:
    __slots__ = ("t", "ap")

    def __init__(self, t, ap):
        self.t = t
        self.ap = ap


class T:
    __slots__ = ("name", "t", "w", "r", "key", "space")

    def __init__(self, name, t, space, key=None):
        self.name = name
        self.t = t
        self.space = space
        self.w = []
        self.r = []
        self.key = key or name

    def __getitem__(self, k):
        return V(self, self.t[k])

    def all(self):
        return V(self, self.t[:])


class Prog:
    def __init__(self, nc):
        self.nc = nc
        self.es = ExitStack()
        self.ops = {e: [] for e in ENGS}
        self.esem = {e: self.es.enter_context(nc.semaphore("s_" + e)) for e in CENG}
        self.dsems = {}
        self.dcnt = {}
        self.extra = {e: [] for e in ENGS}
        self.ro = T("ro", None, "dram")
        self.vcache = {}

    def sb(self, name, shape, dt):
        return T(name, self.es.enter_context(self.nc.sbuf_tensor(name, list(shape), dt)), "sb")

    def ps(self, name, shape, dt):
        return T(name, self.es.enter_context(self.nc.psum_tensor(name, list(shape), dt)), "ps")

    def dram(self, name, shape, dt, kind="Internal"):
        return T(name, self.nc.dram_tensor(name, list(shape), dt, kind=kind).ap(), "dram")

    def region(self, name):
        return T(name, None, "dram")

    def view(self, name, raw, lo, n, dt, pat=None, **kw):
        ck = (raw.name, lo, n, str(dt), pat, tuple(sorted(kw.items())))
        if ck in self.vcache:
            return self.vcache[ck]
        words = n if dt in (F32, F32R) else n // 2
        ap = raw.t[:, lo:lo + words]
        if dt != ap.dtype:
            ap = ap.bitcast(dt)
        if pat:
            ap = ap.rearrange(pat, **kw)
        t = T(name, ap, raw.space, key="%s@%d" % (raw.name, lo))
        self.vcache[ck] = t
        return t

    def _deps(self, eng, reads, writes, shared=()):
        deps = list(self.extra[eng])
        self.extra[eng] = []
        for t in reads:
            deps.extend(t.w)
        for t in shared:
            deps.extend(t.r)
        for t in writes:
            deps.extend(t.w)
            deps.extend(t.r)
        out = []
        for d in deps:
            if d[0] == "E":
                if d[1] == "pe" and eng == "pe":
                    continue
                self.ops[d[1]][d[2]]["flag"] = True
            out.append(d)
        return out

    def _mark(self, me, reads, writes, shared=()):
        for t in reads:
            t.r.append(me)
            if len(t.r) > 64:
                t.r = t.r[-48:]
        for t in writes:
            t.w = [me]
            t.r = []
        for t in shared:
            t.w.append(me)

    def op(self, eng, fn, reads=(), writes=()):
        writes = list(dict.fromkeys(list(writes) + [x for x in reads if x.space == "ps"]))
        reads = [x for x in dict.fromkeys(reads) if x is not self.ro and x.space != "ps"]
        deps = self._deps(eng, reads, writes)
        idx = len(self.ops[eng])
        self.ops[eng].append(dict(fn=fn, waits=deps, flag=False, dma=None))
        self._mark(("E", eng, idx), reads, writes)

    def dma(self, fn, reads=(), writes=(), shared=(), q="sp"):
        reads = [x for x in reads if x is not self.ro]
        deps = self._deps(q, reads, writes, shared)
        st = [t for t in list(writes) + list(shared) + list(reads) if t.space != "dram"][0]
        if st.key not in self.dsems:
            self.dsems[st.key] = self.es.enter_context(self.nc.semaphore("d%d" % len(self.dsems)))
            self.dcnt[st.key] = 0
        sem = self.dsems[st.key]
        if self.dcnt[st.key]:
            deps.append(("D", sem, self.dcnt[st.key]))
        self.dcnt[st.key] += 16
        me = ("D", sem, self.dcnt[st.key])
        self.ops[q].append(dict(fn=fn, waits=deps, flag=False, dma=sem))
        self._mark(me, reads, writes, shared)
        return me

    def barrier(self):
        deps = []
        for e in CENG:
            if self.ops[e]:
                i = len(self.ops[e]) - 1
                self.ops[e][i]["flag"] = True
                deps.append(("E", e, i))
        for k, sem in self.dsems.items():
            deps.append(("D", sem, self.dcnt[k]))
        for e in ENGS:
            self.extra[e] = list(deps) + self.extra[e]

    @staticmethod
    def _ts(*vs):
        return [v.t for v in vs if isinstance(v, V)]

    @staticmethod
    def _a(v):
        return v.ap if isinstance(v, V) else v

    def mm(self, out, lhsT, rhs, start=True, stop=True):
        self.op("pe", lambda e: e.matmul(out.ap, lhsT=lhsT.ap, rhs=rhs.ap, start=start, stop=stop),
                reads=self._ts(lhsT, rhs), writes=self._ts(out))

    def tr(self, out, in_, ident):
        self.op("pe", lambda e: e.transpose(out.ap, in_.ap, ident.ap),
                reads=self._ts(in_, ident), writes=self._ts(out))

    def act(self, out, in_, func, scale=None, bias=None, accum=None):
        kw = {}
        if scale is not None:
            kw["scale"] = self._a(scale)
        if bias is not None:
            kw["bias"] = self._a(bias)
        if accum is not None:
            kw["accum_out"] = accum.ap
        self.op("act", lambda e: e.activation(out=out.ap, in_=in_.ap, func=func, **kw),
                reads=self._ts(in_, scale, bias), writes=self._ts(out, accum))

    def tt(self, eng, out, in0, in1, op):
        self.op(eng, lambda e: e.tensor_tensor(out=out.ap, in0=in0.ap, in1=in1.ap, op=op),
                reads=self._ts(in0, in1), writes=self._ts(out))

    def tsc(self, eng, out, in0, s1, op0, s2=None, op1=None):
        if op1 is None:
            fn = lambda e: e.tensor_scalar(out=out.ap, in0=in0.ap, scalar1=self._a(s1), scalar2=None, op0=op0)
        else:
            fn = lambda e: e.tensor_scalar(out=out.ap, in0=in0.ap, scalar1=self._a(s1),
                                           scalar2=self._a(s2), op0=op0, op1=op1)
        self.op(eng, fn, reads=self._ts(in0, s1, s2), writes=self._ts(out))

    def stt(self, out, in0, scalar, in1, op0, op1):
        self.op("dve", lambda e: e.scalar_tensor_tensor(out=out.ap, in0=in0.ap, scalar=self._a(scalar),
                                                         in1=in1.ap, op0=op0, op1=op1),
                reads=self._ts(in0, scalar, in1), writes=self._ts(out))

    def cp(self, eng, out, in_):
        if eng == "act":
            self.op("act", lambda e: e.copy(out=out.ap, in_=in_.ap), reads=self._ts(in_), writes=self._ts(out))
        else:
            self.op(eng, lambda e: e.tensor_copy(out=out.ap, in_=in_.ap), reads=self._ts(in_),
                    writes=self._ts(out))

    def memset(self, eng, out, val):
        self.op(eng, lambda e: e.memset(out.ap, val), writes=self._ts(out))

    def ld(self, out, src_ap, src_t=None, q="sp"):
        return self.dma(lambda e: e.dma_start(out=out.ap, in_=src_ap), reads=[src_t or self.ro],
                        writes=[out.t], q=q)

    def st(self, dst_ap, dst_t, in_, shared=True, q="sp"):
        if shared:
            return self.dma(lambda e: e.dma_start(out=dst_ap, in_=in_.ap), reads=[in_.t], shared=[dst_t], q=q)
        return self.dma(lambda e: e.dma_start(out=dst_ap, in_=in_.ap), reads=[in_.t], writes=[dst_t], q=q)

    def emit(self):
        nc = self.nc
        cum = {}
        for e in CENG:
            c = 0
            arr = []
            for o in self.ops[e]:
                if o["flag"]:
                    c += 1
                arr.append(c)
            cum[e] = arr
        self.stats = {}

        def run(e, eng):
            seen = {}
            nw = 0
            for o in self.ops[e]:
                for d in o["waits"]:
                    if d[0] == "E":
                        sem, val, key = self.esem[d[1]], cum[d[1]][d[2]], d[1]
                    else:
                        sem, val, key = d[1], d[2], id(d[1])
                    if seen.get(key, 0) >= val:
                        continue
                    seen[key] = val
                    eng.wait_ge(sem, val)
                    nw += 1
                ins = o["fn"](eng)
                if o["dma"] is not None:
                    ins.then_inc(o["dma"], 16)
                elif o["flag"]:
                    ins.then_inc(self.esem[e], 1)
            if e == "sp":
                for k, sem in self.dsems.items():
                    if seen.get(id(sem), 0) < self.dcnt[k]:
                        eng.wait_ge(sem, self.dcnt[k])
            self.stats[e] = (len(self.ops[e]), nw)

        with nc.Block() as block:
            @block.sync
            def _(eng):
                run("sp", eng)

            @block.tensor
            def _(eng):
                run("pe", eng)

            @block.scalar
            def _(eng):
                run("act", eng)

            @block.vector
            def _(eng):
                run("dve", eng)

            @block.gpsimd
            def _(eng):
                run("pool", eng)
        self.es.close()


D = 1024
TT = 4352
NCH = 34
NST = 17
EPS = 1e-6
DEPTH = 4

(C_ID, C_LS, C_LI, C_US, C_UI, C_NLSD, C_NUSD, C_ONE, C_O1024, C_O128, C_RT,
 C_NLSO, C_NUSO, C_NLSO1, C_NUSO1) = [i * 128 for i in range(15)]
NCST = 15 * 128

V_C = 0
V_NG = V_C + 16
V_AB = V_NG + 32
V_FG = V_AB + 192
V_CW = V_FG + 8
V_QG = V_CW + 320
V_KG = V_QG + 2
V_GO = V_KG + 2
V_AL = V_GO + 256
V_DT = V_AL + 64
NV = V_DT + 64


def _const_tables():
    p = np.arange(128)[:, None]
    f = np.arange(128)[None, :]
    t = np.zeros((128, NCST), np.float32)
    t[:, C_ID:C_ID + 128] = (p == f)
    t[:, C_LS:C_LS + 128] = (p > f)
    t[:, C_LI:C_LI + 128] = (p >= f)
    t[:, C_US:C_US + 128] = (p < f)
    t[:, C_UI:C_UI + 128] = (p <= f)
    bd = (p // 64) == (f // 64)
    bd32 = (p // 32) == (f // 32)
    t[:, C_NLSD:C_NLSD + 128] = -((p > f) & bd32).astype(np.float32)
    t[:, C_NUSD:C_NUSD + 128] = -((p < f) & bd32).astype(np.float32)
    t[:, C_NLSO1:C_NLSO1 + 128] = -((p > f) & bd & ~bd32).astype(np.float32)
    t[:, C_NUSO1:C_NUSO1 + 128] = -((p < f) & bd & ~bd32).astype(np.float32)
    t[:, C_NLSO:C_NLSO + 128] = -((p > f) & ~bd).astype(np.float32)
    t[:, C_NUSO:C_NUSO + 128] = -((p < f) & ~bd).astype(np.float32)
    t[:, C_ONE:C_ONE + 128] = 1.0
    t[:, C_O1024:C_O1024 + 128] = 1.0 / 1024.0
    t[:, C_O128:C_O128 + 128] = 1.0 / 128.0
    R = np.zeros((128, 128), np.float32)
    for m in range(128):
        if (m % 64) < 32:
            R[m, m + 32] = -1.0
        else:
            R[m, m - 32] = 1.0
    t[:, C_RT:C_RT + 128] = R.T
    tok = np.arange(4096)
    row = (tok // 64).astype(np.float32)
    col = (tok % 64).astype(np.float32)
    inv = (10000.0 ** (-np.arange(0, 64, 2, dtype=np.float32) / 64.0)).astype(np.float32)
    ang_r = row[None, :] * inv[:, None]
    ang_c = col[None, :] * inv[:, None]
    ang = np.concatenate([ang_r, ang_r, ang_c, ang_c], 0).astype(np.float32)
    return t, np.cos(ang).astype(np.float32), np.sin(ang).astype(np.float32)


def build(layers=(0, 1, 2, 3), dbg=False, upto="out", pairs=range(8), skip_z=False):
    nc = bass.Bass("TRN2", target_bir_lowering=False)
    P = Prog(nc)
    RO = P.ro
    EI = "ExternalInput"
    hT0 = P.dram("hT0", [8, 128, TT], F32, EI)
    vecs = P.dram("vecs", [128, NV], F32, EI)
    cst = P.dram("cst", [128, NCST], F32, EI)
    ropeC = P.dram("ropeC", [128, 4096], F32, EI)
    ropeS = P.dram("ropeS", [128, 4096], F32, EI)
    ada_w = P.dram("ada_w", [4, 1024, 3072], F32, EI)
    dn_w_in = P.dram("dn_w_in", [2, 1024, 6208], F32, EI)
    dn_w_out = P.dram("dn_w_out", [2, 2048, 1024], F32, EI)
    att_w_in = P.dram("att_w_in", [2, 1024, 2560], F32, EI)
    att_w_out = P.dram("att_w_out", [2, 1024, 1024], F32, EI)
    outT = P.dram("outT", [8, 128, 4096], F32, "ExternalOutput")
    IK = "ExternalOutput" if dbg else "Internal"
    HT = P.dram("HT", [8, 128, TT], F32, IK)
    UT = P.dram("UT", [8, 128, TT], BF16, IK)
    YPT = P.dram("YPT", [16, 128, TT], BF16, IK)
    ZST = P.dram("ZST", [16, 128, TT], BF16, IK)
    HTr = [P.region("HTr%d" % i) for i in range(NST)]
    UTr = [P.region("UTr%d" % i) for i in range(NST)]
    YPr = [P.region("YPr%d" % i) for i in range(NST)]
    ZSr = [P.region("ZSr%d" % i) for i in range(NST)]
    OUTr = P.region("OUTr")
    dbg_out = {}

    CST = P.sb("CST", [128, NCST], F32)
    VEC = P.sb("VEC", [128, NV], F32)
    IDB = P.sb("IDB", [128, 128], BF16)
    ONEB = P.sb("ONEB", [128, 128], BF16)
    SC = P.sb("SC", [128, 16], F32)
    MOD = P.sb("MOD", [128, 48], F32)
    GSC = P.sb("GSC", [128, 16], F32)
    SMALL = P.sb("SMALL", [128, 64], F32)
    RA = P.sb("RA", [128, 8704], F32)
    RB = P.sb("RB", [128, 4480], F32)
    RC = P.sb("RC", [128, 4352], F32)
    RD = P.sb("RD", [128, 6528], F32)
    RE = P.sb("RE", [128, 4096], F32)
    RR = P.sb("RR", [128, 5120], F32R)
    RF = P.sb("RF", [128, 5440], F32)
    RG = P.sb("RG", [128, 5120], F32)
    RH = P.sb("RH", [128, 3072], F32)
    RI = P.sb("RI", [128, 2048], F32)
    PSB = [P.ps("PSB%d" % i, [128, 512], F32) for i in range(8)]

    def cs(col, n=128):
        return CST[:, col:col + n]

    def vcol(col):
        return VEC[:, col:col + 1]

    def rsq(dst, src):
        P.act(dst, src, AF.Sqrt, bias=SMALL[:, 1:2])
        P.op("dve", lambda e: e.reciprocal(out=dst.ap, in_=dst.ap), reads=[dst.t], writes=[dst.t])

    class PV:
        def __init__(self, bank, ap):
            self.bank, self.ap = bank, ap

        def __getitem__(self, k):
            return V(self.bank, self.ap[k])

        def all(self):
            return V(self.bank, self.ap)

    def psv(name, bank, lo, n, dt=F32):
        words = n if dt == F32 else n // 2
        ap = PSB[bank].t[:, lo:lo + words]
        if dt != F32:
            ap = ap.bitcast(dt)
        return PV(PSB[bank], ap)

    P.ld(CST.all(), cst.t[:, :])
    P.ld(VEC.all(), vecs.t[:, :])
    P.cp("dve", IDB.all(), cs(C_ID))
    P.cp("dve", ONEB.all(), cs(C_ONE))
    P.act(SC.all(), VEC[:, V_C:V_C + 16], AF.Silu)

    def stage_mod(i):
        P.barrier()
        WA = [P.view("WA%d" % b, RG, b * 1024, 1024, F32, "p (k n) -> p k n", k=8) for b in range(2)]
        pm = psv("pm", 0, 0, 48)
        aw = ada_w.t[i].rearrange("(k p) n -> p k n", p=128)
        for m in range(24):
            w = WA[m % 2]
            P.ld(w.all(), aw[:, :, m * 128:(m + 1) * 128])
            for k in range(8):
                P.mm(pm[:, 2 * m:2 * m + 2], w[:, k, :], SC[:, 2 * k:2 * k + 2], start=(k == 0), stop=(k == 7))
        P.tt("dve", MOD.all(), pm.all(), VEC[:, V_AB + 48 * i:V_AB + 48 * (i + 1)], ALU.add)
        for k in range(8):
            P.tsc("dve", GSC[:, 2 * k:2 * k + 2], MOD[:, 2 * (8 + k):2 * (8 + k) + 2], 1.0, ALU.add,
                  vcol(V_NG + 8 * i + k), ALU.mult)

    def stage_norm(i, src, srcr):
        P.barrier()
        HIN = [P.view("HIN%d" % b, RC, b * 2048, 2048, F32, "p (k n) -> p k n", k=8) for b in range(2)]
        SQ = P.view("SQ", RB, 0, 2048, F32, "p (k n) -> p k n", k=8)
        TMP = P.view("TMPn", RB, 2048, 2048, F32, "p (k n) -> p k n", k=8)
        UO = [P.view("UO%d" % b, RH, b * 1024, 2048, BF16, "p (k n) -> p k n", k=8) for b in range(2)]
        RS = P.view("RS", RI, 0, 256, F32)
        pn = psv("pn", 0, 0, 256)

        def load(st):
            P.ld(HIN[st % 2].all(), src.t[:, :, st * 256:(st + 1) * 256].rearrange("k p t -> p k t"), srcr[st])
        load(0)
        for st in range(NST):
            if st + 1 < NST:
                load(st + 1)
            h = HIN[st % 2]
            r = 1 if st == 0 else 0
            P.act(SQ.all(), h.all(), AF.Square)
            for k in range(8):
                P.mm(pn.all(), cs(C_O1024), SQ[:, k, :], start=(k == 0), stop=(k == 7))
            rsq(RS.all(), pn.all())
            uo = UO[st % 2]
            for k in range(8):
                P.tt("pool" if k % 2 else "dve", TMP[:, k, :], h[:, k, :], RS.all(), ALU.mult)
                P.act(uo[:, k, :], TMP[:, k, :], AF.Identity, scale=GSC[:, 2 * k + r:2 * k + r + 1],
                      bias=MOD[:, 2 * k + r:2 * k + r + 1])
            P.st(UT.t[:, :, st * 256:(st + 1) * 256].rearrange("k p t -> p k t"), UTr[st], uo.all(), shared=False)

    wstate = {"n": 0}

    def load_wchunk(dst, wsrc2d, c0, ncols=128, kc=8):
        b = wstate["n"] % 2
        wstate["n"] += 1
        stg = P.view("WSTG%d" % b, RG, 3072 + b * 1024, kc * ncols, F32, "p (k n) -> p k n", k=kc)
        P.ld(stg.all(), wsrc2d.rearrange("(k p) n -> p k n", p=128)[:, :, c0:c0 + ncols])
        P.cp("pool", dst.all(), stg.all())

    def stage_out(i, wout2d, kc, use_z, last):
        P.barrier()
        WO = P.view("WO", RA, 0, kc * 1024, BF16, "p (k n) -> p k n", k=kc)
        for c in range(8):
            for k0 in range(0, kc, 8):
                b = wstate["n"] % 2
                wstate["n"] += 1
                stg = P.view("WSTG%d" % b, RG, 3072 + b * 1024, 1024, F32, "p (k n) -> p k n", k=8)
                P.ld(stg.all(), wout2d.rearrange("(k p) n -> p k n", p=128)[:, k0:k0 + 8, c * 128:(c + 1) * 128])
                P.cp("pool", WO[:, k0:k0 + 8, c * 128:(c + 1) * 128], stg.all())
        YT = [P.view("YT%d" % b, RC, b * 2048, kc * 256, BF16, "p (k n) -> p k n", k=kc) for b in range(2)]
        ZT = [P.view("ZT%d" % b, RD, b * 2048, kc * 256, BF16, "p (k n) -> p k n", k=kc) for b in range(2)]
        HI = [P.view("HI%d" % b, RB, b * 2048, 2048, F32, "p (k n) -> p k n", k=8) for b in range(2)]
        HO = [P.view("HO%d" % b, RE, b * 2048, 2048, F32, "p (k n) -> p k n", k=8) for b in range(2)]
        SQ = P.view("SQo", RD, 4096, 2048, F32, "p (k n) -> p k n", k=8)
        RS = P.view("RSo", RI, 0, 256, F32)
        pn = psv("pno", 4, 0, 256)
        pys = [psv("py%d" % b, b, 0, 256) for b in range(4)]
        hsrc = hT0 if i == layers[0] else HT

        def load(st):
            b = st % 2
            P.ld(YT[b].all(), YPT.t[0:kc, :, st * 256:(st + 1) * 256].rearrange("k p t -> p k t"), YPr[st])
            if use_z:
                P.ld(ZT[b].all(), ZST.t[0:kc, :, st * 256:(st + 1) * 256].rearrange("k p t -> p k t"), ZSr[st])
            P.ld(HI[b].all(), hsrc.t[:, :, st * 256:(st + 1) * 256].rearrange("k p t -> p k t"), HTr[st])
        first = 1 if last else 0
        load(first)
        for st in range(first, NST):
            if st + 1 < NST:
                load(st + 1)
            b = st % 2
            r = 1 if st == 0 else 0
            y = YT[b]
            if use_z:
                P.tt("pool", y.all(), y.all(), ZT[b].all(), ALU.mult)
            ho = HO[b]
            for c in range(8):
                py = pys[c % 4]
                for k in range(kc):
                    P.mm(py.all(), WO[:, k, c * 128:(c + 1) * 128], y[:, k, :], start=(k == 0), stop=(k == kc - 1))
                P.stt(ho[:, c, :], py.all(), MOD[:, 2 * (16 + c) + r:2 * (16 + c) + r + 1], HI[b][:, c, :],
                      ALU.mult, ALU.add)
            if not last:
                P.st(HT.t[:, :, st * 256:(st + 1) * 256].rearrange("k p t -> p k t"), HTr[st], ho.all(), shared=False)
            else:
                P.act(SQ.all(), ho.all(), AF.Square)
                for k in range(8):
                    P.mm(pn.all(), cs(C_O1024), SQ[:, k, :], start=(k == 0), stop=(k == 7))
                rsq(RS.all(), pn.all())
                for k in range(8):
                    P.stt(SQ[:, k, :], ho[:, k, :], vcol(V_FG + k), RS.all(), ALU.mult, ALU.mult)
                P.st(outT.t[:, :, (st - 1) * 256:st * 256].rearrange("k p t -> p k t"), OUTr, SQ.all())

    def stage_attn(i):
        j = i // 2
        need_ctx = i < DEPTH - 1
        w2d = att_w_in.t[j]
        P.barrier()
        KT = P.view("KT", RA, 0, 2 * TT, BF16, "p (g t) -> p g t", g=2)
        VTM = P.view("VTM", RA, TT, 2 * TT, BF16, "p (g c d) -> p g c d", g=2, c=NCH)
        QT = P.view("QT", RB, 0, TT, BF16)
        ZS = P.view("ZS", RB, TT // 2, TT, BF16)
        UIN = [P.view("UIN%d" % b, RH, b * 1024, 2048, BF16, "p (k n) -> p k n", k=8) for b in range(2)]
        WQ = P.view("WQ", RG, 0, 1024, BF16, "p (k n) -> p k n", k=8)
        WZ = P.view("WZ", RG, 512, 1024, BF16, "p (k n) -> p k n", k=8)
        WV = P.view("WV", RG, 1024, 1024, BF16, "p (k n) -> p k n", k=8)
        XN = [P.view("XN%d" % b, RD, b * 256, 256, F32) for b in range(2)]
        SQ = [P.view("SQa%d" % b, RD, 512 + b * 256, 256, F32) for b in range(2)]
        RS = [P.view("RSa%d" % b, RD, 1024 + b * 256, 256, F32) for b in range(2)]
        T1 = [P.view("T1a%d" % b, RD, 1536 + b * 256, 256, F32) for b in range(2)]
        T2 = [P.view("T2a%d" % b, RD, 2048 + b * 256, 256, F32) for b in range(2)]
        CT = [P.view("CT%d" % b, RD, 2560 + b * 256, 256, F32) for b in range(2)]
        STb = [P.view("STb%d" % b, RD, 3072 + b * 256, 256, F32) for b in range(2)]
        PT = [P.view("PTa%d" % b, RE, b * 256, 512, BF16) for b in range(3)]
        RIV = P.view("RIV", RE, 1024, 512, F32)
        OT = P.view("OTa", RE, 1536, 512, F32)
        YO = [P.view("YO%d" % b, RE, 2048 + b * 256, 512, BF16) for b in range(2)]
        pqs = [psv("pq%d" % b, 4 + b, 0, 256) for b in range(2)]
        pzs = [psv("pz%d" % b, 2 + b, 0, 256) for b in range(2)]
        psss = [psv("pss%d" % b, 6 + b, 0, 256) for b in range(2)]
        prots = [psv("prot%d" % b, b, 0, 256) for b in range(2)]
        pv = [psv("pv%d" % b, 2 + b, 256, 128) for b in range(2)]

        def load_u(st):
            P.ld(UIN[st % 2].all(), UT.t[:, :, st * 256:(st + 1) * 256].rearrange("k p t -> p k t"), UTr[st])

        rope_n = {"n": 0}

        def proj_norm_rope(st, w, gcol, dst, extra_scale):
            u = UIN[st % 2]
            b2 = rope_n["n"] % 2
            rope_n["n"] += 1
            pq, pss, prot = pqs[b2], psss[b2], prots[b2]
            for k in range(8):
                P.mm(pq.all(), w[:, k, :], u[:, k, :], start=(k == 0), stop=(k == 7))
            P.act(SQ[b2].all(), pq.all(), AF.Square)
            P.mm(pss.all(), cs(C_O128), SQ[b2].all())
            rsq(RS[b2].all(), pss.all())
            if st == 0:
                P.stt(dst, pq.all(), gcol, RS[b2].all(), ALU.mult, ALU.mult)
                return
            P.stt(XN[b2].all(), pq.all(), gcol, RS[b2].all(), ALU.mult, ALU.mult)
            P.ld(CT[b2].all(), ropeC.t[:, (st - 1) * 256:st * 256])
            P.ld(STb[b2].all(), ropeS.t[:, (st - 1) * 256:st * 256])
            P.mm(prot.all(), cs(C_RT), XN[b2].all())
            P.tt("pool", T1[b2].all(), XN[b2].all(), CT[b2].all(), ALU.mult)
            P.tt("dve", T2[b2].all(), prot.all(), STb[b2].all(), ALU.mult)
            P.tt("pool", dst, T1[b2].all(), T2[b2].all(), ALU.add)

        for g in range(2):
            load_wchunk(WQ, w2d, 1024 + g * 128)
            load_wchunk(WV, w2d, 1280 + g * 128)
            load_u(0)
            for st in range(NST):
                if st + 1 < NST:
                    load_u(st + 1)
                proj_norm_rope(st, WQ, vcol(V_KG + j), KT[:, g, st * 256:(st + 1) * 256], None)
                u = UIN[st % 2]
                for half in range(2):
                    c = st * 2 + half
                    p = pv[half]
                    for k in range(8):
                        P.mm(p.all(), u[:, k, half * 128:(half + 1) * 128], WV[:, k, :], start=(k == 0), stop=(k == 7))
                    P.cp("act", VTM[:, g, c, :], p.all())
        pS = [psv("pS%d" % b, b, 0, 512) for b in range(2)]
        pOs = [psv("pO%d" % b, 2 + 2 * b, 0, 512) for b in range(2)]
        pSums = [psv("pSum%d" % b, 3 + 2 * b, 0, 512) for b in range(2)]
        zn = 0
        for h in range(8):
            g = h // 4
            load_wchunk(WQ, w2d, h * 128)
            load_wchunk(WZ, w2d, 1536 + h * 128)
            load_u(0)
            for st in range(NST):
                if st + 1 < NST:
                    load_u(st + 1)
                proj_norm_rope(st, WQ, vcol(V_QG + j), QT[:, st * 256:(st + 1) * 256], None)
                u = UIN[st % 2]
                pz = pzs[zn % 2]
                zn += 1
                for k in range(8):
                    P.mm(pz.all(), WZ[:, k, :], u[:, k, :], start=(k == 0), stop=(k == 7))
                P.act(ZS[:, st * 256:(st + 1) * 256], pz.all(), AF.Silu)
            blocks = ([(0, 256, 2)] if need_ctx else []) + [(256 + 512 * qb, 512, NCH) for qb in range(8)]
            for bi, (q0, nq, nk) in enumerate(blocks):
                pO, pSum = pOs[bi % 2], pSums[bi % 2]

                def score(kt):
                    P.mm(pS[kt % 2][:, 0:nq], KT[:, g, kt * 128:(kt + 1) * 128], QT[:, q0:q0 + nq])
                score(0)
                for kt in range(nk):
                    if kt + 1 < nk:
                        score(kt + 1)
                    pt = PT[kt % 3]
                    P.act(pt[:, 0:nq], pS[kt % 2][:, 0:nq], AF.Exp, scale=float(128.0 ** -0.5), bias=SMALL[:, 0:1])
                    P.mm(pO[:, 0:nq], VTM[:, g, kt, :], pt[:, 0:nq], start=(kt == 0), stop=(kt == nk - 1))
                    P.mm(pSum[:, 0:nq], ONEB.all(), pt[:, 0:nq], start=(kt == 0), stop=(kt == nk - 1))
                P.op("dve", lambda e, o=RIV[:, 0:nq], s=pSum[:, 0:nq]: e.reciprocal(out=o.ap, in_=s.ap),
                     reads=[pSum.bank], writes=[RIV])
                P.tt("dve", OT[:, 0:nq], pO[:, 0:nq], RIV[:, 0:nq], ALU.mult)
                yo = YO[bi % 2]
                P.tt("pool", yo[:, 0:nq], OT[:, 0:nq], ZS[:, q0:q0 + nq], ALU.mult)
                regs = [YPr[(q0 + s * 256) // 256] for s in range(nq // 256)]
                P.dma(lambda e, d=YPT.t[h, :, q0:q0 + nq], y_=yo[:, 0:nq]: e.dma_start(out=d, in_=y_.ap),
                      reads=[yo], shared=regs)


    def stage_dn(i):
        j = i // 2
        w2d = dn_w_in.t[j]
        P.barrier()
        UIN = [P.view("UIN%d" % b, RH, b * 1024, 2048, BF16, "p (k n) -> p k n", k=8) for b in range(2)]
        BETA = P.view("BETA", RF, 0, 1088, F32, "p (c h) -> p c h", c=NCH)
        G = P.view("G", RF, 1088, 1088, F32, "p (c h) -> p c h", c=NCH)
        EGL = P.view("EGL", RF, 2176, 1088, F32, "p (c h) -> p c h", c=NCH)
        EG = P.view("EG", RF, 3264, 1088, F32, "p (c h) -> p c h", c=NCH)
        ET = P.view("ET", RF, 4352, 1088, F32, "p (c h) -> p c h", c=NCH)
        NEGA = P.view("NEGA", RI, 0, 32, F32)

        def load_u(st):
            P.ld(UIN[st % 2].all(), UT.t[:, :, st * 256:(st + 1) * 256].rearrange("k p t -> p k t"), UTr[st])

        WAB = P.view("WAB", RG, 0, 512, BF16, "p (k n) -> p k n", k=8)
        load_wchunk(WAB, w2d, 6144, ncols=64)
        pab = psv("pab", 6, 0, 64)
        pgc = psv("pgc", 7, 0, 64)
        P.act(NEGA.all(), VEC[:, V_AL + 32 * j:V_AL + 32 * (j + 1)], AF.Exp)
        P.tsc("dve", NEGA.all(), NEGA.all(), -1.0, ALU.mult)
        load_u(0)
        for st in range(NST):
            if st + 1 < NST:
                load_u(st + 1)
            u = UIN[st % 2]
            for half in range(2):
                c = st * 2 + half
                for k in range(8):
                    P.mm(pab.all(), u[:, k, half * 128:(half + 1) * 128], WAB[:, k, :], start=(k == 0), stop=(k == 7))
                P.cp("dve", BETA[:, c, :], pab[:, 0:32])
                P.tt("dve", G[:, c, :], pab[:, 32:64], VEC[:, V_DT + 32 * j:V_DT + 32 * (j + 1)], ALU.add)
        P.act(BETA.all(), BETA.all(), AF.Sigmoid)
        P.act(G.all(), G.all(), AF.Exp)
        P.act(G.all(), G.all(), AF.Ln, bias=SMALL[:, 2:3])
        for c in range(NCH):
            P.tt("pool", G[:, c, :], G[:, c, :], NEGA.all(), ALU.mult)
            P.mm(pgc[:, 0:16], cs(C_UI), G[:, c, 0:16])
            P.mm(pgc[:, 16:32], cs(C_LI), G[:, c, 16:32])
            P.mm(pgc[:, 32:64], cs(C_ONE), G[:, c, :])
            P.cp("dve", EGL[:, c, :], pgc[:, 0:32])
            P.cp("dve", ET[:, c, :], pgc[:, 32:64])
        P.act(EG.all(), EGL.all(), AF.Exp)
        P.tt("pool", EGL.all(), ET.all(), EGL.all(), ALU.subtract)
        P.act(EGL.all(), EGL.all(), AF.Exp)
        P.act(ET.all(), ET.all(), AF.Exp)
        BEG = P.view("BEG", RB, 256, 1088, F32, "p (c h) -> p c h", c=NCH)
        P.tt("pool", BEG.all(), BETA.all(), EG.all(), ALU.mult)

        P.barrier()
        WZ = P.view("WZd", RG, 512, 1024, BF16, "p (k n) -> p k n", k=8)
        ZO = [P.view("ZO%d" % b, RI, 64 + b * 128, 256, BF16) for b in range(2)]
        pz = [psv("pzd%d" % b, 4 + b, 0, 256) for b in range(2)]
        n = 0
        for hh in ([] if skip_z else range(16)):
            load_wchunk(WZ, w2d, 4096 + hh * 128)
            load_u(0)
            for st in range(NST):
                if st + 1 < NST:
                    load_u(st + 1)
                u = UIN[st % 2]
                p = pz[n % 2]
                for k in range(8):
                    P.mm(p.all(), WZ[:, k, :], u[:, k, :], start=(k == 0), stop=(k == 7))
                zo = ZO[n % 2]
                n += 1
                P.act(zo.all(), p.all(), AF.Silu)
                P.st(ZST.t[hh, :, st * 256:(st + 1) * 256], ZSr[st], zo.all())

        if dbg:
            P.barrier()
            DRF = P.dram("DRF", [128, 5440], F32, "ExternalOutput")
            P.st(DRF.t[:, :], P.region("drf"), RF.all())
        for jp in pairs:
            dn_pair(i, j, jp, BETA, G, EGL, EG, ET, BEG)

    def dn_pair(i, j, jp, BETA, G, EGL, EG, ET, BEG):
        w2d = dn_w_in.t[j]
        P.barrier()
        QNT = P.view("QNT", RD, 0, TT, BF16)
        KNT = P.view("KNT", RD, 2176, TT, BF16)
        KTM = P.view("KTM", RD, 4352, TT, BF16, "p (c d) -> p c d", c=NCH)
        VTM = P.view("VTMd", RC, 0, 2 * TT, BF16, "p (e c d) -> p e c d", e=2, c=NCH)
        OACC = P.view("OACC", RA, 0, 2 * TT, F32, "p (e c d) -> p e c d", e=2, c=NCH)
        WX = [P.view("WX%d" % x, RG, 512 * x, 1024, BF16, "p (k n) -> p k n", k=8) for x in range(4)]
        cols = [jp * 128, 1024 + jp * 128, 2048 + (2 * jp) * 128, 2048 + (2 * jp + 1) * 128]
        for x in range(4):
            load_wchunk(WX[x], w2d, cols[x])
        UH = [P.view("UH%d" % b, RH, b * 1040, 2080, BF16, "p (k n) -> p k n", k=8) for b in range(2)]
        PJ = [[P.view("PJ%d_%d" % (b, x), RE, (b * 4 + x) * 260, 260, F32) for x in range(4)] for b in range(2)]
        CV = [[P.view("CV%d_%d" % (b, x), RI, (b * 4 + x) * 256, 256, F32) for x in range(4)] for b in range(2)]
        SQc = [[P.view("SQc%d_%d" % (b, x), RB, 3648 + (b * 2 + x) * 128, 256, BF16) for x in range(2)] for b in range(2)]
        RSc = [[P.view("RSc%d_%d" % (b, x), RA, (b * 2 + x) * 256, 256, F32) for x in range(2)] for b in range(2)]
        VF = [[P.view("VF%d_%d" % (b, x), RA, 1024 + (b * 2 + x) * 128, 256, BF16) for x in range(2)] for b in range(2)]
        pp = [psv("pp%d" % x, x, 0, 260) for x in range(4)]
        pssq = [psv("pssd%d" % x, 4 + x, 0, 256) for x in range(2)]
        ptr = [psv("ptr%d" % b, 6 + b, 0, 512, BF16) for b in range(2)]

        def load_uh(st):
            t0 = st * 256
            lr = 1 if st >= 2 else 0
            rr = 1 if 1 <= st <= 15 else 0
            uh = UH[st % 2]
            a, b = t0 - 2 * lr, t0 + 256 + 2 * rr
            P.ld(uh[:, :, 2 - 2 * lr:258 + 2 * rr], UT.t[:, :, a:b].rearrange("k p t -> p k t"),
                 UTr[st])
            if not lr:
                P.memset("pool", uh[:, :, 0:2], 0.0)
            if not rr:
                P.memset("pool", uh[:, :, 258:260], 0.0)
        def phase1(st):
            uh = UH[st % 2]
            b2 = st % 2
            for x in range(4):
                for k in range(8):
                    P.mm(pp[x].all(), WX[x][:, k, :], uh[:, k, :], start=(k == 0), stop=(k == 7))
                P.cp("act", PJ[b2][x].all(), pp[x].all())
            for x in range(4):
                ch = (cols[x] // 128)
                cw = lambda tap: vcol(V_CW + 160 * j + ch * 5 + tap)
                pj, cv = PJ[b2][x], CV[b2][x]
                P.tsc("dve", cv.all(), pj[:, 0:256], cw(0), ALU.mult)
                for tap in range(1, 5):
                    P.stt(cv.all(), pj[:, tap:tap + 256], cw(tap), cv.all(), ALU.mult, ALU.add)

        def phase2(st):
            b2 = st % 2
            t0 = st * 256
            for x in range(2):
                P.act(CV[b2][x].all(), CV[b2][x].all(), AF.Silu)
            for x in range(2):
                P.act(VF[b2][x].all(), CV[b2][2 + x].all(), AF.Silu)
            for x in range(2):
                P.act(SQc[b2][x].all(), CV[b2][x].all(), AF.Square)
                P.mm(pssq[x].all(), ONEB.all(), SQc[b2][x].all())
            for x in range(2):
                rsq(RSc[b2][x].all(), pssq[x].all())
            for x in range(2):
                dst = (QNT if x == 0 else KNT)[:, t0:t0 + 256]
                P.stt(dst, CV[b2][x].all(), float(128.0 ** -0.5) if x == 0 else 1.0, RSc[b2][x].all(),
                      ALU.mult, ALU.mult)
            pt = ptr[0]
            for hf in range(2):
                P.tr(pt[:, hf * 128:(hf + 1) * 128], KNT[:, t0 + hf * 128:t0 + (hf + 1) * 128], IDB.all())
            P.cp("act", KTM[:, 2 * st:2 * st + 2, :], V(pt.bank, pt.ap[:, 0:256].rearrange("p (a b) -> p a b", a=2)))
            pt = ptr[1]
            for x in range(2):
                for hf in range(2):
                    P.tr(pt[:, x * 256 + hf * 128:x * 256 + (hf + 1) * 128], VF[b2][x][:, hf * 128:(hf + 1) * 128],
                         IDB.all())
            for x in range(2):
                P.cp("act", VTM[:, x, 2 * st:2 * st + 2, :],
                     V(pt.bank, pt.ap[:, x * 256:(x + 1) * 256].rearrange("p (a b) -> p a b", a=2)))

        load_uh(0)
        load_uh(1)
        phase1(0)
        for st in range(NST):
            if st + 2 < NST:
                load_uh(st + 2)
            if st + 1 < NST:
                phase1(st + 1)
            phase2(st)

        P.barrier()
        if dbg and jp == 0:
            DRD = P.dram("DRD", [128, 6528], F32, "ExternalOutput")
            DRC = P.dram("DRC", [128, 4352], F32, "ExternalOutput")
            P.st(DRD.t[:, :], P.region("drd"), RD.all())
            P.st(DRC.t[:, :], P.region("drc"), RC.all())
        NW = 1664

        def ctile(ch, k, dt=F32):
            if dt == F32:
                return P.view("c%df%d" % (ch, k), RE, ch * NW + 128 * k, 128, F32)
            return P.view("c%db%d" % (ch, k), RE, ch * NW + 1152 + 64 * k, 128, BF16)
        KK = [P.view("KK%d" % d, RI, 128 * d, 128, F32) for d in range(2)]
        KKO = [P.view("KKO%d" % d, RB, 128 * d, 128, F32) for d in range(2)]
        KKO1 = [P.view("KKO1%d" % d, RB, 3392 + 128 * d, 128, F32) for d in range(2)]
        QK = [P.view("QK%d" % d, RI, 256 + 128 * d, 128, F32) for d in range(2)]
        S = [P.view("S%d" % c, RI, 512 + 128 * c, 128, F32) for c in range(4)]
        Sb = [P.view("Sb%d" % c, RI, 1024 + 64 * c, 128, BF16) for c in range(4)]
        ONt = [P.view("ON%d" % c, RI, 1280 + 64 * c, 128, BF16) for c in range(4)]
        YPt = [P.view("YP%d" % c, RI, 1536 + 64 * c, 128, BF16) for c in range(4)]
        JK = P.view("JK", RI, 1792, 128, BF16)
        SSt = P.view("SSt", RI, 1856, 8, F32)
        for c in range(4):
            P.memset("pool", S[c].all(), 0.0)
            P.memset("pool", Sb[c].all(), 0.0)
        pkq = [psv("pkq%d" % d, 4 + d, 0, 256) for d in range(2)]
        pfin = psv("pfin", 7, 0, 512, BF16)
        gob = VEC[:, V_GO + 128 * j:V_GO + 128 * (j + 1)]
        fin_n = {"n": 0}

        def finish(ch, e, m):
            hh = 2 * jp + e
            o = OACC[:, e, m, :]
            ss = SSt[:, ch:ch + 1]
            P.op("dve", lambda e_, jk=JK.all(), o_=o, s_=ss: e_.scalar_tensor_tensor(
                out=jk.ap, in0=o_.ap, scalar=1.0, in1=o_.ap, op0=ALU.mult, op1=ALU.mult, accum_out=s_.ap),
                reads=[OACC], writes=[JK, SSt])
            P.act(ss, ss, AF.Ln, scale=1.0 / 128.0, bias=SMALL[:, 1:2])
            P.act(ss, ss, AF.Exp, scale=-0.5)
            P.stt(ONt[ch].all(), o, ss, gob, ALU.mult, ALU.mult)
            q = fin_n["n"] % 4
            fin_n["n"] += 1
            P.tr(pfin[:, q * 128:(q + 1) * 128], ONt[ch].all(), IDB.all())
            P.cp("act", YPt[ch].all(), pfin[:, q * 128:(q + 1) * 128])
            P.st(YPT.t[hh, :, m * 128:(m + 1) * 128], YPr[m // 2], YPt[ch].all())

        def f(v):
            return V(v.t, v.ap.bitcast(F32))

        def unit(ch, e, d, m, first):
            hh = 2 * jp + e
            col = d * 16 + hh
            sc = lambda X: X[:, m, col:col + 1]
            Ud = cs(C_UI) if d == 0 else cs(C_LI)
            Vd = cs(C_LS) if d == 0 else cs(C_US)
            base = ch * 384
            rb = ch * 1280
            NNs = [P.view("c%dN%d" % (ch, k), RR, rb + 128 * k, 128, F32R) for k in range(2)]
            NTYs = [P.view("c%dNTY%d" % (ch, k), RR, rb + 256 + 256 * k, 256, F32R) for k in range(2)]
            NOt = P.view("c%dNO" % ch, RR, rb + 768, 128, F32R)
            TDt = P.view("c%dTD" % ch, RR, rb + 896, 128, F32R)
            Mtt = P.view("c%dMt" % ch, RR, rb + 1024, 128, F32R)
            NO1t = P.view("c%dNO1" % ch, RR, rb + 1152, 128, F32R)
            UG = P.view("c%dUG" % ch, RE, base, 128, F32).all()
            Dm = P.view("c%dDm" % ch, RE, base + 128, 128, F32).all()
            O1 = P.view("c%dO1" % ch, RE, base + 256, 128, F32).all()
            Tt, Pm, PTt, KBG, VB, WTN, VN, VND = [P.view("c%db%d" % (ch, k), RB, 1344 + ch * 512 + 64 * k, 128, BF16)
                                                  for k in range(8)]
            pA = psv("", ch, 0, 128)
            pB = psv("", ch, 128, 128)
            pC = psv("", ch, 256, 128)
            pD = psv("", ch, 384, 128)
            pCD = psv("", ch, 256, 256)
            pCb = psv("", ch, 256, 256, BF16)
            msl = slice(m * 128, (m + 1) * 128)
            if e == 0:
                P.mm(pkq[d][:, 0:128], KNT[:, msl], KNT[:, msl])
                P.mm(pkq[d][:, 128:256], QNT[:, msl], KNT[:, msl])
                P.tt("dve", KK[d].all(), pkq[d][:, 0:128], cs(C_NLSD) if d == 0 else cs(C_NUSD), ALU.mult)
                P.tt("dve", KKO[d].all(), pkq[d][:, 0:128], cs(C_NLSO) if d == 0 else cs(C_NUSO), ALU.mult)
                P.tt("dve", KKO1[d].all(), pkq[d][:, 0:128], cs(C_NLSO1) if d == 0 else cs(C_NUSO1), ALU.mult)
                P.tt("dve", QK[d].all(), pkq[d][:, 128:256], cs(C_LI) if d == 0 else cs(C_UI), ALU.mult)
            P.act(UG, Ud, AF.Identity, scale=sc(G))
            P.mm(pA.all(), UG, Vd)
            P.cp("pool", NTYs[0][:, 128:256], cs(C_ID))
            yield
            P.act(Dm, pA.all(), AF.Exp)
            P.act(KBG.all(), KTM[:, m, :], AF.Identity, scale=sc(BEG))
            P.act(VB.all(), VTM[:, e, m, :], AF.Identity, scale=sc(BETA))
            yield
            P.stt(NNs[0].all(), Dm, sc(BETA), KK[d].all(), ALU.mult, ALU.mult)
            P.stt(NO1t.all(), Dm, sc(BETA), KKO1[d].all(), ALU.mult, ALU.mult)
            P.stt(NOt.all(), Dm, sc(BETA), KKO[d].all(), ALU.mult, ALU.mult)
            P.tt("pool", Pm.all(), Dm, QK[d].all(), ALU.mult)
            yield
            P.tr(pB.all(), f(NNs[0].all()), cs(C_ID))
            P.tr(pCb[:, 0:128], Pm.all(), IDB.all())
            yield
            P.cp("act", NTYs[0][:, 0:128], pB.all())
            P.cp("act", PTt.all(), pCb[:, 0:128])
            yield
            a = 0
            for k in range(5):
                N, NTY = NNs[a], NTYs[a]
                N2, NTY2 = NNs[1 - a], NTYs[1 - a]
                if k < 4:
                    P.mm(pA.all(), NTY[:, 0:128], N.all())
                    P.mm(pCD.all(), N.all(), NTY.all())
                    yield
                    P.cp("act", N2.all(), pA.all())
                    P.cp("act", NTY2[:, 0:128], pCD[:, 0:128])
                    P.tt("dve", NTY2[:, 128:256], pCD[:, 128:256], f(NTY[:, 128:256]), ALU.add)
                    yield
                else:
                    P.mm(pC.all(), N.all(), NTY[:, 128:256])
                    yield
                    P.tt("dve", NTY2[:, 128:256], pC.all(), f(NTY[:, 128:256]), ALU.add)
                    yield
                a = 1 - a
            for (NOx, lastm) in ((NO1t, False), (NOt, True)):
                Yd = NTYs[a][:, 128:256]
                P.tr(pA.all(), f(Yd), cs(C_ID))
                P.mm(pB.all(), NOx.all(), Yd)
                yield
                P.cp("act", TDt.all(), pA.all())
                P.cp("dve", Mtt.all(), pB.all())
                yield
                P.mm(pC.all(), TDt.all(), Mtt.all())
                yield
                if lastm:
                    P.tt("dve", Tt.all(), pC.all(), f(Yd), ALU.add)
                else:
                    P.tt("dve", NTYs[1 - a][:, 128:256], pC.all(), f(Yd), ALU.add)
                    a = 1 - a
                yield
            P.mm(pA.all(), KBG.all(), Tt.all())
            yield
            P.act(WTN.all(), pA.all(), AF.Identity, scale=-1.0)
            yield
            P.mm(pB.all(), Tt.all(), VB.all(), start=True, stop=False)
            P.mm(pB.all(), WTN.all(), Sb[ch].all(), start=False, stop=True)
            P.mm(pC.all(), QNT[:, msl], Sb[ch].all())
            yield
            P.cp("dve", VN.all(), pB.all())
            P.act(VND.all(), pB.all(), AF.Identity, scale=sc(EGL))
            P.act(O1, pC.all(), AF.Identity, scale=sc(EG))
            yield
            P.mm(pD.all(), PTt.all(), VN.all())
            P.mm(pA.all(), KTM[:, m, :], VND.all())
            yield
            if not first:
                P.tt("pool", O1, O1, OACC[:, e, m, :], ALU.add)
            P.tt("dve", OACC[:, e, m, :], pD.all(), O1, ALU.add)
            P.stt(S[ch].all(), S[ch].all(), sc(ET), pA.all(), ALU.mult, ALU.add)
            yield
            P.cp("act", Sb[ch].all(), S[ch].all())
            if not first:
                finish(ch, e, m)

        chains = [[], [], [], []]
        steps = [(0, 1, True), (1, 0, False)] + [(2 + n_, 33 - n_, n_ < 16) for n_ in range(32)]
        for (mf, mb, first) in steps:
            chains[0].append((0, 0, mf, first))
            chains[1].append((1, 0, mf, first))
            chains[2].append((0, 1, mb, first))
            chains[3].append((1, 1, mb, first))
        gens = [None] * 4
        pos = [0] * 4
        live = True
        rnd = 0
        delay = {0: 0, 2: 0, 1: 0, 3: 0}
        while live:
            live = False
            rnd += 1
            for ch in (0, 2, 1, 3):
                if rnd <= delay[ch]:
                    live = True
                    continue
                if gens[ch] is None:
                    if pos[ch] >= len(chains[ch]):
                        continue
                    e, d, m, first = chains[ch][pos[ch]]
                    pos[ch] += 1
                    gens[ch] = unit(ch, e, d, m, first)
                live = True
                try:
                    next(gens[ch])
                except StopIteration:
                    gens[ch] = None

    P.memset("pool", SMALL[:, 0:1], -8.0)
    P.memset("pool", SMALL[:, 1:2], EPS)
    P.memset("pool", SMALL[:, 2:3], 1.0)
    for i in layers:
        stage_mod(i)
        stage_norm(i, hT0 if i == layers[0] else HT, HTr)
        if upto == "norm":
            continue
        if i % 2 == 1:
            stage_attn(i)
            if upto == "attn":
                continue
            stage_out(i, att_w_out.t[i // 2], 8, False, i == DEPTH - 1)
        else:
            stage_dn(i)
            if upto == "dn":
                continue
            stage_out(i, dn_w_out.t[i // 2], 16, True, False)
    P.barrier()
    P.emit()
    return nc, P


def _host_inputs(inputs):
    cstt, ropec, ropes = _const_tables()
    f = lambda a: np.ascontiguousarray(np.asarray(a, dtype=np.float32))
    x, c, ctx, c_ctx = f(inputs["x"]), f(inputs["c"]), f(inputs["ctx"]), f(inputs["c_ctx"])
    col = lambda v: v.reshape(-1, 128).T
    maps = []
    for b in range(8):
        vec = np.zeros((128, NV), np.float32)
        cc = np.stack([col(c[b]), col(c_ctx)], -1)
        vec[:, V_C:V_C + 16] = cc.reshape(128, 16)
        ng = f(inputs["norm_g"])
        for l in range(4):
            vec[:, V_NG + 8 * l:V_NG + 8 * (l + 1)] = col(ng[l])
            ab = col(f(inputs["ada_b"])[l])
            vec[:, V_AB + 48 * l:V_AB + 48 * (l + 1)] = np.repeat(ab, 2, axis=1)
        vec[:, V_FG:V_FG + 8] = col(f(inputs["final_norm_g"]))
        cw = f(inputs["dn_conv_w"])
        for l in range(2):
            t = cw[l].T.reshape(32, 128, 5).transpose(1, 0, 2)
            vec[:, V_CW + 160 * l:V_CW + 160 * (l + 1)] = t.reshape(128, 160)
            vec[:, V_QG + l] = f(inputs["att_q_norm_g"])[l]
            vec[:, V_KG + l] = f(inputs["att_k_norm_g"])[l]
            vec[:, V_GO + 128 * l:V_GO + 128 * (l + 1)] = f(inputs["dn_o_norm_g"])[l][None, :]
            vec[:, V_AL + 32 * l:V_AL + 32 * (l + 1)] = f(inputs["dn_a_log"])[l].reshape(1, 32)
            vec[:, V_DT + 32 * l:V_DT + 32 * (l + 1)] = f(inputs["dn_dt_bias"])[l].reshape(1, 32)
        h0 = np.concatenate([ctx[b], x[b]], 0)
        hT = np.ascontiguousarray(h0.T).reshape(8, 128, TT)
        maps.append({
            "hT0": hT, "vecs": vec, "cst": cstt, "ropeC": ropec, "ropeS": ropes,
            "ada_w": f(inputs["ada_w"]), "dn_w_in": f(inputs["dn_w_in"]), "dn_w_out": f(inputs["dn_w_out"]),
            "att_w_in": f(inputs["att_w_in"]), "att_w_out": f(inputs["att_w_out"]),
        })
    return maps


def kernel(**inputs):
    nc, _ = build()
    maps = _host_inputs(inputs)
    res = run_bass_kernel_spmd(nc, maps, core_ids=list(range(8)))
    out = np.stack([np.asarray(r["outT"]).reshape(1024, 4096).T for r in res.results], 0)
    return np.ascontiguousarray(out.astype(np.float32))
```

```python
import numpy as np
import ml_dtypes
import concourse.bass as bass
import concourse.mybir as mybir
from concourse.bass_utils import run_bass_kernel_spmd
from contextlib import ExitStack

F32 = mybir.dt.float32
BF16 = mybir.dt.bfloat16
F32R = mybir.dt.float32r
AF = mybir.ActivationFunctionType
ALU = mybir.AluOpType
CENG = ("pe", "act", "dve", "pool")
ENGS = ("pe", "act", "dve", "pool", "sp")


class V:
    __slots__ = ("t", "ap")

    def __init__(self, t, ap):
        self.t = t
        self.ap = ap


class T:
    __slots__ = ("name", "t", "w", "r", "key", "space")

    def __init__(self, name, t, space, key=None):
        self.name = name
        self.t = t
        self.space = space
        self.w = []
        self.r = []
        self.key = key or name

    def __getitem__(self, k):
        return V(self, self.t[k])

    def all(self):
        return V(self, self.t[:])


class Prog:
    def __init__(self, nc):
        self.nc = nc
        self.es = ExitStack()
        self.ops = {e: [] for e in ENGS}
        self.esem = {e: self.es.enter_context(nc.semaphore("s_" + e)) for e in CENG}
        self.dsems = {}
        self.dcnt = {}
        self.extra = {e: [] for e in ENGS}
        self.ro = T("ro", None, "dram")
        self.vcache = {}

    def sb(self, name, shape, dt):
        return T(name, self.es.enter_context(self.nc.sbuf_tensor(name, list(shape), dt)), "sb")

    def ps(self, name, shape, dt):
        return T(name, self.es.enter_context(self.nc.psum_tensor(name, list(shape), dt)), "ps")

    def dram(self, name, shape, dt, kind="Internal"):
        return T(name, self.nc.dram_tensor(name, list(shape), dt, kind=kind).ap(), "dram")

    def region(self, name):
        return T(name, None, "dram")

    def view(self, name, raw, lo, n, dt, pat=None, **kw):
        ck = (raw.name, lo, n, str(dt), pat, tuple(sorted(kw.items())))
        if ck in self.vcache:
            return self.vcache[ck]
        words = n if dt in (F32, F32R) else n // 2
        ap = raw.t[:, lo:lo + words]
        if dt != ap.dtype:
            ap = ap.bitcast(dt)
        if pat:
            ap = ap.rearrange(pat, **kw)
        t = T(name, ap, raw.space, key="%s@%d" % (raw.name, lo))
        self.vcache[ck] = t
        return t

    def _deps(self, eng, reads, writes, shared=()):
        deps = list(self.extra[eng])
        self.extra[eng] = []
        for t in reads:
            deps.extend(t.w)
        for t in shared:
            deps.extend(t.r)
        for t in writes:
            deps.extend(t.w)
            deps.extend(t.r)
        out = []
        for d in deps:
            if d[0] == "E":
                if d[1] == "pe" and eng == "pe":
                    continue
                self.ops[d[1]][d[2]]["flag"] = True
            out.append(d)
        return out

    def _mark(self, me, reads, writes, shared=()):
        for t in reads:
            t.r.append(me)
            if len(t.r) > 64:
                t.r = t.r[-48:]
        for t in writes:
            t.w = [me]
            t.r = []
        for t in shared:
            t.w.append(me)

    def op(self, eng, fn, reads=(), writes=()):
        writes = list(dict.fromkeys(list(writes) + [x for x in reads if x.space == "ps"]))
        reads = [x for x in dict.fromkeys(reads) if x is not self.ro and x.space != "ps"]
        deps = self._deps(eng, reads, writes)
        idx = len(self.ops[eng])
        self.ops[eng].append(dict(fn=fn, waits=deps, flag=False, dma=None))
        self._mark(("E", eng, idx), reads, writes)

    def dma(self, fn, reads=(), writes=(), shared=(), q="sp"):
        reads = [x for x in reads if x is not self.ro]
        deps = self._deps(q, reads, writes, shared)
        st = [t for t in list(writes) + list(shared) + list(reads) if t.space != "dram"][0]
        if st.key not in self.dsems:
            self.dsems[st.key] = self.es.enter_context(self.nc.semaphore("d%d" % len(self.dsems)))
            self.dcnt[st.key] = 0
        sem = self.dsems[st.key]
        if self.dcnt[st.key]:
            deps.append(("D", sem, self.dcnt[st.key]))
        self.dcnt[st.key] += 16
        me = ("D", sem, self.dcnt[st.key])
        self.ops[q].append(dict(fn=fn, waits=deps, flag=False, dma=sem))
        self._mark(me, reads, writes, shared)
        return me

    def barrier(self):
        deps = []
        for e in CENG:
            if self.ops[e]:
                i = len(self.ops[e]) - 1
                self.ops[e][i]["flag"] = True
                deps.append(("E", e, i))
        for k, sem in self.dsems.items():
            deps.append(("D", sem, self.dcnt[k]))
        for e in ENGS:
            self.extra[e] = list(deps) + self.extra[e]

    @staticmethod
    def _ts(*vs):
        return [v.t for v in vs if isinstance(v, V)]

    @staticmethod
    def _a(v):
        return v.ap if isinstance(v, V) else v

    def mm(self, out, lhsT, rhs, start=True, stop=True):
        self.op("pe", lambda e: e.matmul(out.ap, lhsT=lhsT.ap, rhs=rhs.ap, start=start, stop=stop),
                reads=self._ts(lhsT, rhs), writes=self._ts(out))

    def tr(self, out, in_, ident):
        self.op("pe", lambda e: e.transpose(out.ap, in_.ap, ident.ap),
                reads=self._ts(in_, ident), writes=self._ts(out))

    def act(self, out, in_, func, scale=None, bias=None, accum=None):
        kw = {}
        if scale is not None:
            kw["scale"] = self._a(scale)
        if bias is not None:
            kw["bias"] = self._a(bias)
        if accum is not None:
            kw["accum_out"] = accum.ap
        self.op("act", lambda e: e.activation(out=out.ap, in_=in_.ap, func=func, **kw),
                reads=self._ts(in_, scale, bias), writes=self._ts(out, accum))

    def tt(self, eng, out, in0, in1, op):
        self.op(eng, lambda e: e.tensor_tensor(out=out.ap, in0=in0.ap, in1=in1.ap, op=op),
                reads=self._ts(in0, in1), writes=self._ts(out))

    def tsc(self, eng, out, in0, s1, op0, s2=None, op1=None):
        if op1 is None:
            fn = lambda e: e.tensor_scalar(out=out.ap, in0=in0.ap, scalar1=self._a(s1), scalar2=None, op0=op0)
        else:
            fn = lambda e: e.tensor_scalar(out=out.ap, in0=in0.ap, scalar1=self._a(s1),
                                           scalar2=self._a(s2), op0=op0, op1=op1)
        self.op(eng, fn, reads=self._ts(in0, s1, s2), writes=self._ts(out))

    def stt(self, out, in0, scalar, in1, op0, op1):
        self.op("dve", lambda e: e.scalar_tensor_tensor(out=out.ap, in0=in0.ap, scalar=self._a(scalar),
                                                         in1=in1.ap, op0=op0, op1=op1),
                reads=self._ts(in0, scalar, in1), writes=self._ts(out))

    def cp(self, eng, out, in_):
        if eng == "act":
            self.op("act", lambda e: e.copy(out=out.ap, in_=in_.ap), reads=self._ts(in_), writes=self._ts(out))
        else:
            self.op(eng, lambda e: e.tensor_copy(out=out.ap, in_=in_.ap), reads=self._ts(in_),
                    writes=self._ts(out))

    def memset(self, eng, out, val):
        self.op(eng, lambda e: e.memset(out.ap, val), writes=self._ts(out))

    def ld(self, out, src_ap, src_t=None, q="sp"):
        return self.dma(lambda e: e.dma_start(out=out.ap, in_=src_ap), reads=[src_t or self.ro],
                        writes=[out.t], q=q)

    def st(self, dst_ap, dst_t, in_, shared=True, q="sp"):
        if shared:
            return self.dma(lambda e: e.dma_start(out=dst_ap, in_=in_.ap), reads=[in_.t], shared=[dst_t], q=q)
        return self.dma(lambda e: e.dma_start(out=dst_ap, in_=in_.ap), reads=[in_.t], writes=[dst_t], q=q)

    def emit(self):
        nc = self.nc
        cum = {}
        for e in CENG:
            c = 0
            arr = []
            for o in self.ops[e]:
                if o["flag"]:
                    c += 1
                arr.append(c)
            cum[e] = arr
        self.stats = {}

        def run(e, eng):
            seen = {}
            nw = 0
            for o in self.ops[e]:
                for d in o["waits"]:
                    if d[0] == "E":
                        sem, val, key = self.esem[d[1]], cum[d[1]][d[2]], d[1]
                    else:
                        sem, val, key = d[1], d[2], id(d[1])
                    if seen.get(key, 0) >= val:
                        continue
                    seen[key] = val
                    eng.wait_ge(sem, val)
                    nw += 1
                ins = o["fn"](eng)
                if o["dma"] is not None:
                    ins.then_inc(o["dma"], 16)
                elif o["flag"]:
                    ins.then_inc(self.esem[e], 1)
            if e == "sp":
                for k, sem in self.dsems.items():
                    if seen.get(id(sem), 0) < self.dcnt[k]:
                        eng.wait_ge(sem, self.dcnt[k])
            self.stats[e] = (len(self.ops[e]), nw)

        with nc.Block() as block:
            @block.sync
            def _(eng):
                run("sp", eng)

            @block.tensor
            def _(eng):
                run("pe", eng)

            @block.scalar
            def _(eng):
                run("act", eng)

            @block.vector
            def _(eng):
                run("dve", eng)

            @block.gpsimd
            def _(eng):
                run("pool", eng)
        self.es.close()


D = 1024
TT = 4352
NCH = 34
NST = 17
EPS = 1e-6
DEPTH = 4

(C_ID, C_LS, C_LI, C_US, C_UI, C_NLSD, C_NUSD, C_ONE, C_O1024, C_O128, C_RT,
 C_NLSO, C_NUSO, C_NLSO1, C_NUSO1) = [i * 128 for i in range(15)]
NCST = 15 * 128

V_C = 0
V_NG = V_C + 16
V_AB = V_NG + 32
V_FG = V_AB + 192
V_CW = V_FG + 8
V_QG = V_CW + 320
V_KG = V_QG + 2
V_GO = V_KG + 2
V_AL = V_GO + 256
V_DT = V_AL + 64
NV = V_DT + 64


def _const_tables():
    p = np.arange(128)[:, None]
    f = np.arange(128)[None, :]
    t = np.zeros((128, NCST), np.float32)
    t[:, C_ID:C_ID + 128] = (p == f)
    t[:, C_LS:C_LS + 128] = (p > f)
    t[:, C_LI:C_LI + 128] = (p >= f)
    t[:, C_US:C_US + 128] = (p < f)
    t[:, C_UI:C_UI + 128] = (p <= f)
    bd = (p // 64) == (f // 64)
    bd32 = (p // 32) == (f // 32)
    t[:, C_NLSD:C_NLSD + 128] = -((p > f) & bd32).astype(np.float32)
    t[:, C_NUSD:C_NUSD + 128] = -((p < f) & bd32).astype(np.float32)
    t[:, C_NLSO1:C_NLSO1 + 128] = -((p > f) & bd & ~bd32).astype(np.float32)
    t[:, C_NUSO1:C_NUSO1 + 128] = -((p < f) & bd & ~bd32).astype(np.float32)
    t[:, C_NLSO:C_NLSO + 128] = -((p > f) & ~bd).astype(np.float32)
    t[:, C_NUSO:C_NUSO + 128] = -((p < f) & ~bd).astype(np.float32)
    t[:, C_ONE:C_ONE + 128] = 1.0
    t[:, C_O1024:C_O1024 + 128] = 1.0 / 1024.0
    t[:, C_O128:C_O128 + 128] = 1.0 / 128.0
    R = np.zeros((128, 128), np.float32)
    for m in range(128):
        if (m % 64) < 32:
            R[m, m + 32] = -1.0
        else:
            R[m, m - 32] = 1.0
    t[:, C_RT:C_RT + 128] = R.T
    tok = np.arange(4096)
    row = (tok // 64).astype(np.float32)
    col = (tok % 64).astype(np.float32)
    inv = (10000.0 ** (-np.arange(0, 64, 2, dtype=np.float32) / 64.0)).astype(np.float32)
    ang_r = row[None, :] * inv[:, None]
    ang_c = col[None, :] * inv[:, None]
    ang = np.concatenate([ang_r, ang_r, ang_c, ang_c], 0).astype(np.float32)
    return t, np.cos(ang).astype(np.float32), np.sin(ang).astype(np.float32)


def build(layers=(0, 1, 2, 3), dbg=False, upto="out", pairs=range(8), skip_z=False):
    nc = bass.Bass("TRN2", target_bir_lowering=False)
    P = Prog(nc)
    RO = P.ro
    EI = "ExternalInput"
    hT0 = P.dram("hT0", [8, 128, TT], F32, EI)
    vecs = P.dram("vecs", [128, NV], F32, EI)
    cst = P.dram("cst", [128, NCST], F32, EI)
    ropeC = P.dram("ropeC", [128, 4096], F32, EI)
    ropeS = P.dram("ropeS", [128, 4096], F32, EI)
    ada_w = P.dram("ada_w", [4, 1024, 3072], F32, EI)
    dn_w_in = P.dram("dn_w_in", [2, 1024, 6208], F32, EI)
    dn_w_out = P.dram("dn_w_out", [2, 2048, 1024], F32, EI)
    att_w_in = P.dram("att_w_in", [2, 1024, 2560], F32, EI)
    att_w_out = P.dram("att_w_out", [2, 1024, 1024], F32, EI)
    outT = P.dram("outT", [8, 128, 4096], F32, "ExternalOutput")
    IK = "ExternalOutput" if dbg else "Internal"
    HT = P.dram("HT", [8, 128, TT], F32, IK)
    UT = P.dram("UT", [8, 128, TT], BF16, IK)
    YPT = P.dram("YPT", [16, 128, TT], BF16, IK)
    ZST = P.dram("ZST", [16, 128, TT], BF16, IK)
    HTr = [P.region("HTr%d" % i) for i in range(NST)]
    UTr = [P.region("UTr%d" % i) for i in range(NST)]
    YPr = [P.region("YPr%d" % i) for i in range(NST)]
    ZSr = [P.region("ZSr%d" % i) for i in range(NST)]
    OUTr = P.region("OUTr")
    dbg_out = {}

    CST = P.sb("CST", [128, NCST], F32)
    VEC = P.sb("VEC", [128, NV], F32)
    IDB = P.sb("IDB", [128, 128], BF16)
    ONEB = P.sb("ONEB", [128, 128], BF16)
    SC = P.sb("SC", [128, 16], F32)
    MOD = P.sb("MOD", [128, 48], F32)
    GSC = P.sb("GSC", [128, 16], F32)
    SMALL = P.sb("SMALL", [128, 64], F32)
    RA = P.sb("RA", [128, 8704], F32)
    RB = P.sb("RB", [128, 4480], F32)
    RC = P.sb("RC", [128, 4352], F32)
    RD = P.sb("RD", [128, 6528], F32)
    RE = P.sb("RE", [128, 4096], F32)
    RR = P.sb("RR", [128, 5120], F32R)
    RF = P.sb("RF", [128, 5440], F32)
    RG = P.sb("RG", [128, 5120], F32)
    RH = P.sb("RH", [128, 3072], F32)
    RI = P.sb("RI", [128, 2048], F32)
    PSB = [P.ps("PSB%d" % i, [128, 512], F32) for i in range(8)]

    def cs(col, n=128):
        return CST[:, col:col + n]

    def vcol(col):
        return VEC[:, col:col + 1]

    def rsq(dst, src):
        P.act(dst, src, AF.Sqrt, bias=SMALL[:, 1:2])
        P.op("dve", lambda e: e.reciprocal(out=dst.ap, in_=dst.ap), reads=[dst.t], writes=[dst.t])

    class PV:
        def __init__(self, bank, ap):
            self.bank, self.ap = bank, ap

        def __getitem__(self, k):
            return V(self.bank, self.ap[k])

        def all(self):
            return V(self.bank, self.ap)

    def psv(name, bank, lo, n, dt=F32):
        words = n if dt == F32 else n // 2
        ap = PSB[bank].t[:, lo:lo + words]
        if dt != F32:
            ap = ap.bitcast(dt)
        return PV(PSB[bank], ap)

    P.ld(CST.all(), cst.t[:, :])
    P.ld(VEC.all(), vecs.t[:, :])
    P.cp("dve", IDB.all(), cs(C_ID))
    P.cp("dve", ONEB.all(), cs(C_ONE))
    P.act(SC.all(), VEC[:, V_C:V_C + 16], AF.Silu)

    def stage_mod(i):
        P.barrier()
        WA = [P.view("WA%d" % b, RG, b * 1024, 1024, F32, "p (k n) -> p k n", k=8) for b in range(2)]
        pm = psv("pm", 0, 0, 48)
        aw = ada_w.t[i].rearrange("(k p) n -> p k n", p=128)
        for m in range(24):
            w = WA[m % 2]
            P.ld(w.all(), aw[:, :, m * 128:(m + 1) * 128])
            for k in range(8):
                P.mm(pm[:, 2 * m:2 * m + 2], w[:, k, :], SC[:, 2 * k:2 * k + 2], start=(k == 0), stop=(k == 7))
        P.tt("dve", MOD.all(), pm.all(), VEC[:, V_AB + 48 * i:V_AB + 48 * (i + 1)], ALU.add)
        for k in range(8):
            P.tsc("dve", GSC[:, 2 * k:2 * k + 2], MOD[:, 2 * (8 + k):2 * (8 + k) + 2], 1.0, ALU.add,
                  vcol(V_NG + 8 * i + k), ALU.mult)

    def stage_norm(i, src, srcr):
        P.barrier()
        HIN = [P.view("HIN%d" % b, RC, b * 2048, 2048, F32, "p (k n) -> p k n", k=8) for b in range(2)]
        SQ = P.view("SQ", RB, 0, 2048, F32, "p (k n) -> p k n", k=8)
        TMP = P.view("TMPn", RB, 2048, 2048, F32, "p (k n) -> p k n", k=8)
        UO = [P.view("UO%d" % b, RH, b * 1024, 2048, BF16, "p (k n) -> p k n", k=8) for b in range(2)]
        RS = P.view("RS", RI, 0, 256, F32)
        pn = psv("pn", 0, 0, 256)

        def load(st):
            P.ld(HIN[st % 2].all(), src.t[:, :, st * 256:(st + 1) * 256].rearrange("k p t -> p k t"), srcr[st])
        load(0)
        for st in range(NST):
            if st + 1 < NST:
                load(st + 1)
            h = HIN[st % 2]
            r = 1 if st == 0 else 0
            P.act(SQ.all(), h.all(), AF.Square)
            for k in range(8):
                P.mm(pn.all(), cs(C_O1024), SQ[:, k, :], start=(k == 0), stop=(k == 7))
            rsq(RS.all(), pn.all())
            uo = UO[st % 2]
            for k in range(8):
                P.tt("pool" if k % 2 else "dve", TMP[:, k, :], h[:, k, :], RS.all(), ALU.mult)
                P.act(uo[:, k, :], TMP[:, k, :], AF.Identity, scale=GSC[:, 2 * k + r:2 * k + r + 1],
                      bias=MOD[:, 2 * k + r:2 * k + r + 1])
            P.st(UT.t[:, :, st * 256:(st + 1) * 256].rearrange("k p t -> p k t"), UTr[st], uo.all(), shared=False)

    wstate = {"n": 0}

    def load_wchunk(dst, wsrc2d, c0, ncols=128, kc=8):
        b = wstate["n"] % 2
        wstate["n"] += 1
        stg = P.view("WSTG%d" % b, RG, 3072 + b * 1024, kc * ncols, F32, "p (k n) -> p k n", k=kc)
        P.ld(stg.all(), wsrc2d.rearrange("(k p) n -> p k n", p=128)[:, :, c0:c0 + ncols])
        P.cp("pool", dst if isinstance(dst, V) else dst.all(), stg.all())

    def stage_out(i, wout2d, kc, use_z, last):
        P.barrier()
        WO = P.view("WO", RA, 0, kc * 1024, BF16, "p (k n) -> p k n", k=kc)
        for c in range(8):
            for k0 in range(0, kc, 8):
                b = wstate["n"] % 2
                wstate["n"] += 1
                stg = P.view("WSTG%d" % b, RG, 3072 + b * 1024, 1024, F32, "p (k n) -> p k n", k=8)
                P.ld(stg.all(), wout2d.rearrange("(k p) n -> p k n", p=128)[:, k0:k0 + 8, c * 128:(c + 1) * 128])
                P.cp("pool", WO[:, k0:k0 + 8, c * 128:(c + 1) * 128], stg.all())
        YT = [P.view("YT%d" % b, RC, b * 2048, kc * 256, BF16, "p (k n) -> p k n", k=kc) for b in range(2)]
        ZT = [P.view("ZT%d" % b, RD, b * 2048, kc * 256, BF16, "p (k n) -> p k n", k=kc) for b in range(2)]
        HI = [P.view("HI%d" % b, RB, b * 2048, 2048, F32, "p (k n) -> p k n", k=8) for b in range(2)]
        HO = [P.view("HO%d" % b, RE, b * 2048, 2048, F32, "p (k n) -> p k n", k=8) for b in range(2)]
        SQ = P.view("SQo", RD, 4096, 2048, F32, "p (k n) -> p k n", k=8)
        RS = P.view("RSo", RI, 0, 256, F32)
        pn = psv("pno", 4, 0, 256)
        pys = [psv("py%d" % b, b, 0, 256) for b in range(4)]
        hsrc = hT0 if i == layers[0] else HT

        def load(st):
            b = st % 2
            P.ld(YT[b].all(), YPT.t[0:kc, :, st * 256:(st + 1) * 256].rearrange("k p t -> p k t"), YPr[st])
            if use_z:
                P.ld(ZT[b].all(), ZST.t[0:kc, :, st * 256:(st + 1) * 256].rearrange("k p t -> p k t"), ZSr[st])
            P.ld(HI[b].all(), hsrc.t[:, :, st * 256:(st + 1) * 256].rearrange("k p t -> p k t"), HTr[st])
        first = 1 if last else 0
        load(first)
        for st in range(first, NST):
            if st + 1 < NST:
                load(st + 1)
            b = st % 2
            r = 1 if st == 0 else 0
            y = YT[b]
            if use_z:
                P.tt("pool", y.all(), y.all(), ZT[b].all(), ALU.mult)
            ho = HO[b]
            for c in range(8):
                py = pys[c % 4]
                for k in range(kc):
                    P.mm(py.all(), WO[:, k, c * 128:(c + 1) * 128], y[:, k, :], start=(k == 0), stop=(k == kc - 1))
                P.stt(ho[:, c, :], py.all(), MOD[:, 2 * (16 + c) + r:2 * (16 + c) + r + 1], HI[b][:, c, :],
                      ALU.mult, ALU.add)
            if not last:
                P.st(HT.t[:, :, st * 256:(st + 1) * 256].rearrange("k p t -> p k t"), HTr[st], ho.all(), shared=False)
            else:
                P.act(SQ.all(), ho.all(), AF.Square)
                for k in range(8):
                    P.mm(pn.all(), cs(C_O1024), SQ[:, k, :], start=(k == 0), stop=(k == 7))
                rsq(RS.all(), pn.all())
                for k in range(8):
                    P.stt(SQ[:, k, :], ho[:, k, :], vcol(V_FG + k), RS.all(), ALU.mult, ALU.mult)
                P.st(outT.t[:, :, (st - 1) * 256:st * 256].rearrange("k p t -> p k t"), OUTr, SQ.all())

    def stage_attn(i):
        j = i // 2
        need_ctx = i < DEPTH - 1
        w2d = att_w_in.t[j]
        P.barrier()
        KT = P.view("KT", RA, 0, 2 * TT, BF16, "p (g t) -> p g t", g=2)
        VTM = P.view("VTM", RA, TT, 2 * TT, BF16, "p (g c d) -> p g c d", g=2, c=NCH)
        QT = P.view("QT", RB, 0, TT, BF16)
        ZS = P.view("ZS", RB, TT // 2, TT, BF16)
        UIN = [P.view("UIN%d" % b, RH, b * 1024, 2048, BF16, "p (k n) -> p k n", k=8) for b in range(2)]
        WQ = P.view("WQ", RG, 0, 1024, BF16, "p (k n) -> p k n", k=8)
        WZ = P.view("WZ", RG, 512, 1024, BF16, "p (k n) -> p k n", k=8)
        WV = P.view("WV", RG, 1024, 1024, BF16, "p (k n) -> p k n", k=8)
        XN = [P.view("XN%d" % b, RD, b * 256, 256, F32) for b in range(2)]
        SQ = [P.view("SQa%d" % b, RD, 512 + b * 256, 256, F32) for b in range(2)]
        RS = [P.view("RSa%d" % b, RD, 1024 + b * 256, 256, F32) for b in range(2)]
        T1 = [P.view("T1a%d" % b, RD, 1536 + b * 256, 256, F32) for b in range(2)]
        T2 = [P.view("T2a%d" % b, RD, 2048 + b * 256, 256, F32) for b in range(2)]
        CT = [P.view("CT%d" % b, RD, 2560 + b * 256, 256, F32) for b in range(2)]
        STb = [P.view("STb%d" % b, RD, 3072 + b * 256, 256, F32) for b in range(2)]
        PT = [P.view("PTa%d" % b, RE, b * 256, 512, BF16) for b in range(3)]
        RIV = P.view("RIV", RE, 1024, 512, F32)
        OT = P.view("OTa", RE, 1536, 512, F32)
        YO = [P.view("YO%d" % b, RE, 2048 + b * 256, 512, BF16) for b in range(2)]
        pqs = [psv("pq%d" % b, 4 + b, 0, 256) for b in range(2)]
        pzs = [psv("pz%d" % b, 2 + b, 0, 256) for b in range(2)]
        psss = [psv("pss%d" % b, 6 + b, 0, 256) for b in range(2)]
        prots = [psv("prot%d" % b, b, 0, 256) for b in range(2)]
        pv = [psv("pv%d" % b, 2 + b, 256, 128) for b in range(2)]

        def load_u(st):
            P.ld(UIN[st % 2].all(), UT.t[:, :, st * 256:(st + 1) * 256].rearrange("k p t -> p k t"), UTr[st])

        rope_n = {"n": 0}

        def proj_norm_rope(st, w, gcol, dst, extra_scale):
            u = UIN[st % 2]
            b2 = rope_n["n"] % 2
            rope_n["n"] += 1
            pq, pss, prot = pqs[b2], psss[b2], prots[b2]
            for k in range(8):
                P.mm(pq.all(), w[:, k, :], u[:, k, :], start=(k == 0), stop=(k == 7))
            P.act(SQ[b2].all(), pq.all(), AF.Square)
            P.mm(pss.all(), cs(C_O128), SQ[b2].all())
            rsq(RS[b2].all(), pss.all())
            if st == 0:
                P.stt(dst, pq.all(), gcol, RS[b2].all(), ALU.mult, ALU.mult)
                return
            P.stt(XN[b2].all(), pq.all(), gcol, RS[b2].all(), ALU.mult, ALU.mult)
            P.ld(CT[b2].all(), ropeC.t[:, (st - 1) * 256:st * 256])
            P.ld(STb[b2].all(), ropeS.t[:, (st - 1) * 256:st * 256])
            P.mm(prot.all(), cs(C_RT), XN[b2].all())
            P.tt("pool", T1[b2].all(), XN[b2].all(), CT[b2].all(), ALU.mult)
            P.tt("dve", T2[b2].all(), prot.all(), STb[b2].all(), ALU.mult)
            P.tt("pool", dst, T1[b2].all(), T2[b2].all(), ALU.add)

        for g in range(2):
            load_wchunk(WQ, w2d, 1024 + g * 128)
            load_wchunk(WV, w2d, 1280 + g * 128)
            load_u(0)
            for st in range(NST):
                if st + 1 < NST:
                    load_u(st + 1)
                proj_norm_rope(st, WQ, vcol(V_KG + j), KT[:, g, st * 256:(st + 1) * 256], None)
                u = UIN[st % 2]
                for half in range(2):
                    c = st * 2 + half
                    p = pv[half]
                    for k in range(8):
                        P.mm(p.all(), u[:, k, half * 128:(half + 1) * 128], WV[:, k, :], start=(k == 0), stop=(k == 7))
                    P.cp("act", VTM[:, g, c, :], p.all())
        pS = [psv("pS%d" % b, b, 0, 512) for b in range(2)]
        pOs = [psv("pO%d" % b, 2 + 2 * b, 0, 512) for b in range(2)]
        pSums = [psv("pSum%d" % b, 3 + 2 * b, 0, 512) for b in range(2)]
        zn = 0
        for h in range(8):
            g = h // 4
            load_wchunk(WQ, w2d, h * 128)
            load_wchunk(WZ, w2d, 1536 + h * 128)
            load_u(0)
            for st in range(NST):
                if st + 1 < NST:
                    load_u(st + 1)
                proj_norm_rope(st, WQ, vcol(V_QG + j), QT[:, st * 256:(st + 1) * 256], None)
                u = UIN[st % 2]
                pz = pzs[zn % 2]
                zn += 1
                for k in range(8):
                    P.mm(pz.all(), WZ[:, k, :], u[:, k, :], start=(k == 0), stop=(k == 7))
                P.act(ZS[:, st * 256:(st + 1) * 256], pz.all(), AF.Silu)
            blocks = ([(0, 256, 2)] if need_ctx else []) + [(256 + 512 * qb, 512, NCH) for qb in range(8)]
            for bi, (q0, nq, nk) in enumerate(blocks):
                pO, pSum = pOs[bi % 2], pSums[bi % 2]

                def score(kt):
                    P.mm(pS[kt % 2][:, 0:nq], KT[:, g, kt * 128:(kt + 1) * 128], QT[:, q0:q0 + nq])
                score(0)
                for kt in range(nk):
                    if kt + 1 < nk:
                        score(kt + 1)
                    pt = PT[kt % 3]
                    P.act(pt[:, 0:nq], pS[kt % 2][:, 0:nq], AF.Exp, scale=float(128.0 ** -0.5), bias=SMALL[:, 0:1])
                    P.mm(pO[:, 0:nq], VTM[:, g, kt, :], pt[:, 0:nq], start=(kt == 0), stop=(kt == nk - 1))
                    P.mm(pSum[:, 0:nq], ONEB.all(), pt[:, 0:nq], start=(kt == 0), stop=(kt == nk - 1))
                P.op("dve", lambda e, o=RIV[:, 0:nq], s=pSum[:, 0:nq]: e.reciprocal(out=o.ap, in_=s.ap),
                     reads=[pSum.bank], writes=[RIV])
                P.tt("dve", OT[:, 0:nq], pO[:, 0:nq], RIV[:, 0:nq], ALU.mult)
                yo = YO[bi % 2]
                P.tt("pool", yo[:, 0:nq], OT[:, 0:nq], ZS[:, q0:q0 + nq], ALU.mult)
                regs = [YPr[(q0 + s * 256) // 256] for s in range(nq // 256)]
                P.dma(lambda e, d=YPT.t[h, :, q0:q0 + nq], y_=yo[:, 0:nq]: e.dma_start(out=d, in_=y_.ap),
                      reads=[yo], shared=regs)


    def stage_dn(i):
        j = i // 2
        w2d = dn_w_in.t[j]
        P.barrier()
        UIN = [P.view("UIN%d" % b, RH, b * 1024, 2048, BF16, "p (k n) -> p k n", k=8) for b in range(2)]
        BETA = P.view("BETA", RF, 0, 1088, F32, "p (c h) -> p c h", c=NCH)
        G = P.view("G", RF, 1088, 1088, F32, "p (c h) -> p c h", c=NCH)
        EGL = P.view("EGL", RF, 2176, 1088, F32, "p (c h) -> p c h", c=NCH)
        EG = P.view("EG", RF, 3264, 1088, F32, "p (c h) -> p c h", c=NCH)
        ET = P.view("ET", RF, 4352, 1088, F32, "p (c h) -> p c h", c=NCH)
        NEGA = P.view("NEGA", RI, 0, 32, F32)

        def load_u(st):
            P.ld(UIN[st % 2].all(), UT.t[:, :, st * 256:(st + 1) * 256].rearrange("k p t -> p k t"), UTr[st])

        WAB = P.view("WAB", RG, 0, 512, BF16, "p (k n) -> p k n", k=8)
        load_wchunk(WAB, w2d, 6144, ncols=64)
        pab = psv("pab", 6, 0, 64)
        pgc = psv("pgc", 7, 0, 64)
        P.act(NEGA.all(), VEC[:, V_AL + 32 * j:V_AL + 32 * (j + 1)], AF.Exp)
        P.tsc("dve", NEGA.all(), NEGA.all(), -1.0, ALU.mult)
        load_u(0)
        for st in range(NST):
            if st + 1 < NST:
                load_u(st + 1)
            u = UIN[st % 2]
            for half in range(2):
                c = st * 2 + half
                for k in range(8):
                    P.mm(pab.all(), u[:, k, half * 128:(half + 1) * 128], WAB[:, k, :], start=(k == 0), stop=(k == 7))
                P.cp("dve", BETA[:, c, :], pab[:, 0:32])
                P.tt("dve", G[:, c, :], pab[:, 32:64], VEC[:, V_DT + 32 * j:V_DT + 32 * (j + 1)], ALU.add)
        P.act(BETA.all(), BETA.all(), AF.Sigmoid)
        P.act(G.all(), G.all(), AF.Exp)
        P.act(G.all(), G.all(), AF.Ln, bias=SMALL[:, 2:3])
        for c in range(NCH):
            P.tt("pool", G[:, c, :], G[:, c, :], NEGA.all(), ALU.mult)
            P.mm(pgc[:, 0:16], cs(C_UI), G[:, c, 0:16])
            P.mm(pgc[:, 16:32], cs(C_LI), G[:, c, 16:32])
            P.mm(pgc[:, 32:64], cs(C_ONE), G[:, c, :])
            P.cp("dve", EGL[:, c, :], pgc[:, 0:32])
            P.cp("dve", ET[:, c, :], pgc[:, 32:64])
        P.act(EG.all(), EGL.all(), AF.Exp)
        P.tt("pool", EGL.all(), ET.all(), EGL.all(), ALU.subtract)
        P.act(EGL.all(), EGL.all(), AF.Exp)
        P.act(ET.all(), ET.all(), AF.Exp)
        BEG = P.view("BEG", RB, 256, 1088, F32, "p (c h) -> p c h", c=NCH)
        P.tt("pool", BEG.all(), BETA.all(), EG.all(), ALU.mult)

        P.barrier()
        if not skip_z:
            WZA = P.view("WZA", RA, 0, 16 * 1024, BF16, "p (h k n) -> p h k n", h=16, k=8)
            for hh in range(16):
                load_wchunk(WZA[:, hh, :, :], w2d, 4096 + hh * 128)
            ZO4 = [P.view("ZO4_%d" % b, RI, 64 + b * 512, 1024, BF16, "p (h n) -> p h n", h=4) for b in range(2)]
            pz = [psv("pzd%d" % b, 4 + b, 0, 256) for b in range(2)]
            n = 0
            load_u(0)
            for st in range(NST):
                if st + 1 < NST:
                    load_u(st + 1)
                u = UIN[st % 2]
                for q in range(4):
                    zo = ZO4[n % 2]
                    n += 1
                    for hq in range(4):
                        hh = q * 4 + hq
                        p = pz[hh % 2]
                        for k in range(8):
                            P.mm(p.all(), WZA[:, hh, k, :], u[:, k, :], start=(k == 0), stop=(k == 7))
                        P.act(zo[:, hq, :], p.all(), AF.Silu)
                    P.st(ZST.t[q * 4:(q + 1) * 4, :, st * 256:(st + 1) * 256].rearrange("h p t -> p h t"),
                         ZSr[st], zo.all())

        if dbg:
            P.barrier()
            DRF = P.dram("DRF", [128, 5440], F32, "ExternalOutput")
            P.st(DRF.t[:, :], P.region("drf"), RF.all())
        for jp in pairs:
            dn_pair(i, j, jp, BETA, G, EGL, EG, ET, BEG)

    def dn_pair(i, j, jp, BETA, G, EGL, EG, ET, BEG):
        w2d = dn_w_in.t[j]
        P.barrier()
        QNT = P.view("QNT", RD, 0, TT, BF16)
        KNT = P.view("KNT", RD, 2176, TT, BF16)
        KTM = P.view("KTM", RD, 4352, TT, BF16, "p (c d) -> p c d", c=NCH)
        VTM = P.view("VTMd", RC, 0, 2 * TT, BF16, "p (e c d) -> p e c d", e=2, c=NCH)
        OACC = P.view("OACC", RA, 0, 2 * TT, F32, "p (e c d) -> p e c d", e=2, c=NCH)
        WX = [P.view("WX%d" % x, RG, 512 * x, 1024, BF16, "p (k n) -> p k n", k=8) for x in range(4)]
        cols = [jp * 128, 1024 + jp * 128, 2048 + (2 * jp) * 128, 2048 + (2 * jp + 1) * 128]
        for x in range(4):
            load_wchunk(WX[x], w2d, cols[x])
        UH = [P.view("UH%d" % b, RH, b * 1040, 2080, BF16, "p (k n) -> p k n", k=8) for b in range(2)]
        PJ = [[P.view("PJ%d_%d" % (b, x), RE, (b * 4 + x) * 260, 260, F32) for x in range(4)] for b in range(2)]
        CV = [[P.view("CV%d_%d" % (b, x), RI, (b * 4 + x) * 256, 256, F32) for x in range(4)] for b in range(2)]
        SQc = [[P.view("SQc%d_%d" % (b, x), RB, 3648 + (b * 2 + x) * 128, 256, BF16) for x in range(2)] for b in range(2)]
        RSc = [[P.view("RSc%d_%d" % (b, x), RA, (b * 2 + x) * 256, 256, F32) for x in range(2)] for b in range(2)]
        VF = [[P.view("VF%d_%d" % (b, x), RA, 1024 + (b * 2 + x) * 128, 256, BF16) for x in range(2)] for b in range(2)]
        pp = [psv("pp%d" % x, x, 0, 260) for x in range(4)]
        pssq = [psv("pssd%d" % x, 4 + x, 0, 256) for x in range(2)]
        ptr = [psv("ptr%d" % b, 6 + b, 0, 512, BF16) for b in range(2)]

        def load_uh(st):
            t0 = st * 256
            lr = 1 if st >= 2 else 0
            rr = 1 if 1 <= st <= 15 else 0
            uh = UH[st % 2]
            a, b = t0 - 2 * lr, t0 + 256 + 2 * rr
            P.ld(uh[:, :, 2 - 2 * lr:258 + 2 * rr], UT.t[:, :, a:b].rearrange("k p t -> p k t"),
                 UTr[st])
            if not lr:
                P.memset("pool", uh[:, :, 0:2], 0.0)
            if not rr:
                P.memset("pool", uh[:, :, 258:260], 0.0)
        def phase1(st):
            uh = UH[st % 2]
            b2 = st % 2
            for x in range(4):
                for k in range(8):
                    P.mm(pp[x].all(), WX[x][:, k, :], uh[:, k, :], start=(k == 0), stop=(k == 7))
                P.cp("act", PJ[b2][x].all(), pp[x].all())
            for x in range(4):
                ch = (cols[x] // 128)
                cw = lambda tap: vcol(V_CW + 160 * j + ch * 5 + tap)
                pj, cv = PJ[b2][x], CV[b2][x]
                P.tsc("dve", cv.all(), pj[:, 0:256], cw(0), ALU.mult)
                for tap in range(1, 5):
                    P.stt(cv.all(), pj[:, tap:tap + 256], cw(tap), cv.all(), ALU.mult, ALU.add)

        def phase2(st):
            b2 = st % 2
            t0 = st * 256
            for x in range(2):
                P.act(CV[b2][x].all(), CV[b2][x].all(), AF.Silu)
            for x in range(2):
                P.act(VF[b2][x].all(), CV[b2][2 + x].all(), AF.Silu)
            for x in range(2):
                P.act(SQc[b2][x].all(), CV[b2][x].all(), AF.Square)
                P.mm(pssq[x].all(), ONEB.all(), SQc[b2][x].all())
            for x in range(2):
                rsq(RSc[b2][x].all(), pssq[x].all())
            for x in range(2):
                dst = (QNT if x == 0 else KNT)[:, t0:t0 + 256]
                P.stt(dst, CV[b2][x].all(), float(128.0 ** -0.5) if x == 0 else 1.0, RSc[b2][x].all(),
                      ALU.mult, ALU.mult)
            pt = ptr[0]
            for hf in range(2):
                P.tr(pt[:, hf * 128:(hf + 1) * 128], KNT[:, t0 + hf * 128:t0 + (hf + 1) * 128], IDB.all())
            P.cp("act", KTM[:, 2 * st:2 * st + 2, :], V(pt.bank, pt.ap[:, 0:256].rearrange("p (a b) -> p a b", a=2)))
            pt = ptr[1]
            for x in range(2):
                for hf in range(2):
                    P.tr(pt[:, x * 256 + hf * 128:x * 256 + (hf + 1) * 128], VF[b2][x][:, hf * 128:(hf + 1) * 128],
                         IDB.all())
            for x in range(2):
                P.cp("act", VTM[:, x, 2 * st:2 * st + 2, :],
                     V(pt.bank, pt.ap[:, x * 256:(x + 1) * 256].rearrange("p (a b) -> p a b", a=2)))

        load_uh(0)
        load_uh(1)
        phase1(0)
        for st in range(NST):
            if st + 2 < NST:
                load_uh(st + 2)
            if st + 1 < NST:
                phase1(st + 1)
            phase2(st)

        P.barrier()
        if dbg and jp == 0:
            DRD = P.dram("DRD", [128, 6528], F32, "ExternalOutput")
            DRC = P.dram("DRC", [128, 4352], F32, "ExternalOutput")
            P.st(DRD.t[:, :], P.region("drd"), RD.all())
            P.st(DRC.t[:, :], P.region("drc"), RC.all())
        NW = 1664

        def ctile(ch, k, dt=F32):
            if dt == F32:
                return P.view("c%df%d" % (ch, k), RE, ch * NW + 128 * k, 128, F32)
            return P.view("c%db%d" % (ch, k), RE, ch * NW + 1152 + 64 * k, 128, BF16)
        KK = [P.view("KK%d" % d, RI, 128 * d, 128, F32) for d in range(2)]
        KKO = [P.view("KKO%d" % d, RB, 128 * d, 128, F32) for d in range(2)]
        KKO1 = [P.view("KKO1%d" % d, RB, 3392 + 128 * d, 128, F32) for d in range(2)]
        QK = [P.view("QK%d" % d, RI, 256 + 128 * d, 128, F32) for d in range(2)]
        S = [P.view("S%d" % c, RI, 512 + 128 * c, 128, F32) for c in range(4)]
        Sb = [P.view("Sb%d" % c, RI, 1024 + 64 * c, 128, BF16) for c in range(4)]
        ONt = [P.view("ON%d" % c, RI, 1280 + 64 * c, 128, BF16) for c in range(4)]
        YPt = [P.view("YP%d" % c, RI, 1536 + 64 * c, 128, BF16) for c in range(4)]
        JK = P.view("JK", RI, 1792, 128, BF16)
        SSt = P.view("SSt", RI, 1856, 8, F32)
        for c in range(4):
            P.memset("pool", S[c].all(), 0.0)
            P.memset("pool", Sb[c].all(), 0.0)
        pkq = [psv("pkq%d" % d, 4 + d, 0, 256) for d in range(2)]
        pfin = psv("pfin", 7, 0, 512, BF16)
        gob = VEC[:, V_GO + 128 * j:V_GO + 128 * (j + 1)]
        fin_n = {"n": 0}

        def finish(ch, e, m):
            hh = 2 * jp + e
            o = OACC[:, e, m, :]
            ss = SSt[:, ch:ch + 1]
            P.op("dve", lambda e_, jk=JK.all(), o_=o, s_=ss: e_.scalar_tensor_tensor(
                out=jk.ap, in0=o_.ap, scalar=1.0, in1=o_.ap, op0=ALU.mult, op1=ALU.mult, accum_out=s_.ap),
                reads=[OACC], writes=[JK, SSt])
            P.act(ss, ss, AF.Ln, scale=1.0 / 128.0, bias=SMALL[:, 1:2])
            P.act(ss, ss, AF.Exp, scale=-0.5)
            P.stt(ONt[ch].all(), o, ss, gob, ALU.mult, ALU.mult)
            q = fin_n["n"] % 4
            fin_n["n"] += 1
            P.tr(pfin[:, q * 128:(q + 1) * 128], ONt[ch].all(), IDB.all())
            P.cp("act", YPt[ch].all(), pfin[:, q * 128:(q + 1) * 128])
            P.st(YPT.t[hh, :, m * 128:(m + 1) * 128], YPr[m // 2], YPt[ch].all())

        def f(v):
            return V(v.t, v.ap.bitcast(F32))

        def unit(ch, e, d, m, first):
            hh = 2 * jp + e
            col = d * 16 + hh
            sc = lambda X: X[:, m, col:col + 1]
            Ud = cs(C_UI) if d == 0 else cs(C_LI)
            Vd = cs(C_LS) if d == 0 else cs(C_US)
            base = ch * 384
            rb = ch * 1280
            NNs = [P.view("c%dN%d" % (ch, k), RR, rb + 128 * k, 128, F32R) for k in range(2)]
            NTYs = [P.view("c%dNTY%d" % (ch, k), RR, rb + 256 + 256 * k, 256, F32R) for k in range(2)]
            NOt = P.view("c%dNO" % ch, RR, rb + 768, 128, F32R)
            TDt = P.view("c%dTD" % ch, RR, rb + 896, 128, F32R)
            Mtt = P.view("c%dMt" % ch, RR, rb + 1024, 128, F32R)
            NO1t = P.view("c%dNO1" % ch, RR, rb + 1152, 128, F32R)
            UG = P.view("c%dUG" % ch, RE, base, 128, F32).all()
            Dm = P.view("c%dDm" % ch, RE, base + 128, 128, F32).all()
            O1 = P.view("c%dO1" % ch, RE, base + 256, 128, F32).all()
            Tt, Pm, PTt, KBG, VB, WTN, VN, VND = [P.view("c%db%d" % (ch, k), RB, 1344 + ch * 512 + 64 * k, 128, BF16)
                                                  for k in range(8)]
            pA = psv("", ch, 0, 128)
            pB = psv("", ch, 128, 128)
            pC = psv("", ch, 256, 128)
            pD = psv("", ch, 384, 128)
            pCD = psv("", ch, 256, 256)
            pCb = psv("", ch, 256, 256, BF16)
            msl = slice(m * 128, (m + 1) * 128)
            if e == 0:
                P.mm(pkq[d][:, 0:128], KNT[:, msl], KNT[:, msl])
                P.mm(pkq[d][:, 128:256], QNT[:, msl], KNT[:, msl])
                P.tt("dve", KK[d].all(), pkq[d][:, 0:128], cs(C_NLSD) if d == 0 else cs(C_NUSD), ALU.mult)
                P.tt("dve", KKO[d].all(), pkq[d][:, 0:128], cs(C_NLSO) if d == 0 else cs(C_NUSO), ALU.mult)
                P.tt("dve", KKO1[d].all(), pkq[d][:, 0:128], cs(C_NLSO1) if d == 0 else cs(C_NUSO1), ALU.mult)
                P.tt("dve", QK[d].all(), pkq[d][:, 128:256], cs(C_LI) if d == 0 else cs(C_UI), ALU.mult)
            P.act(UG, Ud, AF.Identity, scale=sc(G))
            P.mm(pA.all(), UG, Vd)
            P.cp("pool", NTYs[0][:, 128:256], cs(C_ID))
            yield
            P.act(Dm, pA.all(), AF.Exp)
            P.act(KBG.all(), KTM[:, m, :], AF.Identity, scale=sc(BEG))
            P.act(VB.all(), VTM[:, e, m, :], AF.Identity, scale=sc(BETA))
            yield
            P.stt(NNs[0].all(), Dm, sc(BETA), KK[d].all(), ALU.mult, ALU.mult)
            P.stt(NO1t.all(), Dm, sc(BETA), KKO1[d].all(), ALU.mult, ALU.mult)
            P.stt(NOt.all(), Dm, sc(BETA), KKO[d].all(), ALU.mult, ALU.mult)
            P.tt("pool", Pm.all(), Dm, QK[d].all(), ALU.mult)
            yield
            P.tr(pB.all(), f(NNs[0].all()), cs(C_ID))
            P.tr(pCb[:, 0:128], Pm.all(), IDB.all())
            yield
            P.cp("act", NTYs[0][:, 0:128], pB.all())
            P.cp("act", PTt.all(), pCb[:, 0:128])
            yield
            a = 0
            for k in range(5):
                N, NTY = NNs[a], NTYs[a]
                N2, NTY2 = NNs[1 - a], NTYs[1 - a]
                if k < 4:
                    P.mm(pA.all(), NTY[:, 0:128], N.all())
                    P.mm(pCD.all(), N.all(), NTY.all())
                    yield
                    P.cp("act", N2.all(), pA.all())
                    P.cp("act", NTY2[:, 0:128], pCD[:, 0:128])
                    P.tt("dve", NTY2[:, 128:256], pCD[:, 128:256], f(NTY[:, 128:256]), ALU.add)
                    yield
                else:
                    P.mm(pC.all(), N.all(), NTY[:, 128:256])
                    yield
                    P.tt("dve", NTY2[:, 128:256], pC.all(), f(NTY[:, 128:256]), ALU.add)
                    yield
                a = 1 - a
            for (NOx, lastm) in ((NO1t, False), (NOt, True)):
                Yd = NTYs[a][:, 128:256]
                P.tr(pA.all(), f(Yd), cs(C_ID))
                P.mm(pB.all(), NOx.all(), Yd)
                yield
                P.cp("act", TDt.all(), pA.all())
                P.cp("dve", Mtt.all(), pB.all())
                yield
                P.mm(pC.all(), TDt.all(), Mtt.all())
                yield
                if lastm:
                    P.tt("dve", Tt.all(), pC.all(), f(Yd), ALU.add)
                else:
                    P.tt("dve", NTYs[1 - a][:, 128:256], pC.all(), f(Yd), ALU.add)
                    a = 1 - a
                yield
            P.mm(pA.all(), KBG.all(), Tt.all())
            yield
            P.act(WTN.all(), pA.all(), AF.Identity, scale=-1.0)
            yield
            P.mm(pB.all(), Tt.all(), VB.all(), start=True, stop=False)
            P.mm(pB.all(), WTN.all(), Sb[ch].all(), start=False, stop=True)
            P.mm(pC.all(), QNT[:, msl], Sb[ch].all())
            yield
            P.cp("dve", VN.all(), pB.all())
            P.act(VND.all(), pB.all(), AF.Identity, scale=sc(EGL))
            P.act(O1, pC.all(), AF.Identity, scale=sc(EG))
            yield
            P.mm(pD.all(), PTt.all(), VN.all())
            P.mm(pA.all(), KTM[:, m, :], VND.all())
            yield
            if not first:
                P.tt("pool", O1, O1, OACC[:, e, m, :], ALU.add)
            P.tt("dve", OACC[:, e, m, :], pD.all(), O1, ALU.add)
            P.stt(S[ch].all(), S[ch].all(), sc(ET), pA.all(), ALU.mult, ALU.add)
            yield
            P.cp("act", Sb[ch].all(), S[ch].all())
            if not first:
                finish(ch, e, m)

        chains = [[], [], [], []]
        steps = [(0, 1, True), (1, 0, False)] + [(2 + n_, 33 - n_, n_ < 16) for n_ in range(32)]
        for (mf, mb, first) in steps:
            chains[0].append((0, 0, mf, first))
            chains[1].append((1, 0, mf, first))
            chains[2].append((0, 1, mb, first))
            chains[3].append((1, 1, mb, first))
        gens = [None] * 4
        pos = [0] * 4
        live = True
        rnd = 0
        delay = {0: 0, 2: 0, 1: 0, 3: 0}
        while live:
            live = False
            rnd += 1
            for ch in (0, 2, 1, 3):
                if rnd <= delay[ch]:
                    live = True
                    continue
                if gens[ch] is None:
                    if pos[ch] >= len(chains[ch]):
                        continue
                    e, d, m, first = chains[ch][pos[ch]]
                    pos[ch] += 1
                    gens[ch] = unit(ch, e, d, m, first)
                live = True
                try:
                    next(gens[ch])
                except StopIteration:
                    gens[ch] = None

    P.memset("pool", SMALL[:, 0:1], -8.0)
    P.memset("pool", SMALL[:, 1:2], EPS)
    P.memset("pool", SMALL[:, 2:3], 1.0)
    for i in layers:
        stage_mod(i)
        stage_norm(i, hT0 if i == layers[0] else HT, HTr)
        if upto == "norm":
            continue
        if i % 2 == 1:
            stage_attn(i)
            if upto == "attn":
                continue
            stage_out(i, att_w_out.t[i // 2], 8, False, i == DEPTH - 1)
        else:
            stage_dn(i)
            if upto == "dn":
                continue
            stage_out(i, dn_w_out.t[i // 2], 16, True, False)
    P.barrier()
    P.emit()
    return nc, P


def _host_inputs(inputs):
    cstt, ropec, ropes = _const_tables()
    f = lambda a: np.ascontiguousarray(np.asarray(a, dtype=np.float32))
    x, c, ctx, c_ctx = f(inputs["x"]), f(inputs["c"]), f(inputs["ctx"]), f(inputs["c_ctx"])
    col = lambda v: v.reshape(-1, 128).T
    maps = []
    for b in range(8):
        vec = np.zeros((128, NV), np.float32)
        cc = np.stack([col(c[b]), col(c_ctx)], -1)
        vec[:, V_C:V_C + 16] = cc.reshape(128, 16)
        ng = f(inputs["norm_g"])
        for l in range(4):
            vec[:, V_NG + 8 * l:V_NG + 8 * (l + 1)] = col(ng[l])
            ab = col(f(inputs["ada_b"])[l])
            vec[:, V_AB + 48 * l:V_AB + 48 * (l + 1)] = np.repeat(ab, 2, axis=1)
        vec[:, V_FG:V_FG + 8] = col(f(inputs["final_norm_g"]))
        cw = f(inputs["dn_conv_w"])
        for l in range(2):
            t = cw[l].T.reshape(32, 128, 5).transpose(1, 0, 2)
            vec[:, V_CW + 160 * l:V_CW + 160 * (l + 1)] = t.reshape(128, 160)
            vec[:, V_QG + l] = f(inputs["att_q_norm_g"])[l]
            vec[:, V_KG + l] = f(inputs["att_k_norm_g"])[l]
            vec[:, V_GO + 128 * l:V_GO + 128 * (l + 1)] = f(inputs["dn_o_norm_g"])[l][None, :]
            vec[:, V_AL + 32 * l:V_AL + 32 * (l + 1)] = f(inputs["dn_a_log"])[l].reshape(1, 32)
            vec[:, V_DT + 32 * l:V_DT + 32 * (l + 1)] = f(inputs["dn_dt_bias"])[l].reshape(1, 32)
        h0 = np.concatenate([ctx[b], x[b]], 0)
        hT = np.ascontiguousarray(h0.T).reshape(8, 128, TT)
        maps.append({
            "hT0": hT, "vecs": vec, "cst": cstt, "ropeC": ropec, "ropeS": ropes,
            "ada_w": f(inputs["ada_w"]), "dn_w_in": f(inputs["dn_w_in"]), "dn_w_out": f(inputs["dn_w_out"]),
            "att_w_in": f(inputs["att_w_in"]), "att_w_out": f(inputs["att_w_out"]),
        })
    return maps


def kernel(**inputs):
    nc, _ = build()
    maps = _host_inputs(inputs)
    res = run_bass_kernel_spmd(nc, maps, core_ids=list(range(8)))
    out = np.stack([np.asarray(r["outT"]).reshape(1024, 4096).T for r in res.results], 0)
    return np.ascontiguousarray(out.astype(np.float32))
```
